# Optimizing a Trainium2 kernel written in Bass

```python
import math
import jax
import jax.numpy as jnp
from jax import lax
import numpy as np

D_MODEL = 1024
BATCH = 8
SEQ = 2048
DEPTH = 2
DEC_BATCH = 32
DEC_SEQ = 4
PAST_LEN = 16384
PAGE_SIZE = 128

CONV_WIDTH = 4
SSD_HEADS = 16
SSD_HEAD_DIM = 64
SSD_WIDTH = SSD_HEADS * SSD_HEAD_DIM
SSD_GROUPS = 2
SSD_STATE = 128
SSD_CONV_DIM = SSD_WIDTH + 2 * SSD_GROUPS * SSD_STATE
SSD_CHUNK = 128
MLA_HEADS = 8
MLA_NOPE = 64
MLA_ROPE = 32
MLA_V = 64
MLA_WIDTH = MLA_HEADS * MLA_V
MLA_Q_RANK = 384
MLA_KV_RANK = 256
MLA_SCALE = (MLA_NOPE + MLA_ROPE) ** -0.5
ROPE_THETA = 10000.0
Q_BLOCK = 128
GDN_HEADS = 4
GDN_HEAD_DIM = 128
GDN_WIDTH = GDN_HEADS * GDN_HEAD_DIM
GDN_CONV_DIM = 3 * GDN_WIDTH
GDN_CHUNK = 64

MIX_WIDTH = SSD_WIDTH + MLA_WIDTH + GDN_WIDTH
IN_SIZES = (SSD_WIDTH, SSD_CONV_DIM, SSD_HEADS,
            MLA_Q_RANK, MLA_KV_RANK, MLA_ROPE, MLA_WIDTH,
            GDN_CONV_DIM, GDN_WIDTH, GDN_HEADS, GDN_HEADS)
IN_WIDTH = sum(IN_SIZES)
IN_OFFSETS = tuple(int(o) for o in np.cumsum(IN_SIZES)[:-1])

DEEPNORM_ALPHA = (2 * DEPTH) ** 0.25
DEEPNORM_BETA = (8 * DEPTH) ** -0.25
LN_EPS = 1e-5
RMS_EPS = 1e-6
L2_EPS = 1e-6

kernel_name = 'hymba_ssd_mla_gdn_deepnorm_step'


def layer_norm(x, g, b):
    xf = x.astype(jnp.float32)
    mu = jnp.mean(xf, -1, keepdims=True)
    var = jnp.mean(jnp.square(xf - mu), -1, keepdims=True)
    return ((xf - mu) * lax.rsqrt(var + LN_EPS) * g + b).astype(x.dtype)


def rms_norm(x, g):
    xf = x.astype(jnp.float32)
    return (xf * lax.rsqrt(jnp.mean(xf * xf, -1, keepdims=True) + RMS_EPS) * g).astype(x.dtype)


def l2_normalize(x):
    return x * lax.rsqrt(jnp.sum(x * x, -1, keepdims=True) + L2_EPS)


def causal_conv(x, prev, w):
    xp = jnp.concatenate([prev.astype(x.dtype), x], axis=1)
    y = lax.conv_general_dilated(xp, w[:, None, :].astype(x.dtype), (1,), 'VALID',
                                 dimension_numbers=('NWC', 'WIO', 'NWC'),
                                 feature_group_count=x.shape[-1])
    return y, xp[:, xp.shape[1] - (CONV_WIDTH - 1):]


def rope(x, pos):
    half = x.shape[-1] // 2
    inv = ROPE_THETA ** (-jnp.arange(half, dtype=jnp.float32) / half)
    ang = pos.astype(jnp.float32)[:, None] * inv[None, :]
    cos, sin = jnp.cos(ang)[:, None, :], jnp.sin(ang)[:, None, :]
    xf = x.astype(jnp.float32)
    x1, x2 = xf[..., :half], xf[..., half:]
    return jnp.concatenate([x1 * cos - x2 * sin, x2 * cos + x1 * sin], -1).astype(x.dtype)


def chunk_len(t, c):
    return c if t % c == 0 else t


def to_chunks(a, L):
    b, t = a.shape[:2]
    return jnp.moveaxis(a.reshape(b, t // L, L, *a.shape[2:]), 1, 0)


def from_chunks(a):
    nc, b, L = a.shape[:3]
    return jnp.moveaxis(a, 0, 1).reshape(b, nc * L, *a.shape[3:])


def ssd_scan(x, dt, a, bm, cm, h0):
    L = chunk_len(x.shape[1], SSD_CHUNK)
    causal = jnp.tril(jnp.ones((L, L), bool))

    def step(h, inp):
        xc, dtc, bc, cc = inp
        acum = jnp.cumsum(dtc * a, axis=1)
        seg = acum[:, :, None, :] - acum[:, None, :, :]
        decay = jnp.exp(jnp.where(causal[None, :, :, None], seg, -jnp.inf))
        xdt = xc * dtc[..., None]
        scores = jnp.einsum('bthn,bshn->btsh', cc, bc) * decay
        y_in = jnp.einsum('btsh,bshp->bthp', scores, xdt)
        y_st = jnp.einsum('bthn,bhpn->bthp', cc, h) * jnp.exp(acum)[..., None]
        last = acum[:, -1]
        wdec = jnp.exp(last[:, None, :] - acum)
        h_new = h * jnp.exp(last)[:, :, None, None] + jnp.einsum('bshn,bshp->bhpn', bc * wdec[..., None], xdt)
        return h_new, y_in + y_st

    h_fin, ys = lax.scan(step, h0, (to_chunks(x, L), to_chunks(dt, L), to_chunks(bm, L), to_chunks(cm, L)))
    return from_chunks(ys), h_fin


def gdn_scan(q, k, v, g, beta, s0):
    L = chunk_len(q.shape[1], GDN_CHUNK)
    incl = jnp.tril(jnp.ones((L, L), bool))
    strict = jnp.tril(jnp.ones((L, L), bool), -1)
    eye = jnp.eye(L, dtype=jnp.float32)

    def step(s, inp):
        qc, kc, vc, gc, bc = inp
        gcum = jnp.cumsum(gc, axis=1)
        gh = jnp.swapaxes(gcum, 1, 2)
        diff = gh[..., :, None] - gh[..., None, :]
        dec = jnp.exp(jnp.where(incl, diff, -jnp.inf))
        kb = kc * bc[..., None]
        amat = jnp.where(strict, jnp.einsum('bthd,bshd->bhts', kb, kc) * dec, 0.0)
        tmat = lax.linalg.triangular_solve(amat + eye, jnp.broadcast_to(eye, amat.shape),
                                           left_side=True, lower=True)
        u = jnp.einsum('bhts,bshe->bthe', tmat, vc * bc[..., None])
        w = jnp.einsum('bhts,bshd->bthd', tmat, kb * jnp.exp(gcum)[..., None])
        v_new = u - jnp.einsum('bthd,bhde->bthe', w, s)
        attn = jnp.where(incl, jnp.einsum('bthd,bshd->bhts', qc, kc) * dec, 0.0)
        o = (jnp.einsum('bthd,bhde->bthe', qc * jnp.exp(gcum)[..., None], s)
             + jnp.einsum('bhts,bshe->bthe', attn, v_new))
        last = gcum[:, -1]
        kd = kc * jnp.exp(last[:, None, :] - gcum)[..., None]
        s_new = s * jnp.exp(last)[..., None, None] + jnp.einsum('bshd,bshe->bhde', kd, v_new)
        return s_new, o

    s_fin, os_ = lax.scan(step, s0, (to_chunks(q, L), to_chunks(k, L), to_chunks(v, L),
                                     to_chunks(g, L), to_chunks(beta, L)))
    return from_chunks(os_), s_fin


def mla_attention(q_lat, q_rope, kv_lat, k_rope, q_pos):
    nt = q_lat.shape[1]
    k_pos = jnp.arange(kv_lat.shape[1])

    def attend(blk):
        ql, qr, qp = blk
        s = (jnp.einsum('bthr,bsr->bhts', ql, kv_lat)
             + jnp.einsum('bthd,bsd->bhts', qr, k_rope)).astype(jnp.float32) * MLA_SCALE
        s = jnp.where(k_pos[None, :] <= qp[:, None], s, -jnp.inf)
        p = jax.nn.softmax(s, axis=-1).astype(kv_lat.dtype)
        return jnp.einsum('bhts,bsr->bthr', p, kv_lat)

    if nt > Q_BLOCK and nt % Q_BLOCK == 0:
        blocks = (to_chunks(q_lat, Q_BLOCK), to_chunks(q_rope, Q_BLOCK), q_pos.reshape(nt // Q_BLOCK, Q_BLOCK))
        return from_chunks(lax.map(attend, blocks))
    return attend((q_lat, q_rope, q_pos))


def mixer_layer(x, pos, past_lat, past_rope, ssd_conv_prev, ssd_prev, gdn_conv_prev, gdn_prev,
                w_in, ssd_conv_w, ssd_conv_b, ssd_dt_bias, ssd_a_log, ssd_d, ssd_norm_w,
                mla_q_norm_w, mla_w_uq, mla_kv_norm_w, mla_w_uk, mla_w_uv,
                gdn_conv_w, gdn_dt_bias, gdn_a_log, gdn_norm_w, w_out, ln_g, ln_b):
    nb, nt, _ = x.shape
    f32 = jnp.float32
    proj = x @ w_in
    (ssd_z, ssd_xbc, ssd_dt, mla_cq, mla_ckv, mla_kr, mla_gate,
     gdn_qkv, gdn_z, gdn_b, gdn_a) = jnp.split(proj, IN_OFFSETS, axis=-1)

    xbc, ssd_conv_new = causal_conv(ssd_xbc, ssd_conv_prev, ssd_conv_w)
    xbc = jax.nn.silu(xbc + ssd_conv_b)
    xs, bs, cs = jnp.split(xbc, (SSD_WIDTH, SSD_WIDTH + SSD_GROUPS * SSD_STATE), axis=-1)
    xh = xs.reshape(nb, nt, SSD_HEADS, SSD_HEAD_DIM).astype(f32)
    rep = SSD_HEADS // SSD_GROUPS
    bh = jnp.repeat(bs.reshape(nb, nt, SSD_GROUPS, SSD_STATE).astype(f32), rep, axis=2)
    ch = jnp.repeat(cs.reshape(nb, nt, SSD_GROUPS, SSD_STATE).astype(f32), rep, axis=2)
    dt = jax.nn.softplus(ssd_dt.astype(f32) + ssd_dt_bias)
    a = -jnp.exp(ssd_a_log.astype(f32))
    y, ssd_new = ssd_scan(xh, dt, a, bh, ch, ssd_prev.astype(f32))
    y = y + ssd_d.astype(f32)[:, None] * xh
    y = (y.reshape(nb, nt, SSD_WIDTH) * jax.nn.silu(ssd_z.astype(f32))).reshape(nb, nt, SSD_GROUPS, -1)
    y_ssd = rms_norm(y, ssd_norm_w.reshape(SSD_GROUPS, -1)).reshape(nb, nt, SSD_WIDTH).astype(x.dtype)

    cq = rms_norm(mla_cq, mla_q_norm_w)
    q = (cq @ mla_w_uq).reshape(nb, nt, MLA_HEADS, MLA_NOPE + MLA_ROPE)
    q_nope, q_rope = q[..., :MLA_NOPE], rope(q[..., MLA_NOPE:], pos)
    ckv = rms_norm(mla_ckv, mla_kv_norm_w)
    kr = rope(mla_kr[:, :, None, :], pos)[:, :, 0]
    q_lat = jnp.einsum('bthd,rhd->bthr', q_nope, mla_w_uk)
    keys_lat = jnp.concatenate([past_lat.astype(x.dtype), ckv], axis=1)
    keys_rope = jnp.concatenate([past_rope.astype(x.dtype), kr], axis=1)
    o_lat = mla_attention(q_lat, q_rope, keys_lat, keys_rope, pos)
    o = jnp.einsum('bthr,rhd->bthd', o_lat, mla_w_uv).reshape(nb, nt, MLA_WIDTH)
    y_mla = o * jax.nn.silu(mla_gate)

    qkv, gdn_conv_new = causal_conv(gdn_qkv, gdn_conv_prev, gdn_conv_w)
    qkv = jax.nn.silu(qkv).astype(f32).reshape(nb, nt, 3, GDN_HEADS, GDN_HEAD_DIM)
    gq = l2_normalize(qkv[:, :, 0]) * GDN_HEAD_DIM ** -0.5
    gk = l2_normalize(qkv[:, :, 1])
    gv = qkv[:, :, 2]
    beta = jax.nn.sigmoid(gdn_b.astype(f32))
    g = -jnp.exp(gdn_a_log.astype(f32)) * jax.nn.softplus(gdn_a.astype(f32) + gdn_dt_bias)
    go, gdn_new = gdn_scan(gq, gk, gv, g, beta, gdn_prev.astype(f32))
    go = rms_norm(go, gdn_norm_w) * jax.nn.silu(gdn_z.astype(f32).reshape(nb, nt, GDN_HEADS, GDN_HEAD_DIM))
    y_gdn = go.reshape(nb, nt, GDN_WIDTH).astype(x.dtype)

    mix = jnp.concatenate([y_ssd, y_mla, y_gdn], axis=-1)
    x_new = layer_norm(DEEPNORM_ALPHA * x + mix @ w_out, ln_g, ln_b)
    return x_new, (ckv, kr, ssd_conv_new, ssd_new.astype(x.dtype), gdn_conv_new, gdn_new.astype(x.dtype))


def trunk(x, pos, past_lat, past_rope, ssd_conv, ssd_state, gdn_conv, gdn_state,
          emb_ln_g, emb_ln_b, layer_weights):
    h = layer_norm(x, emb_ln_g, emb_ln_b)
    new = []
    for l in range(DEPTH):
        h, st = mixer_layer(h, pos, past_lat[l], past_rope[l], ssd_conv[l], ssd_state[l],
                            gdn_conv[l], gdn_state[l], *[w[l] for w in layer_weights])
        new.append(st)
    return h, tuple(jnp.stack(s) for s in zip(*new))


def setup_inputs(seed: int = 0) -> dict:
    key = jax.random.key(seed)
    ks = list(jax.random.split(key, 32))
    f32 = jnp.float32

    def nrm(i, shape, scale):
        return jax.random.normal(ks[i], shape, f32) * scale

    def gain(i, n):
        return 1.0 + nrm(i, (DEPTH, n), 0.02)

    def dt_bias(i, n):
        dt = jnp.exp(jax.random.uniform(ks[i], (DEPTH, n), f32, math.log(1e-3), math.log(1e-1)))
        return dt + jnp.log(-jnp.expm1(-dt))

    def a_log(i, n):
        return jnp.log(jax.random.uniform(ks[i], (DEPTH, n), f32, 1.0, 16.0))

    n_pages = PAST_LEN // PAGE_SIZE
    n_used = DEC_BATCH * n_pages
    n_phys = n_used + max(1, n_used // 4)
    page_table = jax.random.permutation(ks[0], n_phys)[:n_used].reshape(DEC_BATCH, n_pages).astype(jnp.int32)
    return {
        'x_prompt': nrm(1, (BATCH, SEQ, D_MODEL), 1.0),
        'x_sample': nrm(2, (DEC_BATCH, DEC_SEQ, D_MODEL), 1.0),
        'cache_kv_latent': nrm(3, (DEPTH, n_phys, PAGE_SIZE, MLA_KV_RANK), 1.0),
        'cache_k_rope': nrm(4, (DEPTH, n_phys, PAGE_SIZE, MLA_ROPE), 1.0),
        'state_ssd_conv': nrm(5, (DEPTH, DEC_BATCH, CONV_WIDTH - 1, SSD_CONV_DIM), 1.0),
        'state_ssd': nrm(6, (DEPTH, DEC_BATCH, SSD_HEADS, SSD_HEAD_DIM, SSD_STATE), 0.1),
        'state_gdn_conv': nrm(7, (DEPTH, DEC_BATCH, CONV_WIDTH - 1, GDN_CONV_DIM), 1.0),
        'state_gdn': nrm(8, (DEPTH, DEC_BATCH, GDN_HEADS, GDN_HEAD_DIM, GDN_HEAD_DIM), 0.1),
        'page_table': page_table,
        'emb_ln_g': 1.0 + nrm(9, (D_MODEL,), 0.02),
        'emb_ln_b': nrm(10, (D_MODEL,), 0.02),
        'w_in': nrm(11, (DEPTH, D_MODEL, IN_WIDTH), D_MODEL ** -0.5),
        'ssd_conv_w': nrm(12, (DEPTH, CONV_WIDTH, SSD_CONV_DIM), CONV_WIDTH ** -0.5),
        'ssd_conv_b': nrm(13, (DEPTH, SSD_CONV_DIM), 0.02),
        'ssd_dt_bias': dt_bias(14, SSD_HEADS),
        'ssd_a_log': a_log(15, SSD_HEADS),
        'ssd_d': gain(16, SSD_HEADS),
        'ssd_norm_w': gain(17, SSD_WIDTH),
        'mla_q_norm_w': gain(18, MLA_Q_RANK),
        'mla_w_uq': nrm(19, (DEPTH, MLA_Q_RANK, MLA_HEADS * (MLA_NOPE + MLA_ROPE)), MLA_Q_RANK ** -0.5),
        'mla_kv_norm_w': gain(20, MLA_KV_RANK),
        'mla_w_uk': nrm(21, (DEPTH, MLA_KV_RANK, MLA_HEADS, MLA_NOPE), MLA_KV_RANK ** -0.5),
        'mla_w_uv': nrm(22, (DEPTH, MLA_KV_RANK, MLA_HEADS, MLA_V), MLA_KV_RANK ** -0.5),
        'gdn_conv_w': nrm(23, (DEPTH, CONV_WIDTH, GDN_CONV_DIM), CONV_WIDTH ** -0.5),
        'gdn_dt_bias': dt_bias(24, GDN_HEADS),
        'gdn_a_log': a_log(25, GDN_HEADS),
        'gdn_norm_w': gain(26, GDN_HEAD_DIM),
        'w_out': nrm(27, (DEPTH, MIX_WIDTH, D_MODEL), DEEPNORM_BETA * MIX_WIDTH ** -0.5),
        'ln_g': gain(28, D_MODEL),
        'ln_b': nrm(29, (DEPTH, D_MODEL), 0.02),
    }


def reference(x_prompt, x_sample, cache_kv_latent, cache_k_rope, state_ssd_conv, state_ssd,
              state_gdn_conv, state_gdn, page_table, emb_ln_g, emb_ln_b, w_in, ssd_conv_w, ssd_conv_b,
              ssd_dt_bias, ssd_a_log, ssd_d, ssd_norm_w, mla_q_norm_w, mla_w_uq, mla_kv_norm_w,
              mla_w_uk, mla_w_uv, gdn_conv_w, gdn_dt_bias, gdn_a_log, gdn_norm_w, w_out, ln_g, ln_b):
    layer_weights = (w_in, ssd_conv_w, ssd_conv_b, ssd_dt_bias, ssd_a_log, ssd_d, ssd_norm_w,
                     mla_q_norm_w, mla_w_uq, mla_kv_norm_w, mla_w_uk, mla_w_uv,
                     gdn_conv_w, gdn_dt_bias, gdn_a_log, gdn_norm_w, w_out, ln_g, ln_b)
    dtype = x_prompt.dtype

    bp, tp, _ = x_prompt.shape
    pos_p = jnp.arange(tp)
    y_prompt, (p_lat, p_rope, p_ssd_conv, p_ssd, p_gdn_conv, p_gdn) = trunk(
        x_prompt, pos_p,
        jnp.zeros((DEPTH, bp, 0, MLA_KV_RANK), dtype), jnp.zeros((DEPTH, bp, 0, MLA_ROPE), dtype),
        jnp.zeros((DEPTH, bp, CONV_WIDTH - 1, SSD_CONV_DIM), dtype),
        jnp.zeros((DEPTH, bp, SSD_HEADS, SSD_HEAD_DIM, SSD_STATE), dtype),
        jnp.zeros((DEPTH, bp, CONV_WIDTH - 1, GDN_CONV_DIM), dtype),
        jnp.zeros((DEPTH, bp, GDN_HEADS, GDN_HEAD_DIM, GDN_HEAD_DIM), dtype),
        emb_ln_g, emb_ln_b, layer_weights)

    bs, ts, _ = x_sample.shape
    past_len = page_table.shape[1] * PAGE_SIZE
    pos_s = past_len + jnp.arange(ts)
    past_lat = [cache_kv_latent[l][page_table].reshape(bs, past_len, MLA_KV_RANK) for l in range(DEPTH)]
    past_rope = [cache_k_rope[l][page_table].reshape(bs, past_len, MLA_ROPE) for l in range(DEPTH)]
    y_sample, (s_lat, s_rope, s_ssd_conv, s_ssd, s_gdn_conv, s_gdn) = trunk(
        x_sample, pos_s, past_lat, past_rope, state_ssd_conv, state_ssd, state_gdn_conv, state_gdn,
        emb_ln_g, emb_ln_b, layer_weights)

    return (y_prompt, y_sample, p_lat, p_rope, p_ssd_conv, p_ssd, p_gdn_conv, p_gdn,
            s_lat, s_rope, s_ssd_conv, s_ssd, s_gdn_conv, s_gdn)
```

```python
import contextlib
import numpy as np
import ml_dtypes
import concourse.bass as bass
import concourse.mybir as mybir
from concourse.bass_utils import run_bass_kernel_spmd

F32 = mybir.dt.float32
BF16 = mybir.dt.bfloat16
I32 = mybir.dt.int32
AF = mybir.ActivationFunctionType
import os
SILU = AF.Silu if os.environ.get('NOSILU') is None else AF.Sigmoid
ALU = mybir.AluOpType
AX = mybir.AxisListType

D = 1024
DEPTH = 2
NCORES = 8
ALPHA = float((2 * DEPTH) ** 0.25)
LN_EPS, RMS_EPS, L2_EPS = 1e-5, 1e-6, 1e-6
OFF = dict(ssd_z=0, ssd_xbc=1024, ssd_dt=2560, cq=2576, ckv=2960, kr=3216, gate=3248,
           gdn_qkv=3760, gdn_z=5296, gdn_b=5808, gdn_a=5812, end=5816)
MLA_SCALE = float(96 ** -0.5)
NEG = -30000.0


class Buf:
    __slots__ = ("name", "writers", "readers", "dma", "base")

    def __init__(self, name):
        self.name = name
        self.writers = {}
        self.readers = {}
        self.base = {}
        self.dma = None


class Prog:
    CE = ("pe", "act", "dve", "pool")

    def __init__(self, nc, st, n_dma_sems=64):
        self.nc = nc
        self.eng = {"pe": nc.tensor, "act": nc.scalar, "dve": nc.vector, "pool": nc.gpsimd, "sp": nc.sync}
        self.esem = {e: st.enter_context(nc.semaphore("es_" + e)) for e in self.CE}
        self.cnt = {e: 0 for e in self.CE}
        self.waited = {e: {} for e in self.eng}
        self.dpool = [[st.enter_context(nc.semaphore("ds%d" % i)), 0] for i in range(n_dma_sems)]
        self.dfree = {"sp": list(range(0, n_dma_sems * 5 // 8)), "pool": list(range(n_dma_sems * 5 // 8, n_dma_sems))}
        self.dlive = {}
        self.psum_free = []
        self.n_inst = 0

    def _wait(self, eng, sem, val):
        w = self.waited[eng]
        if w.get(id(sem), 0) >= val:
            return
        w[id(sem)] = val
        self.eng[eng].wait_ge(sem, val)
        self.n_inst += 1

    def _deps(self, eng, reads, writes, accs):
        deps = []
        for b in reads:
            deps += [(d, "raw") for d in b.writers.values()]
            if b.name.startswith("psb"):
                deps += [(d, "war") for d in b.readers.values()]
        for b in writes:
            deps += [(d, "waw") for d in b.writers.values()]
            deps += [(d, "war") for d in b.readers.values()]
        for b in accs:
            deps += [(d, "war") for d in b.readers.values()]
            deps += [(d, "waw") for d in b.base.values()]
        for (sem, val, src), kind in deps:
            if src == eng and eng == "pe":
                continue
            self._wait(eng, sem, val)

    def _post(self, ev, reads, writes, accs):
        k = id(ev[0])
        for b in reads:
            b.readers[k] = ev
        for b in writes:
            b.writers = {k: ev}
            b.base = {k: ev}
            b.readers = {}
        for b in accs:
            b.writers[k] = ev

    def op(self, eng, fn, reads=(), writes=(), accs=()):
        self._deps(eng, reads, writes, accs)
        inst = fn(self.eng[eng])
        self.cnt[eng] += 1
        inst.then_inc(self.esem[eng], 1)
        self.n_inst += 1
        self._post((self.esem[eng], self.cnt[eng], eng), reads, writes, accs)

    def dma(self, q, fn, sbuf, reads=(), writes=(), accs=()):
        self._deps(q, reads, writes, accs)
        if sbuf.dma is None:
            sbuf.dma = {}
        if q not in sbuf.dma:
            sbuf.dma[q] = self.dfree[q].pop(0)
            self.dlive[(id(sbuf), q)] = (sbuf, q)
        ent = self.dpool[sbuf.dma[q]]
        inst = fn(self.eng[q])
        ent[1] += 16
        inst.then_inc(ent[0], 16)
        self.n_inst += 1
        self._post((ent[0], ent[1], "dma"), reads, writes, accs)

    def barrier(self, release=True):
        for e in self.eng:
            for e2 in self.CE:
                if e2 != e and self.cnt[e2] > 0:
                    self._wait(e, self.esem[e2], self.cnt[e2])
            for b, q in self.dlive.values():
                ent = self.dpool[b.dma[q]]
                self._wait(e, ent[0], ent[1])
        if release:
            for b, q in list(self.dlive.values()):
                self.dfree[q].append(b.dma[q])
                del b.dma[q]
            self.dlive = {}

    def ps_alloc(self):
        assert self.psum_free, "out of PSUM banks"
        return self.psum_free.pop(0)

    def ps_release(self, bank):
        self.psum_free.append(bank)


class T:
    __slots__ = ("t", "b")

    def __init__(self, t, name):
        self.t = t
        self.b = Buf(name)

    def __getitem__(self, k):
        return self.t[k]


def bc_last(ap, n):
    sh = list(ap.shape)
    return ap.unsqueeze(len(sh)).to_broadcast(sh + [n])


def host_consts(LP, NS, LS):
    c = {}
    c["ident_f"] = np.eye(128, dtype=np.float32)
    c["ident_b"] = np.eye(128, dtype=np.float32).astype(ml_dtypes.bfloat16)
    c["ones_f"] = np.ones((128, 128), np.float32)

    def pack(seq, pos, L):
        i = np.arange(L)
        same = seq[:, None] == seq[None, :]
        U = (same & (pos[:, None] <= pos[None, :])).astype(np.float32)
        mneg_st = np.where(same & (pos[:, None] <= pos[None, :]), 0.0, NEG).astype(np.float32)
        mneg_ts = np.where(same & (pos[None, :] < pos[:, None]), 0.0, NEG).astype(np.float32)
        m01_st = (same & (pos[:, None] <= pos[None, :])).astype(np.float32)
        out = np.zeros((128, 4, 128), np.float32)
        out[:L, 0, :L] = U
        out[:L, 1, :L] = mneg_st
        out[:L, 2, :L] = mneg_ts
        out[:L, 3, :L] = same.astype(np.float32)
        out[:, 1, :][out[:, 1, :] == 0] += 0.0
        out[L:, 1, :] = NEG
        out[L:, 2, :] = NEG
        out[:L, 1, L:] = NEG
        out[:L, 2, L:] = NEG
        return out, m01_st

    seq = np.zeros(LP, np.int64)
    pos = np.arange(LP)
    c["pk_p"], _ = pack(seq, pos, LP)
    Ls = NS * LS
    seq = np.arange(Ls) // LS
    pos = np.arange(Ls) % LS
    c["pk_s"], _ = pack(seq, pos, Ls)
    si = np.zeros((128, NS, 128), np.float32)
    for b in range(NS):
        si[b * LS:(b + 1) * LS, b, :] = 1.0
    c["seqind_s"] = si
    cm = np.zeros((128, NS, Ls), np.float32)
    for b in range(NS):
        cm[:, b, b * LS:(b + 1) * LS] = 1.0
    c["cmask_s"] = cm.astype(ml_dtypes.bfloat16)
    nl = 7
    lm = np.zeros((128, nl, 128), np.float32)
    i = np.arange(128)
    for j in range(nl):
        bsz = 1 << j
        same_pair = (i[:, None] // (2 * bsz)) == (i[None, :] // (2 * bsz))
        up = (i[:, None] % (2 * bsz)) >= bsz
        lo = (i[None, :] % (2 * bsz)) < bsz
        lm[:, j, :] = (same_pair & up & lo).astype(np.float32)
    c["lmask"] = lm
    c["lmaskT"] = np.ascontiguousarray(lm.transpose(2, 1, 0))
    return c


def rope_tables(positions):
    half = 16
    inv = (10000.0 ** (-np.arange(half, dtype=np.float32) / half)).astype(np.float32)
    ang = positions.astype(np.float32)[:, None] * inv[None, :]
    return np.cos(ang).astype(np.float32), np.sin(ang).astype(np.float32)


class K:
    def __init__(self, TP, NPG, NPHYS, debug=()):
        self.TP, self.NPG, self.NPHYS = TP, NPG, NPHYS
        self.NT = TP // 128
        self.NS, self.LS = 4, 4
        self.LSS = 16
        self.debug = set(debug)
        self.cut = None
        self.nc = nc = bass.Bass("TRN2", target_bir_lowering=False)
        self.st = contextlib.ExitStack()
        self.P = Prog(nc, self.st)
        self.uid = 0
        self.inputs = {}
        self.outputs = {}

    def din(self, name, shape, dt=F32):
        ap = self.nc.dram_tensor(name, list(shape), dt, kind="ExternalInput").ap()
        self.inputs[name] = ap
        return ap

    def dout(self, name, shape, dt=F32):
        ap = self.nc.dram_tensor(name, list(shape), dt, kind="ExternalOutput").ap()
        self.outputs[name] = ap
        return ap

    def tile(self, scope, shape, dt=F32, name="t"):
        self.uid += 1
        nm = "%s_%d" % (name, self.uid)
        return T(scope.enter_context(self.nc.sbuf_tensor(nm, list(shape), dt)), nm)

    def mm(self, out, lhsT, rhs, start, stop, reads, writes):
        self.P.op("pe", lambda e: e.matmul(out, lhsT, rhs, start=start, stop=stop),
                  reads=[x.b for x in reads], writes=[x.b for x in writes])

    def tr(self, out, in_, ident, reads, writes):
        self.P.op("pe", lambda e: e.transpose(out, in_, ident),
                  reads=[x.b for x in reads], writes=[x.b for x in writes])

    def act(self, out, in_, func, reads, writes, bias=None, scale=None, accum_out=None, accs=()):
        kw = {}
        if bias is not None:
            kw["bias"] = bias
        if scale is not None:
            kw["scale"] = scale
        if accum_out is not None:
            kw["accum_out"] = accum_out
        self.P.op("act", lambda e: e.activation(out, in_, func, **kw),
                  reads=[x.b for x in reads], writes=[x.b for x in writes], accs=[x.b for x in accs])

    def tt(self, eng, out, in0, in1, op, reads, writes, accs=()):
        self.P.op(eng, lambda e: e.tensor_tensor(out, in0, in1, op),
                  reads=[x.b for x in reads], writes=[x.b for x in writes], accs=[x.b for x in accs])

    def ts(self, eng, out, in0, s1, s2, op0, op1, reads, writes, accs=()):
        if op1 is None:
            f = lambda e: e.tensor_scalar(out, in0, s1, None, op0)
        else:
            f = lambda e: e.tensor_scalar(out, in0, s1, s2, op0, op1)
        self.P.op(eng, f, reads=[x.b for x in reads], writes=[x.b for x in writes], accs=[x.b for x in accs])

    def stt(self, out, in0, scalar, in1, op0, op1, reads, writes, accs=()):
        self.P.op("dve", lambda e: e.scalar_tensor_tensor(out, in0, scalar, in1, op0, op1),
                  reads=[x.b for x in reads], writes=[x.b for x in writes], accs=[x.b for x in accs])

    def cp(self, eng, out, in_, reads, writes, accs=()):
        if eng == "act":
            f = lambda e: e.copy(out, in_)
        else:
            f = lambda e: e.tensor_copy(out, in_)
        self.P.op(eng, f, reads=[x.b for x in reads], writes=[x.b for x in writes], accs=[x.b for x in accs])

    def memset(self, eng, t, ap, val):
        self.P.op(eng, lambda e: e.memset(ap, val), writes=[t.b])

    def load(self, q, t, out, in_, accs=False, **kw):
        self.P.dma(q, lambda e: e.dma_start(out=out, in_=in_, **kw), t.b,
                   writes=[] if accs else [t.b], accs=[t.b] if accs else [])

    def store(self, q, t, out, in_, **kw):
        self.P.dma(q, lambda e: e.dma_start(out=out, in_=in_, **kw), t.b, reads=[t.b])

    def dbg(self, name, t, ap, shape, dt=F32):
        if name in self.debug:
            o = self.dout("dbg_" + name, shape, dt)
            self.store("sp", t, o, ap)


def setup(k):
    nc, st, TP, NPG, NPHYS = k.nc, k.st, k.TP, k.NPG, k.NPHYS
    NTOK = TP + k.LSS
    k.NTOK = NTOK
    i = k.i = {}
    i["x_all"] = k.din("x_all", [NTOK, D])
    i["cache_cat"] = [k.din("cache_cat%d" % l_, [NPHYS, 128, 288]) for l_ in range(DEPTH)]
    i["st_ssd_conv"] = k.din("st_ssd_conv", [DEPTH, 4, 3, 1536])
    i["st_ssd"] = k.din("st_ssd", [DEPTH, 4, 1024, 128])
    i["st_gdn_conv"] = k.din("st_gdn_conv", [DEPTH, 4, 3, 1536])
    i["st_gdn"] = k.din("st_gdn", [DEPTH, 4, 4, 128, 128])
    i["page_table"] = k.din("page_table", [4, NPG], I32)
    for nm, sh in [("emb_ln_g", [D]), ("emb_ln_b", [D]), ("w_in", [DEPTH, D, 5816]),
                   ("ssd_conv_w", [DEPTH, 4, 1536]), ("ssd_conv_b", [DEPTH, 1536]),
                   ("ssd_dt_bias", [DEPTH, 16]), ("ssd_a_log", [DEPTH, 16]), ("ssd_d", [DEPTH, 16]),
                   ("ssd_norm_w", [DEPTH, 1024]), ("mla_q_norm_w", [DEPTH, 384]),
                   ("mla_w_uq", [DEPTH, 384, 768]), ("mla_kv_norm_w", [DEPTH, 256]),
                   ("mla_w_uk", [DEPTH, 256, 512]), ("mla_w_uv", [DEPTH, 256, 512]),
                   ("gdn_conv_w", [DEPTH, 4, 1536]), ("gdn_dt_bias", [DEPTH, 4]),
                   ("gdn_a_log", [DEPTH, 4]), ("gdn_norm_w", [DEPTH, 128]),
                   ("w_out", [DEPTH, 2048, D]), ("ln_g", [DEPTH, D]), ("ln_b", [DEPTH, D])]:
        i[nm] = k.din(nm, sh)
    i["ident_f"] = k.din("ident_f", [128, 128])
    i["ident_b"] = k.din("ident_b", [128, 128], BF16)
    i["ones_f"] = k.din("ones_f", [128, 128])
    i["pk_p"] = k.din("pk_p", [128, 4, 128])
    i["pk_s"] = k.din("pk_s", [128, 4, 128])
    i["seqind_s"] = k.din("seqind_s", [128, 4, 128])
    i["cmask_s"] = k.din("cmask_s", [128, 4, 16], BF16)
    i["lmask"] = k.din("lmask", [128, 7, 128])
    i["lmaskT"] = k.din("lmaskT", [128, 7, 128])
    i["cos_fm"] = k.din("cos_fm", [32, NTOK])
    i["sin_fm"] = k.din("sin_fm", [32, NTOK])
    i["cos_tm"] = k.din("cos_tm", [NTOK, 16])
    i["sin_tm"] = k.din("sin_tm", [NTOK, 16])
    i["smask"] = k.din("smask", [16, 128])
    i["iota_p"] = k.din("iota_p", [128, 1])
    o = k.o = {}
    o["y_all"] = k.dout("y_all", [NTOK, D])
    o["p_lat"] = k.dout("p_lat", [DEPTH, TP, 256])
    o["p_rope"] = k.dout("p_rope", [DEPTH, TP, 32])
    o["p_ssd_conv"] = k.dout("p_ssd_conv", [DEPTH, 1, 3, 1536])
    o["p_ssd"] = k.dout("p_ssd", [DEPTH, 1, 1024, 128])
    o["p_gdn_conv"] = k.dout("p_gdn_conv", [DEPTH, 1, 3, 1536])
    o["p_gdn"] = k.dout("p_gdn", [DEPTH, 1, 4, 128, 128])
    o["s_lat"] = k.dout("s_lat", [DEPTH, 16, 256])
    o["s_rope"] = k.dout("s_rope", [DEPTH, 16, 32])
    o["s_ssd_conv"] = k.dout("s_ssd_conv", [DEPTH, 4, 3, 1536])
    o["s_ssd"] = k.dout("s_ssd", [DEPTH, 4, 1024, 128])
    o["s_gdn_conv"] = k.dout("s_gdn_conv", [DEPTH, 4, 3, 1536])
    o["s_gdn"] = k.dout("s_gdn", [DEPTH, 4, 4, 128, 128])
    k.r_dram = k.dout("r_scratch", [NTOK, D])
    k.hT = k.tile(st, [128, 8, NTOK], BF16, "hT")
    c = k.c = {}
    for nm, sh, dt in [("ident_f", [128, 128], F32), ("ident_b", [128, 128], BF16), ("ones_f", [128, 128], F32),
                       ("pk_p", [128, 4, 128], F32), ("pk_s", [128, 4, 128], F32)]:
        c[nm] = k.tile(st, sh, dt, nm)
        k.load("sp", c[nm], c[nm][:], i[nm])
    k.banks = []
    for b in range(8):
        t = T(st.enter_context(nc.psum_tensor("psb%d" % b, [128, 512], F32)), "psb%d" % b)
        k.banks.append(t)
        k.P.psum_free.append(t)
    k.tiles = [(t * 128, 128, 1, 128, "pk_p") for t in range(k.NT)] + [(TP, 16, 4, 4, "pk_s")]


def bf_view(bank_ap):
    return bank_ap.bitcast(BF16)


def boundary(k, l):
    P, i = k.P, k.i
    with contextlib.ExitStack() as ph:
        gb = k.tile(ph, [128, 2, D], F32, "gb")
        if l == 0:
            g_src, b_src = i["emb_ln_g"], i["emb_ln_b"]
        else:
            g_src, b_src = i["ln_g"][l - 1], i["ln_b"][l - 1]
        k.load("sp", gb, gb[:, 0, :], g_src.partition_broadcast(128))
        k.load("sp", gb, gb[:, 1, :], b_src.partition_broadcast(128), accs=True)
        xs = [k.tile(ph, [128, D], F32, "xs") for _ in range(2)]
        hs = [k.tile(ph, [128, D], F32, "hs") for _ in range(2)]
        hb = [k.tile(ph, [128, D], BF16, "hb") for _ in range(2)]
        sm = [k.tile(ph, [128, 16], F32, "sm") for _ in range(2)]
        for ti, (r0, L, NS, Lb, pk) in enumerate(k.tiles):
            x, h, hbt, s = xs[ti % 2], hs[ti % 2], hb[ti % 2], sm[ti % 2]
            src = i["x_all"] if l == 0 else k.r_dram
            k.load("sp", x, x[0:L, :], src[r0:r0 + L, :])
            P.op("dve", lambda e: e.bn_stats(s[0:L, 0:6], x[0:L, 0:512]), reads=[x.b], writes=[s.b])
            P.op("dve", lambda e: e.bn_stats(s[0:L, 6:12], x[0:L, 512:1024]), reads=[x.b], accs=[s.b])
            P.op("dve", lambda e: e.bn_aggr(s[0:L, 12:14], s[0:L, 0:12]), reads=[s.b], accs=[s.b])
            k.ts("dve", s[0:L, 14:15], s[0:L, 13:14], LN_EPS, None, ALU.add, None, [s], [], accs=[s])
            k.act(s[0:L, 15:16], s[0:L, 14:15], AF.Ln, [s], [], accs=[s])
            k.act(s[0:L, 14:15], s[0:L, 15:16], AF.Exp, [s], [], scale=-0.5, accs=[s])
            k.ts("dve", h[0:L, :], x[0:L, :], s[0:L, 12:13], s[0:L, 14:15], ALU.subtract, ALU.mult, [x, s], [h])
            k.tt("pool", h[0:L, :], h[0:L, :], gb[0:L, 0, :], ALU.mult, [h, gb], [h])
            k.tt("dve", h[0:L, :], h[0:L, :], gb[0:L, 1, :], ALU.add, [h, gb], [h])
            if l == DEPTH:
                k.store("sp", h, k.o["y_all"][r0:r0 + L, :], h[0:L, :])
                continue
            P.op("act", lambda e: e.mul(x[0:L, :], h[0:L, :], ALPHA), reads=[h.b], writes=[x.b])
            k.store("sp", x, k.r_dram[r0:r0 + L, :], x[0:L, :])
            k.cp("pool", hbt[0:L, :], h[0:L, :], [h], [hbt])
            bank = P.ps_alloc()
            bv = bf_view(bank[:, :])
            for kk in range(8):
                k.tr(bv[:, kk * 128:kk * 128 + L], hbt[0:L, kk * 128:(kk + 1) * 128], k.c["ident_b"][0:L, 0:L],
                     [hbt, k.c["ident_b"]], [bank])
            k.cp("act", k.hT[:, :, r0:r0 + L], bv[:, 0:1024].rearrange("p (c t) -> p c t", c=8)[:, :, 0:L],
                 [bank], [], accs=[k.hT])
            P.ps_release(bank)
            if "hT" in k.debug and ti == 0:
                pass
        P.barrier()


def finish(k):
    P = k.P
    P.barrier(release=False)
    k.st.close()


def load_w_cast(k, t, dst3, src2, ncols, kchunks=8):
    for kk in range(kchunks):
        k.load("pool", t, dst3[:, kk, 0:ncols], src2[kk * 128:(kk + 1) * 128, :], accs=True, max_dma_last_dim=2048)


def fm_vec(k, t, dst, src1d, nchunk):
    with k.nc.allow_non_contiguous_dma(reason="tiny per-partition parameter vectors"):
        k.load("sp", t, dst, src1d.rearrange("(c p) -> p c", p=128), accs=True)


def conv_prep(k, ph, conv_w, conv_b):
    cw = k.tile(ph, [128, 12, 4], F32, "cw")
    for kk in range(4):
        with k.nc.allow_non_contiguous_dma(reason="tiny conv taps"):
            k.load("sp", cw, cw[:, :, kk], conv_w[kk].rearrange("(c p) -> p c", p=128), accs=True)
    cb = None
    if conv_b is not None:
        cb = k.tile(ph, [128, 12], F32, "cb")
        fm_vec(k, cb, cb[:, :], conv_b, 12)
    dg = k.tile(ph, [128, 12, 4, 128], BF16, "dg")
    for c in range(12):
        for kk in range(4):
            eng = "pool" if (c + kk) % 2 else "dve"
            k.ts(eng, dg[:, c, kk, :], k.c["ident_f"][:, :], cw[:, c, kk:kk + 1], None, ALU.mult, None,
                 [k.c["ident_f"], cw], [], accs=[dg])
    return dg, cw, cb


def conv_tile(k, tl, pre, prev_pre, wx, dg, first, st_in, csout, want_state):
    P = k.P
    r0, L, NS, Lb, pk = tl
    if first:
        if st_in is None:
            k.memset("pool", pre, pre[:, :, :, 0:3], 0.0)
        else:
            tmp = st_in
            k.cp("pool", pre[:, :, :, 0:3], tmp[:, :, :, :], [tmp], [pre])
    else:
        k.cp("pool", pre[:, :, :, 0:3], prev_pre[:, :, :, Lb:Lb + 3], [prev_pre], [pre])
    for cg in range(3):
        bank = P.ps_alloc()
        for cc in range(4):
            c = cg * 4 + cc
            for kk in range(8):
                k.mm(bank[:, cc * 128:cc * 128 + L], wx[:, kk, c * 128:(c + 1) * 128], k.hT[:, kk, r0:r0 + L],
                     kk == 0, kk == 7, [wx, k.hT], [bank])
        src = bank[:, :].rearrange("p (c t) -> p c t", c=4)[:, :, 0:L].rearrange("p c (b t) -> p c b t", b=NS)
        k.cp("act", pre[:, cg * 4:(cg + 1) * 4, :, 3:3 + Lb], src, [bank], [], accs=[pre])
        if want_state:
            k.cp("dve", csout[:, cg * 4:(cg + 1) * 4, :, :], src[:, :, :, Lb - 3:Lb], [bank], [], accs=[csout])
        P.ps_release(bank)
    outs = []
    for cg in range(3):
        bank = P.ps_alloc()
        for cc in range(4):
            c = cg * 4 + cc
            for kk in range(4):
                k.mm(bank[:, cc * 128:cc * 128 + L].rearrange("p (b t) -> p b t", b=NS), dg[:, c, kk, :],
                     pre[:, c, :, kk:kk + Lb], kk == 0, kk == 3, [dg, pre], [bank])
        outs.append(bank)
    return outs


def conv_state_io(k, ph, st_dram, NS):
    t = k.tile(ph, [128, 12, NS, 3], F32, "cst")
    with k.nc.allow_non_contiguous_dma(reason="small conv state"):
        for b in range(NS):
            for j in range(3):
                k.load("sp", t, t[:, :, b, j], st_dram[b, j].rearrange("(c p) -> p c", p=128), accs=True)
    return t


def conv_state_store(k, csout, out_dram, NS):
    with k.nc.allow_non_contiguous_dma(reason="small conv state"):
        for b in range(NS):
            for j in range(3):
                k.store("sp", csout, out_dram[b, j].rearrange("(c p) -> p c", p=128), csout[:, :, b, j])


def out_proj_add(k, ph_tiles, tl, yT, nk, wo):
    P = k.P
    r0, L, NS, Lb, pk = tl
    rt = ph_tiles
    k.load("sp", rt, rt[0:L, :], k.r_dram[r0:r0 + L, :])
    for nb in range(2):
        bank = P.ps_alloc()
        for kk in range(nk):
            k.mm(bank[0:L, :], yT[:, kk, 0:L], wo[:, kk, nb * 512:(nb + 1) * 512], kk == 0, kk == nk - 1, [yT, wo], [bank])
        k.tt("dve", rt[0:L, nb * 512:(nb + 1) * 512], rt[0:L, nb * 512:(nb + 1) * 512], bank[0:L, :], ALU.add,
             [rt, bank], [], accs=[rt])
        P.ps_release(bank)
    k.store("sp", rt, k.r_dram[r0:r0 + L, :], rt[0:L, :])


def ssd_phase(k, l):
    P, i, o, c = k.P, k.i, k.o, k.c
    with contextlib.ExitStack() as ph:
        wz = k.tile(ph, [128, 8, 1024], BF16, "wz")
        wx = k.tile(ph, [128, 8, 1536], BF16, "wx")
        wdt = k.tile(ph, [128, 8, 16], BF16, "wdt")
        wo = k.tile(ph, [128, 8, 1024], BF16, "wo")
        load_w_cast(k, wz, wz, i["w_in"][l][:, OFF["ssd_z"]:OFF["ssd_z"] + 1024], 1024)
        load_w_cast(k, wx, wx, i["w_in"][l][:, OFF["ssd_xbc"]:OFF["ssd_xbc"] + 1536], 1536)
        load_w_cast(k, wdt, wdt, i["w_in"][l][:, OFF["ssd_dt"]:OFF["ssd_dt"] + 16], 16)
        load_w_cast(k, wo, wo, i["w_out"][l][0:1024, :], 1024)
        dg, cw, cb = conv_prep(k, ph, i["ssd_conv_w"][l], i["ssd_conv_b"][l])
        hp = k.tile(ph, [128, 3, 16], F32, "hp")
        k.load("sp", hp, hp[:, 0, :], i["ssd_dt_bias"][l].partition_broadcast(128))
        k.load("sp", hp, hp[:, 1, :], i["ssd_a_log"][l].partition_broadcast(128), accs=True)
        k.load("sp", hp, hp[:, 2, :], i["ssd_d"][l].partition_broadcast(128), accs=True)
        k.act(hp[:, 1, :], hp[:, 1, :], AF.Exp, [hp], [hp])
        k.ts("dve", hp[:, 1, :], hp[:, 1, :], -1.0, None, ALU.mult, None, [hp], [hp])
        nw = k.tile(ph, [128, 8], F32, "nw")
        fm_vec(k, nw, nw[:, :], i["ssd_norm_w"][l], 8)
        seqind = k.tile(ph, [128, 4, 128], F32, "seqind")
        k.load("sp", seqind, seqind[:], i["seqind_s"])
        cmask = k.tile(ph, [128, 4, 16], BF16, "cmask")
        k.load("sp", cmask, cmask[:], i["cmask_s"])
        W = {}
        for nm, sh, dt in [("zs", [128, 1024], BF16), ("xs_fm", [128, 8, 128], F32), ("B_fm", [128, 2, 128], BF16),
                           ("C_fm", [128, 2, 128], BF16), ("x_tok", [128, 16, 64], F32), ("B_tok", [128, 2, 128], BF16),
                           ("sm", [128, 8, 16], F32), ("xdt", [128, 16, 64], BF16), ("xdtw", [128, 16, 64], BF16),
                           ("adtb", [128, 4, 128], F32), ("tmp", [128, 4, 128], F32), ("dec", [128, 4, 128], F32),
                           ("scT", [128, 16, 128], BF16), ("t1", [128, 16, 64], F32), ("y", [128, 16, 64], F32),
                           ("yn", [128, 1024], BF16), ("yT", [128, 8, 128], BF16), ("rt", [128, 1024], F32),
                           ("elast", [128, 4, 16], F32), ("junk", [128, 512], BF16), ("Bm", [128, 2, 128], BF16),
                           ("Cm", [128, 4, 2, 16], BF16)]:
            W[nm] = k.tile(ph, sh, dt, nm)

        if k.cut == "loads":
            P.barrier()
            return

        def run_stream(tiles, NS, Lb, st_conv_dram, st_dram, out_conv, out_st):
            with contextlib.ExitStack() as sp:
                pres = [k.tile(sp, [128, 12, NS, Lb + 3], BF16, "pre") for _ in range(2)]
                csout = k.tile(sp, [128, 12, NS, 3], F32, "csout")
                hst = [k.tile(sp, [128, 2, 512], F32, "hst") for _ in range(NS)]
                hsb = [k.tile(sp, [128, 2, 512], BF16, "hsb") for _ in range(NS)]
                st_in = None
                if st_dram is None:
                    for b in range(NS):
                        k.memset("pool", hst[b], hst[b][:], 0.0)
                        k.memset("pool", hsb[b], hsb[b][:], 0.0)
                else:
                    if not os.environ.get('NOCSIN'):
                        st_in = conv_state_io(k, sp, st_conv_dram, NS)
                    stg = k.tile(sp, [128, 8, 128], F32, "stg")
                    for b in range(NS if not os.environ.get('NOSTIN') else 0):
                        k.load("sp", stg, stg[:], st_dram[b].rearrange("(j p) n -> p j n", p=128))
                        for half in range(2):
                            bank = P.ps_alloc()
                            for jj in range(4):
                                j = half * 4 + jj
                                k.tr(bank[:, jj * 128:(jj + 1) * 128], stg[:, j, :], c["ident_f"][:, :], [stg, c["ident_f"]], [bank])
                            k.cp("dve", hst[b][:, half, :], bank[:, :], [bank], [], accs=[hst[b]])
                            k.cp("act", hsb[b][:, half, :], hst[b][:, half, :], [hst[b]], [], accs=[hsb[b]])
                            P.ps_release(bank)
                for ti, tl in enumerate(tiles):
                    ssd_tile(k, l, tl, W, pres[ti % 2], pres[(ti + 1) % 2], ti == 0, st_in, csout, ti == len(tiles) - 1,
                             hst, hsb, wz, wx, wdt, wo, dg, cb, hp, nw, seqind, cmask)
                if not os.environ.get('NOCS'):
                    conv_state_store(k, csout, out_conv, NS)
                stg2 = k.tile(sp, [128, 8, 128], F32, "stg2")
                for b in range(NS if not os.environ.get('NOSTOUT') else 0):
                    for half in range(2):
                        bank = P.ps_alloc()
                        for jj in range(4):
                            k.tr(bank[:, jj * 128:(jj + 1) * 128], hst[b][:, half, jj * 128:(jj + 1) * 128], c["ident_f"][:, :],
                                 [hst[b], c["ident_f"]], [bank])
                        k.cp("dve", stg2[:, half * 4:(half + 1) * 4, :], bank[:, :].rearrange("p (j n) -> p j n", j=4),
                             [bank], [stg2] if half == 0 else [], accs=[] if half == 0 else [stg2])
                        P.ps_release(bank)
                    k.store("sp", stg2, out_st[b].rearrange("(j p) n -> p j n", p=128), stg2[:])
                P.barrier()

        run_stream(k.tiles[:k.NT], 1, 128, None, None, o["p_ssd_conv"][l], o["p_ssd"][l])
        if not os.environ.get("NOSAMPLE"):
            run_stream(k.tiles[k.NT:], 4, 4, i["st_ssd_conv"][l], i["st_ssd"][l], o["s_ssd_conv"][l], o["s_ssd"][l])
        P.barrier()


def ssd_tile(k, l, tl, W, pre, prev_pre, first, st_in, csout, last, hst, hsb, wz, wx, wdt, wo, dg, cb, hp, nw, seqind, cmask):
    P, c = k.P, k.c
    r0, L, NS, Lb, pkn = tl
    pk = c[pkn]
    U, MNEG, SAME = pk[0:L, 0, 0:L], pk[0:L, 1, 0:L], pk[0:L, 3, 0:L]
    sm = W["sm"]
    zs = W["zs"]
    for nb in range(2):
        bank = P.ps_alloc()
        for kk in range(8):
            k.mm(bank[0:L, :], k.hT[:, kk, r0:r0 + L], wz[:, kk, nb * 512:(nb + 1) * 512], kk == 0, kk == 7, [k.hT, wz], [bank])
        k.act(zs[0:L, nb * 512:(nb + 1) * 512], bank[0:L, :], SILU, [bank], [zs] if nb == 0 else [], accs=[] if nb == 0 else [zs])
        P.ps_release(bank)
    if k.cut is not None and 2 > int(k.cut):
        return
    cbanks = conv_tile(k, tl, pre, prev_pre, wx, dg, first, st_in, csout, last)
    xs_fm, B_fm, C_fm = W["xs_fm"], W["B_fm"], W["C_fm"]
    for cg in range(3):
        bank = cbanks[cg]
        for cc in range(4):
            ch = cg * 4 + cc
            if ch < 8:
                dst, dt_ = xs_fm[:, ch, 0:L], xs_fm
            elif ch < 10:
                dst, dt_ = B_fm[:, ch - 8, 0:L], B_fm
            else:
                dst, dt_ = C_fm[:, ch - 10, 0:L], C_fm
            k.act(dst, bank[:, cc * 128:cc * 128 + L], SILU, [bank, cb], [], bias=cb[:, ch:ch + 1], accs=[dt_])
        P.ps_release(bank)
    if k.cut is not None and 3 > int(k.cut):
        return
    x_tok, B_tok = W["x_tok"], W["B_tok"]
    for half in range(2):
        bank = P.ps_alloc()
        for jj in range(4):
            ch = half * 4 + jj
            k.tr(bank[0:L, jj * 128:(jj + 1) * 128], xs_fm[:, ch, 0:L], c["ident_f"][:, :], [xs_fm, c["ident_f"]], [bank])
        k.cp("act" if half else "dve", x_tok[0:L, half * 8:(half + 1) * 8, :],
             bank[0:L, :].rearrange("p (h d) -> p h d", h=8), [bank], [], accs=[x_tok])
        P.ps_release(bank)
    bank = P.ps_alloc()
    bv = bf_view(bank[:, :])
    for g in range(2):
        k.tr(bv[0:L, g * 128:(g + 1) * 128], B_fm[:, g, 0:L], c["ident_b"][:, :], [B_fm, c["ident_b"]], [bank])
    k.cp("dve", B_tok[0:L, :, :], bv[0:L, 0:256].rearrange("p (g n) -> p g n", g=2), [bank], [B_tok])
    P.ps_release(bank)
    if k.cut is not None and 4 > int(k.cut):
        return
    bank = P.ps_alloc()
    for kk in range(8):
        k.mm(bank[0:L, 0:16], k.hT[:, kk, r0:r0 + L], wdt[:, kk, :], kk == 0, kk == 7, [k.hT, wdt], [bank])
    k.tt("dve", sm[0:L, 6, :], bank[0:L, 0:16], hp[0:L, 0, :], ALU.add, [bank, hp], [sm])
    k.act(sm[0:L, 6, :], sm[0:L, 6, :], AF.Exp, [sm], [sm])
    k.act(sm[0:L, 0, :], sm[0:L, 6, :], AF.Ln, [sm], [sm], bias=1.0)
    k.tt("dve", sm[0:L, 1, :], sm[0:L, 0, :], hp[0:L, 1, :], ALU.mult, [sm, hp], [sm])
    k.mm(bank[0:L, 16:32], U, sm[0:L, 1, :], True, True, [pk, sm], [bank])
    k.mm(bank[0:L, 32:48], SAME, sm[0:L, 1, :], True, True, [pk, sm], [bank])
    elast = W["elast"]
    for b in range(NS):
        lhs = c["ones_f"][0:L, :] if NS == 1 else seqind[0:L, b, :]
        k.mm(bank[:, 64 + b * 16:64 + (b + 1) * 16], lhs, sm[0:L, 1, :], True, True, [c["ones_f"], seqind, sm], [bank])
    k.cp("dve", sm[0:L, 2, :], bank[0:L, 16:32], [bank], [sm])
    k.ts("dve", sm[0:L, 3, :], bank[0:L, 16:32], -1.0, None, ALU.mult, None, [bank], [sm])
    k.act(sm[0:L, 4, :], bank[0:L, 16:32], AF.Exp, [bank], [sm])
    k.tt("dve", sm[0:L, 6, :], bank[0:L, 32:48], sm[0:L, 2, :], ALU.subtract, [bank, sm], [sm])
    k.act(sm[0:L, 6, :], sm[0:L, 6, :], AF.Exp, [sm], [sm])
    k.tt("dve", sm[0:L, 5, :], sm[0:L, 6, :], sm[0:L, 0, :], ALU.mult, [sm], [sm])
    k.act(elast[:, 0:NS, :], bank[:, 64:64 + NS * 16].rearrange("p (b h) -> p b h", b=NS), AF.Exp, [bank], [elast])
    P.ps_release(bank)
    xdt, xdtw = W["xdt"], W["xdtw"]
    k.tt("dve", xdt[0:L, :, :], x_tok[0:L, :, :], bc_last(sm[0:L, 0, :], 64), ALU.mult, [x_tok, sm], [xdt])
    k.tt("pool", xdtw[0:L, :, :], x_tok[0:L, :, :], bc_last(sm[0:L, 5, :], 64), ALU.mult, [x_tok, sm], [xdtw])
    if k.cut is not None and 5 > int(k.cut):
        return
    adtb, tmp, dec, scT = W["adtb"], W["tmp"], W["dec"], W["scT"]
    cbb = P.ps_alloc()
    for g in range(2):
        k.mm(cbb[0:L, g * 128:g * 128 + L], B_fm[:, g, 0:L], C_fm[:, g, 0:L], True, True, [B_fm, C_fm], [cbb])
    for hq in range(4):
        k.cp("pool", adtb[0:L, :, 0:L], bc_last(sm[0:L, 1, hq * 4:(hq + 1) * 4], L), [sm], [adtb])
        bank = P.ps_alloc()
        for hh in range(4):
            h = hq * 4 + hh
            k.mm(bank[0:L, hh * 128:hh * 128 + L], adtb[0:L, hh, 0:L], U, True, True, [adtb, pk], [bank])
        bview = bank[0:L, :].rearrange("p (h t) -> p h t", h=4)[:, :, 0:L]
        k.tt("dve", tmp[0:L, :, 0:L], bview, MNEG.unsqueeze(1).to_broadcast([L, 4, L]), ALU.add, [bank, pk], [tmp])
        P.ps_release(bank)
        for hh in range(4):
            h = hq * 4 + hh
            k.act(dec[0:L, hh, 0:L], tmp[0:L, hh, 0:L], AF.Exp, [tmp, sm], [dec] if hh == 0 else [], bias=sm[0:L, 3, h:h + 1],
                  accs=[] if hh == 0 else [dec])
        g = hq // 2
        k.tt("dve", scT[0:L, hq * 4:(hq + 1) * 4, 0:L], dec[0:L, :, 0:L],
             cbb[0:L, g * 128:g * 128 + L].unsqueeze(1).to_broadcast([L, 4, L]), ALU.mult, [dec, cbb], [], accs=[scT])
    P.ps_release(cbb)
    if k.cut is not None and 6 > int(k.cut):
        return
    yb = [P.ps_alloc(), P.ps_alloc()]
    for h in range(16):
        g = h // 8
        k.mm(yb[g][0:L, (h % 8) * 64:(h % 8 + 1) * 64], scT[0:L, h, 0:L], xdt[0:L, h, :], True, True, [scT, xdt], [yb[g]])
    Cm = W["Cm"]
    if NS > 1:
        for b in range(NS):
            k.tt("pool", Cm[:, b, :, 0:L], C_fm[:, :, 0:L], cmask[:, b, 0:L].unsqueeze(1).to_broadcast([128, 2, L]),
                 ALU.mult, [C_fm, cmask], [], accs=[Cm])
    t1, y = W["t1"], W["y"]
    for g in range(2):
        bank = P.ps_alloc()
        for b in range(NS):
            lhs = C_fm[:, g, 0:L] if NS == 1 else Cm[:, b, g, 0:L]
            k.mm(bank[0:L, :], lhs, hsb[b][:, g, :], b == 0, b == NS - 1, [C_fm, Cm, hsb[b]], [bank])
        k.tt("dve", t1[0:L, g * 8:(g + 1) * 8, :], bank[0:L, :].rearrange("p (h d) -> p h d", h=8),
             bc_last(sm[0:L, 4, g * 8:(g + 1) * 8], 64), ALU.mult, [bank, sm], [], accs=[t1])
        P.ps_release(bank)
    k.tt("pool", y[0:L, :, :], x_tok[0:L, :, :], bc_last(hp[0:L, 2, :], 64), ALU.mult, [x_tok, hp], [y])
    k.tt("pool", t1[0:L, :, :], t1[0:L, :, :], y[0:L, :, :], ALU.add, [t1, y], [t1])
    for g in range(2):
        k.tt("dve", y[0:L, g * 8:(g + 1) * 8, :], yb[g][0:L, :].rearrange("p (h d) -> p h d", h=8),
             t1[0:L, g * 8:(g + 1) * 8, :], ALU.add, [yb[g], t1], [], accs=[y])
        P.ps_release(yb[g])
    if k.cut is not None and 7 > int(k.cut):
        return
    Bm = W["Bm"]
    for b in range(NS):
        if NS > 1:
            k.ts("pool", Bm[0:L, :, :], B_tok[0:L, :, :], seqind[0:L, b, 0:1], None, ALU.mult, None, [B_tok, seqind], [Bm])
        for g in range(2):
            bank = P.ps_alloc()
            lhs = B_tok[0:L, g, :] if NS == 1 else Bm[0:L, g, :]
            k.mm(bank[:, :], lhs, xdtw[0:L, g * 8:(g + 1) * 8, :], True, True, [B_tok, Bm, xdtw], [bank])
            hv = hst[b][:, g, :].rearrange("p (h d) -> p h d", h=8)
            k.tt("pool", hv, hv, bc_last(elast[:, b, g * 8:(g + 1) * 8], 64), ALU.mult, [hst[b], elast], [hst[b]])
            k.tt("dve", hst[b][:, g, :], hst[b][:, g, :], bank[:, :], ALU.add, [hst[b], bank], [hst[b]])
            k.cp("act", hsb[b][:, g, :], hst[b][:, g, :], [hst[b]], [], accs=[hsb[b]])
            P.ps_release(bank)
    if k.cut is not None and 8 > int(k.cut):
        return
    yf = y[0:L, :, :]
    k.tt("dve", yf, yf, zs[0:L, :].rearrange("p (h d) -> p h d", h=16), ALU.mult, [y, zs], [y])
    junk, yn, yT = W["junk"], W["yn"], W["yT"]
    for g in range(2):
        k.act(junk[0:L, :], y[0:L, g * 8:(g + 1) * 8, :], AF.Square, [y], [junk], accum_out=sm[0:L, 7, g:g + 1], accs=[sm])
    k.ts("dve", sm[0:L, 7, 2:4], sm[0:L, 7, 0:2], 1.0 / 512, RMS_EPS, ALU.mult, ALU.add, [sm], [sm])
    k.act(sm[0:L, 7, 2:4], sm[0:L, 7, 2:4], AF.Ln, [sm], [sm])
    k.act(sm[0:L, 7, 4:6], sm[0:L, 7, 2:4], AF.Exp, [sm], [sm], scale=-0.5)
    for g in range(2):
        k.ts("dve", yn[0:L, g * 512:(g + 1) * 512], y[0:L, g * 8:(g + 1) * 8, :], sm[0:L, 7, 4 + g:5 + g], None, ALU.mult, None,
             [y, sm], [], accs=[yn])
    bank = P.ps_alloc()
    bv = bf_view(bank[:, :])
    for kk in range(8):
        k.tr(bv[:, kk * 128:kk * 128 + L], yn[0:L, kk * 128:(kk + 1) * 128], c["ident_b"][0:L, 0:L], [yn, c["ident_b"]], [bank])
    for kk in range(8):
        k.ts("dve", yT[:, kk, 0:L], bv[:, kk * 128:kk * 128 + L], nw[:, kk:kk + 1], None, ALU.mult, None,
             [bank, nw], [], accs=[yT])
    P.ps_release(bank)
    out_proj_add(k, W["rt"], tl, yT, 8, wo)


def gdn_phase(k, l):
    P, i, o, c = k.P, k.i, k.o, k.c
    with contextlib.ExitStack() as ph:
        wq = k.tile(ph, [128, 8, 1536], BF16, "gwq")
        wz = k.tile(ph, [128, 8, 512], BF16, "gwz")
        wba = k.tile(ph, [128, 8, 8], BF16, "gwba")
        wo = k.tile(ph, [128, 4, 1024], BF16, "gwo")
        load_w_cast(k, wq, wq, i["w_in"][l][:, OFF["gdn_qkv"]:OFF["gdn_qkv"] + 1536], 1536)
        load_w_cast(k, wz, wz, i["w_in"][l][:, OFF["gdn_z"]:OFF["gdn_z"] + 512], 512)
        load_w_cast(k, wba, wba, i["w_in"][l][:, OFF["gdn_b"]:OFF["gdn_b"] + 8], 8)
        load_w_cast(k, wo, wo, i["w_out"][l][1536:2048, :], 1024, kchunks=4)
        dg, cw, _ = conv_prep(k, ph, i["gdn_conv_w"][l], None)
        hp = k.tile(ph, [128, 2, 4], F32, "ghp")
        k.load("sp", hp, hp[:, 0, :], i["gdn_dt_bias"][l].partition_broadcast(128))
        k.load("sp", hp, hp[:, 1, :], i["gdn_a_log"][l].partition_broadcast(128), accs=True)
        k.act(hp[:, 1, :], hp[:, 1, :], AF.Exp, [hp], [hp])
        k.ts("dve", hp[:, 1, :], hp[:, 1, :], -1.0, None, ALU.mult, None, [hp], [hp])
        gnw = k.tile(ph, [128, 128], F32, "gnw")
        k.load("sp", gnw, gnw[:, :], i["gdn_norm_w"][l].partition_broadcast(128))
        seqind = k.tile(ph, [128, 4, 128], F32, "seqind")
        k.load("sp", seqind, seqind[:], i["seqind_s"])
        cmask = k.tile(ph, [128, 4, 16], BF16, "cmask")
        k.load("sp", cmask, cmask[:], i["cmask_s"])
        lm = k.tile(ph, [128, 7, 128], F32, "lm")
        k.load("sp", lm, lm[:], i["lmask"])
        lmT = k.tile(ph, [128, 7, 128], F32, "lmT")
        k.load("sp", lmT, lmT[:], i["lmaskT"])
        W = {}
        for nm, sh, dt in [("qkv_fm", [128, 12, 128], F32), ("qkv_tok", [128, 12, 128], F32), ("sm", [128, 12, 8], F32),
                           ("qk_b", [128, 8, 128], BF16), ("qkT", [128, 8, 128], BF16), ("zs", [128, 512], BF16),
                           ("gbc", [128, 4, 128], F32), ("tmp", [128, 4, 128], F32), ("decst", [128, 4, 128], F32),
                           ("dects", [128, 4, 128], F32), ("A", [128, 4, 128], F32), ("AT", [128, 4, 128], F32),
                           ("Tm", [128, 4, 128], F32), ("TT", [128, 4, 128], F32), ("X", [128, 4, 128], F32),
                           ("attnT", [128, 4, 128], BF16), ("vb", [128, 4, 128], F32), ("kbg", [128, 4, 128], F32),
                           ("kd", [128, 4, 128], BF16), ("wTn", [128, 4, 4, 128], F32), ("vn_f", [128, 4, 128], F32),
                           ("vn_b", [128, 4, 128], BF16), ("os", [128, 4, 128], F32), ("of", [128, 4, 128], F32),
                           ("y", [128, 512], BF16), ("yT", [128, 4, 128], BF16), ("rt", [128, 1024], F32),
                           ("elast", [128, 4, 4], F32), ("junk", [128, 128], BF16), ("qTm", [128, 4, 4, 16], BF16), ("kdm", [128, 4, 128], BF16)]:
            W[nm] = k.tile(ph, sh, dt, "g" + nm)

        def run_stream(tiles, NS, Lb, st_conv_dram, st_dram, out_conv, out_st):
            with contextlib.ExitStack() as sp:
                pres = [k.tile(sp, [128, 12, NS, Lb + 3], BF16, "gpre") for _ in range(2)]
                csout = k.tile(sp, [128, 12, NS, 3], F32, "gcsout")
                S = [k.tile(sp, [128, 4, 128], F32, "gS") for _ in range(NS)]
                Sb = [k.tile(sp, [128, 4, 128], BF16, "gSb") for _ in range(NS)]
                st_in = None
                for b in range(NS):
                    if st_dram is None:
                        k.memset("pool", S[b], S[b][:], 0.0)
                    else:
                        k.load("sp", S[b], S[b][:], st_dram[b].rearrange("h d e -> d h e"))
                    k.cp("act", Sb[b][:], S[b][:], [S[b]], [Sb[b]])
                if st_dram is not None:
                    st_in = conv_state_io(k, sp, st_conv_dram, NS)
                for ti, tl in enumerate(tiles):
                    gdn_tile(k, l, tl, W, pres[ti % 2], pres[(ti + 1) % 2], ti == 0, st_in, csout, ti == len(tiles) - 1,
                             S, Sb, wq, wz, wba, wo, dg, hp, gnw, seqind, cmask, lm, lmT)
                conv_state_store(k, csout, out_conv, NS)
                for b in range(NS):
                    k.store("sp", S[b], out_st[b].rearrange("h d e -> d h e"), S[b][:])
                P.barrier()

        run_stream(k.tiles[:k.NT], 1, 128, None, None, o["p_gdn_conv"][l], o["p_gdn"][l])
        run_stream(k.tiles[k.NT:], 4, 4, i["st_gdn_conv"][l], i["st_gdn"][l], o["s_gdn_conv"][l], o["s_gdn"][l])
        P.barrier()


def gdn_tile(k, l, tl, W, pre, prev_pre, first, st_in, csout, last, S, Sb, wq, wz, wba, wo, dg, hp, gnw, seqind, cmask, lm, lmT):
    P, c = k.P, k.c
    r0, L, NS, Lb, pkn = tl
    pk = c[pkn]
    U, MNEG_ST, MNEG_TS, SAME = pk[0:L, 0, 0:L], pk[0:L, 1, 0:L], pk[0:L, 2, 0:L], pk[0:L, 3, 0:L]
    IDF = c["ident_f"]
    sm = W["sm"]
    nlev = int(np.log2(Lb))

    def b4(ap2):
        return ap2.unsqueeze(1).to_broadcast([L, 4, L])

    cbanks = conv_tile(k, tl, pre, prev_pre, wq, dg, first, st_in, csout, last)
    qkv_fm, qkv_tok = W["qkv_fm"], W["qkv_tok"]
    for cg in range(3):
        k.act(qkv_fm[:, cg * 4:(cg + 1) * 4, 0:L], cbanks[cg][:, :].rearrange("p (c t) -> p c t", c=4)[:, :, 0:L], SILU,
              [cbanks[cg]], [], accs=[qkv_fm])
        P.ps_release(cbanks[cg])
    for cg in range(3):
        bank = P.ps_alloc()
        for cc in range(4):
            k.tr(bank[0:L, cc * 128:(cc + 1) * 128], qkv_fm[:, cg * 4 + cc, 0:L], IDF[:, :], [qkv_fm, IDF], [bank])
        k.cp("dve" if cg % 2 else "act", qkv_tok[0:L, cg * 4:(cg + 1) * 4, :], bank[0:L, :].rearrange("p (c d) -> p c d", c=4),
             [bank], [], accs=[qkv_tok])
        P.ps_release(bank)
    zs = W["zs"]
    bank = P.ps_alloc()
    for kk in range(8):
        k.mm(bank[0:L, :], k.hT[:, kk, r0:r0 + L], wz[:, kk, :], kk == 0, kk == 7, [k.hT, wz], [bank])
    k.act(zs[0:L, :], bank[0:L, :], SILU, [bank], [zs])
    P.ps_release(bank)
    bank = P.ps_alloc()
    for kk in range(8):
        k.mm(bank[0:L, 0:8], k.hT[:, kk, r0:r0 + L], wba[:, kk, :], kk == 0, kk == 7, [k.hT, wba], [bank])
    k.act(sm[0:L, 6, 0:4], bank[0:L, 0:4], AF.Exp, [bank], [sm], scale=-1.0)
    k.ts("dve", sm[0:L, 6, 0:4], sm[0:L, 6, 0:4], 1.0, None, ALU.add, None, [sm], [sm])
    P.op("dve", lambda e: e.reciprocal(sm[0:L, 3, 0:4], sm[0:L, 6, 0:4]), reads=[sm.b], writes=[sm.b])
    k.tt("dve", sm[0:L, 6, 4:8], bank[0:L, 4:8], hp[0:L, 0, :], ALU.add, [bank, hp], [sm])
    k.act(sm[0:L, 6, 4:8], sm[0:L, 6, 4:8], AF.Exp, [sm], [sm])
    k.act(sm[0:L, 6, 4:8], sm[0:L, 6, 4:8], AF.Ln, [sm], [sm], bias=1.0)
    k.tt("dve", sm[0:L, 3, 4:8], sm[0:L, 6, 4:8], hp[0:L, 1, :], ALU.mult, [sm, hp], [sm])
    g = sm[0:L, 3, 4:8]
    k.mm(bank[0:L, 16:20], U, g, True, True, [pk, sm], [bank])
    k.mm(bank[0:L, 32:36], SAME, g, True, True, [pk, sm], [bank])
    elast = W["elast"]
    for b in range(NS):
        lhs = c["ones_f"][0:L, :] if NS == 1 else seqind[0:L, b, :]
        k.mm(bank[:, 64 + b * 4:64 + (b + 1) * 4], lhs, g, True, True, [c["ones_f"], seqind, sm], [bank])
    k.cp("dve", sm[0:L, 4, 0:4], bank[0:L, 16:20], [bank], [sm])
    k.ts("dve", sm[0:L, 4, 4:8], bank[0:L, 16:20], -1.0, None, ALU.mult, None, [bank], [sm])
    k.act(sm[0:L, 5, 0:4], bank[0:L, 16:20], AF.Exp, [bank], [sm])
    k.tt("dve", sm[0:L, 6, 0:4], bank[0:L, 32:36], sm[0:L, 4, 0:4], ALU.subtract, [bank, sm], [sm])
    k.act(sm[0:L, 5, 4:8], sm[0:L, 6, 0:4], AF.Exp, [sm], [sm])
    k.act(elast[:, 0:NS, :], bank[:, 64:64 + NS * 4].rearrange("p (b h) -> p b h", b=NS), AF.Exp, [bank], [elast])
    P.ps_release(bank)
    k.tt("dve", sm[0:L, 7, 0:4], sm[0:L, 3, 0:4], sm[0:L, 5, 0:4], ALU.mult, [sm], [sm])
    junk = W["junk"]
    for j in range(8):
        k.act(junk[0:L, :], qkv_tok[0:L, j, :], AF.Square, [qkv_tok], [junk], accum_out=sm[0:L, 0, j:j + 1], accs=[sm])
    k.ts("dve", sm[0:L, 1, :], sm[0:L, 0, :], L2_EPS, None, ALU.add, None, [sm], [sm])
    k.act(sm[0:L, 1, :], sm[0:L, 1, :], AF.Ln, [sm], [sm])
    k.act(sm[0:L, 1, :], sm[0:L, 1, :], AF.Exp, [sm], [sm], scale=-0.5)
    k.ts("dve", sm[0:L, 1, 0:4], sm[0:L, 1, 0:4], float(128 ** -0.5), None, ALU.mult, None, [sm], [sm])
    qk_b, qkT = W["qk_b"], W["qkT"]
    k.tt("dve", qkv_tok[0:L, 0:8, :], qkv_tok[0:L, 0:8, :], bc_last(sm[0:L, 1, :], 128), ALU.mult, [qkv_tok, sm], [qkv_tok])
    k.cp("pool", qk_b[0:L, :, :], qkv_tok[0:L, 0:8, :], [qkv_tok], [qk_b])
    bank = P.ps_alloc()
    bv = bf_view(bank[:, :])
    for j in range(8):
        k.tr(bv[:, j * 128:j * 128 + L], qk_b[0:L, j, :], c["ident_b"][0:L, 0:L], [qk_b, c["ident_b"]], [bank])
    k.cp("act", qkT[:, :, 0:L], bv[:, 0:1024].rearrange("p (c t) -> p c t", c=8)[:, :, 0:L], [bank], [qkT])
    P.ps_release(bank)
    vb, kbg, kd = W["vb"], W["kbg"], W["kd"]
    k.tt("dve", vb[0:L, :, :], qkv_tok[0:L, 8:12, :], bc_last(sm[0:L, 3, 0:4], 128), ALU.mult, [qkv_tok, sm], [vb])
    k.tt("pool", kbg[0:L, :, :], qkv_tok[0:L, 4:8, :], bc_last(sm[0:L, 7, 0:4], 128), ALU.mult, [qkv_tok, sm], [kbg])
    k.tt("pool", kd[0:L, :, :], qkv_tok[0:L, 4:8, :], bc_last(sm[0:L, 5, 4:8], 128), ALU.mult, [qkv_tok, sm], [kd])
    gbc, tmp, decst, dects = W["gbc"], W["tmp"], W["decst"], W["dects"]
    k.cp("pool", gbc[0:L, :, 0:L], bc_last(g, L), [sm], [gbc])
    gb = P.ps_alloc()
    for h in range(4):
        k.mm(gb[0:L, h * 128:h * 128 + L], gbc[0:L, h, 0:L], U, True, True, [gbc, pk], [gb])
    gbv = gb[0:L, :].rearrange("p (h t) -> p h t", h=4)[:, :, 0:L]
    k.tt("dve", tmp[0:L, :, 0:L], gbv, b4(MNEG_ST), ALU.add, [gb, pk], [tmp])
    for h in range(4):
        k.act(decst[0:L, h, 0:L], tmp[0:L, h, 0:L], AF.Exp, [tmp, sm], [], bias=sm[0:L, 4, 4 + h:5 + h], accs=[decst])
    k.stt(tmp[0:L, :, 0:L], gbv, -1.0, b4(MNEG_TS), ALU.mult, ALU.add, [gb, pk, decst], [tmp])
    P.ps_release(gb)
    for h in range(4):
        k.act(dects[0:L, h, 0:L], tmp[0:L, h, 0:L], AF.Exp, [tmp, sm], [], bias=sm[0:L, 4, h:h + 1], accs=[dects])
    A, AT, attnT = W["A"], W["AT"], W["attnT"]
    kkb = P.ps_alloc()
    qkb = P.ps_alloc()
    for h in range(4):
        k.mm(kkb[0:L, h * 128:h * 128 + L], qkT[:, 4 + h, 0:L], qkT[:, 4 + h, 0:L], True, True, [qkT], [kkb])
        k.mm(qkb[0:L, h * 128:h * 128 + L], qkT[:, 4 + h, 0:L], qkT[:, h, 0:L], True, True, [qkT], [qkb])
    for h in range(4):
        k.stt(A[0:L, h, 0:L], kkb[0:L, h * 128:h * 128 + L], sm[0:L, 3, h:h + 1], dects[0:L, h, 0:L], ALU.mult, ALU.mult,
              [kkb, sm, dects], [], accs=[A])
    P.ps_release(kkb)
    k.tt("dve", attnT[0:L, :, 0:L], qkb[0:L, :].rearrange("p (h t) -> p h t", h=4)[:, :, 0:L], decst[0:L, :, 0:L], ALU.mult,
         [qkb, decst], [attnT])
    P.ps_release(qkb)
    Tm, TT, X = W["Tm"], W["TT"], W["X"]

    def transpose4(dst, src):
        bank = P.ps_alloc()
        for h in range(4):
            k.tr(bank[0:L, h * 128:h * 128 + L], src[0:L, h, 0:L], IDF[0:L, 0:L], [src, IDF], [bank])
        k.cp("act", dst[0:L, :, 0:L], bank[0:L, :].rearrange("p (h t) -> p h t", h=4)[:, :, 0:L], [bank], [dst])
        P.ps_release(bank)

    transpose4(AT, A)
    k.tt("dve", X[0:L, :, 0:L], A[0:L, :, 0:L], b4(lm[0:L, 0, 0:L]), ALU.mult, [A, lm], [X])
    k.tt("dve", Tm[0:L, :, 0:L], b4(IDF[0:L, 0:L]), X[0:L, :, 0:L], ALU.subtract, [IDF, X], [Tm])
    k.tt("pool", tmp[0:L, :, 0:L], AT[0:L, :, 0:L], b4(lmT[0:L, 0, 0:L]), ALU.mult, [AT, lmT], [tmp])
    k.tt("pool", TT[0:L, :, 0:L], b4(IDF[0:L, 0:L]), tmp[0:L, :, 0:L], ALU.subtract, [IDF, tmp], [TT])
    for j in range(1, nlev):
        bank = P.ps_alloc()
        for h in range(4):
            k.mm(bank[0:L, h * 128:h * 128 + L], AT[0:L, h, 0:L], Tm[0:L, h, 0:L], True, True, [AT, Tm], [bank])
        k.cp("act", X[0:L, :, 0:L], bank[0:L, :].rearrange("p (h t) -> p h t", h=4)[:, :, 0:L], [bank], [X])
        P.ps_release(bank)
        bank = P.ps_alloc()
        for h in range(4):
            k.mm(bank[0:L, h * 128:h * 128 + L], TT[0:L, h, 0:L], X[0:L, h, 0:L], True, True, [TT, X], [bank])
        k.tt("dve", tmp[0:L, :, 0:L], bank[0:L, :].rearrange("p (h t) -> p h t", h=4)[:, :, 0:L], b4(lm[0:L, j, 0:L]), ALU.mult,
             [bank, lm], [tmp])
        P.ps_release(bank)
        k.tt("dve", Tm[0:L, :, 0:L], Tm[0:L, :, 0:L], tmp[0:L, :, 0:L], ALU.subtract, [Tm, tmp], [Tm])
        transpose4(TT, Tm)
    wTn, vn_f, vn_b = W["wTn"], W["vn_f"], W["vn_b"]
    bank = P.ps_alloc()
    for h in range(4):
        k.mm(bank[:, h * 128:h * 128 + L], kbg[0:L, h, :], TT[0:L, h, 0:L], True, True, [kbg, TT], [bank])
    wsrc = bank[:, :].rearrange("p (h t) -> p h t", h=4)[:, :, 0:L]
    if NS == 1:
        k.ts("dve", wTn[:, 0, :, 0:L], wsrc, -1.0, None, ALU.mult, None, [bank], [wTn])
    else:
        for b in range(NS):
            k.stt(wTn[:, b, :, 0:L], wsrc, -1.0, cmask[:, b, 0:L].unsqueeze(1).to_broadcast([128, 4, L]), ALU.mult, ALU.mult,
                  [bank, cmask], [] if b else [wTn], accs=[wTn] if b else [])
    P.ps_release(bank)
    bank = P.ps_alloc()
    for h in range(4):
        k.mm(bank[0:L, h * 128:(h + 1) * 128], TT[0:L, h, 0:L], vb[0:L, h, :], True, False, [TT, vb], [bank])
        for b in range(NS):
            k.mm(bank[0:L, h * 128:(h + 1) * 128], wTn[:, b, h, 0:L], S[b][:, h, :], False, b == NS - 1, [wTn, S[b]], [bank])
    k.cp("dve", vn_f[0:L, :, :], bank[0:L, :].rearrange("p (h e) -> p h e", h=4), [bank], [vn_f])
    P.ps_release(bank)
    k.cp("act", vn_b[0:L, :, :], vn_f[0:L, :, :], [vn_f], [vn_b])
    osb, of, qTm = W["os"], W["of"], W["qTm"]
    if NS > 1:
        for b in range(NS):
            k.tt("pool", qTm[:, b, :, 0:L], qkT[:, 0:4, 0:L], cmask[:, b, 0:L].unsqueeze(1).to_broadcast([128, 4, L]), ALU.mult,
                 [qkT, cmask], [] if b else [qTm], accs=[qTm] if b else [])
    bank = P.ps_alloc()
    for h in range(4):
        for b in range(NS):
            lhs = qkT[:, h, 0:L] if NS == 1 else qTm[:, b, h, 0:L]
            k.mm(bank[0:L, h * 128:(h + 1) * 128], lhs, Sb[b][:, h, :], b == 0, b == NS - 1, [qkT, qTm, Sb[b]], [bank])
    k.tt("dve", osb[0:L, :, :], bank[0:L, :].rearrange("p (h e) -> p h e", h=4), bc_last(sm[0:L, 5, 0:4], 128), ALU.mult,
         [bank, sm], [osb])
    P.ps_release(bank)
    bank = P.ps_alloc()
    for h in range(4):
        k.mm(bank[0:L, h * 128:(h + 1) * 128], attnT[0:L, h, 0:L], vn_b[0:L, h, :], True, True, [attnT, vn_b], [bank])
    k.tt("dve", of[0:L, :, :], bank[0:L, :].rearrange("p (h e) -> p h e", h=4), osb[0:L, :, :], ALU.add, [bank, osb], [of])
    P.ps_release(bank)
    for b in range(NS):
        if NS > 1:
            kdm = W["kdm"]
            k.ts("pool", kdm[0:L, :, :], kd[0:L, :, :], seqind[0:L, b, 0:1], None, ALU.mult, None, [kd, seqind], [kdm])
        else:
            kdm = kd
        bank = P.ps_alloc()
        for h in range(4):
            k.mm(bank[:, h * 128:(h + 1) * 128], kdm[0:L, h, :], vn_b[0:L, h, :], True, True, [kdm, vn_b], [bank])
        k.tt("pool", S[b][:, :, :], S[b][:, :, :], bc_last(elast[:, b, :], 128), ALU.mult, [S[b], elast], [S[b]])
        k.tt("dve", S[b][:, :, :], S[b][:, :, :], bank[:, :].rearrange("p (h e) -> p h e", h=4), ALU.add, [S[b], bank], [S[b]])
        k.cp("act", Sb[b][:, :, :], S[b][:, :, :], [S[b]], [Sb[b]])
        P.ps_release(bank)
    for h in range(4):
        k.act(junk[0:L, :], of[0:L, h, :], AF.Square, [of], [junk], accum_out=sm[0:L, 8, h:h + 1], accs=[sm])
    k.ts("dve", sm[0:L, 8, 4:8], sm[0:L, 8, 0:4], 1.0 / 128, RMS_EPS, ALU.mult, ALU.add, [sm], [sm])
    k.act(sm[0:L, 8, 4:8], sm[0:L, 8, 4:8], AF.Ln, [sm], [sm])
    k.act(sm[0:L, 8, 4:8], sm[0:L, 8, 4:8], AF.Exp, [sm], [sm], scale=-0.5)
    k.tt("dve", of[0:L, :, :], of[0:L, :, :], bc_last(sm[0:L, 8, 4:8], 128), ALU.mult, [of, sm], [of])
    k.tt("pool", of[0:L, :, :], of[0:L, :, :], gnw[0:L, :].unsqueeze(1).to_broadcast([L, 4, 128]), ALU.mult, [of, gnw], [of])
    y, yT = W["y"], W["yT"]
    k.tt("dve", y[0:L, :].rearrange("p (h e) -> p h e", h=4), of[0:L, :, :], zs[0:L, :].rearrange("p (h e) -> p h e", h=4), ALU.mult,
         [of, zs], [y])
    bank = P.ps_alloc()
    bv = bf_view(bank[:, :])
    for kk in range(4):
        k.tr(bv[:, kk * 128:kk * 128 + L], y[0:L, kk * 128:(kk + 1) * 128], c["ident_b"][0:L, 0:L], [y, c["ident_b"]], [bank])
    k.cp("act", yT[:, :, 0:L], bv[:, 0:512].rearrange("p (c t) -> p c t", c=4)[:, :, 0:L], [bank], [yT])
    P.ps_release(bank)
    out_proj_add(k, W["rt"], tl, yT, 4, wo)


def mla_phase(k, l):
    P, i, o, c = k.P, k.i, k.o, k.c
    TP, NT, NTOK = k.TP, k.NT, k.NTOK
    with contextlib.ExitStack() as ph:
        cqT = k.tile(ph, [128, 3, NTOK], BF16, "cqT")
        ckvT = k.tile(ph, [128, 2, NTOK], BF16, "ckvT")
        krT = k.tile(ph, [96, NTOK], BF16, "krT")
        ckvs = k.tile(ph, [16, 257], BF16, "ckvs")
        with contextlib.ExitStack() as m1:
            wm = k.tile(m1, [128, 8, 672], BF16, "wm")
            load_w_cast(k, wm, wm, i["w_in"][l][:, OFF["cq"]:OFF["cq"] + 672], 672)
            qnw = k.tile(m1, [128, 3], F32, "qnw")
            fm_vec(k, qnw, qnw[:, :], i["mla_q_norm_w"][l], 3)
            kvw = k.tile(m1, [128, 256], F32, "kvw")
            k.load("sp", kvw, kvw[:, :], i["mla_kv_norm_w"][l].partition_broadcast(128))
            k.memset("pool", ckvs, ckvs[:, 256:257], 1.0)
            tw = [dict(cq=k.tile(m1, [128, 384], BF16, "cqn"), ckv=k.tile(m1, [128, 256], F32, "ckv"),
                       ckb=k.tile(m1, [128, 256], BF16, "ckb"), kr=k.tile(m1, [128, 32], F32, "kr"),
                       krr=k.tile(m1, [128, 32], F32, "krr"), krb=k.tile(m1, [128, 96], BF16, "krb"),
                       cs=k.tile(m1, [128, 2, 16], F32, "cs"), sm=k.tile(m1, [128, 8], F32, "sm"),
                       t=k.tile(m1, [128, 4, 16], F32, "t"), junk=k.tile(m1, [128, 384], BF16, "junk")) for _ in range(2)]
            for w in tw:
                k.memset("pool", w["krb"], w["krb"][:, 0:64], 0.0)
            for ti, (r0, L, NS, Lb, pkn) in enumerate(k.tiles):
                w = tw[ti % 2]
                sm = w["sm"]
                k.load("sp", w["cs"], w["cs"][0:L, 0, :], i["cos_tm"][r0:r0 + L, :])
                k.load("sp", w["cs"], w["cs"][0:L, 1, :], i["sin_tm"][r0:r0 + L, :], accs=True)
                b0, b1 = P.ps_alloc(), P.ps_alloc()
                for kk in range(8):
                    k.mm(b0[0:L, :], k.hT[:, kk, r0:r0 + L], wm[:, kk, 0:512], kk == 0, kk == 7, [k.hT, wm], [b0])
                for kk in range(8):
                    k.mm(b1[0:L, 0:160], k.hT[:, kk, r0:r0 + L], wm[:, kk, 512:672], kk == 0, kk == 7, [k.hT, wm], [b1])
                k.act(w["junk"][0:L, :], b0[0:L, 0:384], AF.Square, [b0], [w["junk"]], accum_out=sm[0:L, 0:1], accs=[sm])
                k.cp("dve", w["ckv"][0:L, 0:128], b0[0:L, 384:512], [b0], [w["ckv"]])
                k.cp("dve", w["ckv"][0:L, 128:256], b1[0:L, 0:128], [b1], [], accs=[w["ckv"]])
                k.cp("dve", w["kr"][0:L, :], b1[0:L, 128:160], [b1], [w["kr"]])
                k.act(w["junk"][0:L, 0:256], w["ckv"][0:L, :], AF.Square, [w["ckv"]], [w["junk"]], accum_out=sm[0:L, 1:2], accs=[sm])
                k.ts("dve", sm[0:L, 2:3], sm[0:L, 0:1], 1.0 / 384, RMS_EPS, ALU.mult, ALU.add, [sm], [sm])
                k.ts("dve", sm[0:L, 3:4], sm[0:L, 1:2], 1.0 / 256, RMS_EPS, ALU.mult, ALU.add, [sm], [sm])
                k.act(sm[0:L, 2:4], sm[0:L, 2:4], AF.Ln, [sm], [sm])
                k.act(sm[0:L, 2:4], sm[0:L, 2:4], AF.Exp, [sm], [sm], scale=-0.5)
                k.ts("dve", w["cq"][0:L, :], b0[0:L, 0:384], sm[0:L, 2:3], None, ALU.mult, None, [b0, sm], [w["cq"]])
                P.ps_release(b0)
                P.ps_release(b1)
                k.stt(w["ckv"][0:L, :], w["ckv"][0:L, :], sm[0:L, 3:4], kvw[0:L, :], ALU.mult, ALU.mult, [w["ckv"], sm, kvw], [w["ckv"]])
                lat_out = o["p_lat"][l][r0:r0 + L, :] if NS == 1 else o["s_lat"][l]
                k.store("sp", w["ckv"], lat_out, w["ckv"][0:L, :])
                k.cp("pool", w["ckb"][0:L, :], w["ckv"][0:L, :], [w["ckv"]], [w["ckb"]])
                if NS > 1:
                    k.cp("pool", ckvs[0:L, 0:256], w["ckv"][0:L, :], [w["ckv"]], [], accs=[ckvs])
                kr, krr, t, cs = w["kr"], w["krr"], w["t"], w["cs"]
                k.tt("dve", t[0:L, 0, :], kr[0:L, 0:16], cs[0:L, 0, :], ALU.mult, [kr, cs], [t])
                k.tt("dve", t[0:L, 1, :], kr[0:L, 16:32], cs[0:L, 1, :], ALU.mult, [kr, cs], [t])
                k.tt("dve", t[0:L, 2, :], kr[0:L, 16:32], cs[0:L, 0, :], ALU.mult, [kr, cs], [t])
                k.tt("dve", t[0:L, 3, :], kr[0:L, 0:16], cs[0:L, 1, :], ALU.mult, [kr, cs], [t])
                k.tt("dve", krr[0:L, 0:16], t[0:L, 0, :], t[0:L, 1, :], ALU.subtract, [t], [krr])
                k.tt("dve", krr[0:L, 16:32], t[0:L, 2, :], t[0:L, 3, :], ALU.add, [t], [krr])
                rope_out = o["p_rope"][l][r0:r0 + L, :] if NS == 1 else o["s_rope"][l]
                k.store("sp", krr, rope_out, krr[0:L, :])
                k.cp("pool", w["krb"][0:L, 64:96], krr[0:L, :], [krr], [], accs=[w["krb"]])
                bank = P.ps_alloc()
                bv = bf_view(bank[:, :])
                for j in range(3):
                    k.tr(bv[:, j * 128:j * 128 + L], w["cq"][0:L, j * 128:(j + 1) * 128], c["ident_b"][0:L, 0:L], [w["cq"], c["ident_b"]], [bank])
                for j in range(2):
                    k.tr(bv[:, (3 + j) * 128:(3 + j) * 128 + L], w["ckb"][0:L, j * 128:(j + 1) * 128], c["ident_b"][0:L, 0:L],
                         [w["ckb"], c["ident_b"]], [bank])
                k.tr(bv[0:96, 5 * 128:5 * 128 + L], w["krb"][0:L, :], c["ident_b"][0:L, 0:L], [w["krb"], c["ident_b"]], [bank])
                for j in range(3):
                    k.ts("dve", cqT[:, j, r0:r0 + L], bv[:, j * 128:j * 128 + L], qnw[:, j:j + 1], None, ALU.mult, None, [bank, qnw], [], accs=[cqT])
                k.cp("act", ckvT[:, :, r0:r0 + L], bv[:, 384:640].rearrange("p (c t) -> p c t", c=2)[:, :, 0:L], [bank], [], accs=[ckvT])
                k.cp("act", krT[64:96, r0:r0 + L], bv[64:96, 640:640 + L], [bank], [], accs=[krT])
                P.ps_release(bank)
            P.barrier()
        with contextlib.ExitStack() as m2:
            wuq = k.tile(m2, [128, 3, 768], BF16, "wuq")
            wuk = k.tile(m2, [128, 2, 512], BF16, "wuk")
            wuv = k.tile(m2, [128, 2, 512], BF16, "wuv")
            load_w_cast(k, wuq, wuq, i["mla_w_uq"][l], 768, kchunks=3)
            load_w_cast(k, wuk, wuk, i["mla_w_uk"][l], 512, kchunks=2)
            load_w_cast(k, wuv, wuv, i["mla_w_uv"][l], 512, kchunks=2)
            wqr = k.tile(m2, [128, 3, 8, 96], BF16, "wqr")
            wq4 = wuq[:, :, :].rearrange("p k (h d) -> p k h d", h=8)
            k.memset("pool", wqr, wqr[:, :, :, 0:64], 0.0)
            k.ts("dve", wqr[:, :, :, 64:80], wq4[:, :, :, 80:96], -1.0, None, ALU.mult, None, [wuq], [], accs=[wqr])
            k.cp("dve", wqr[:, :, :, 80:96], wq4[:, :, :, 64:80], [wuq], [], accs=[wqr])
            o_all = k.tile(m2, [128, NT, 512], BF16, "o_all")
            qn_s = k.tile(m2, [64, 8, 16], BF16, "qn_s")
            qr_s = k.tile(m2, [96, 8, 16], BF16, "qr_s")
            m2a = contextlib.ExitStack()
            csfs = [k.tile(m2a, [96, 2, 512], F32, "csf") for _ in range(2)]
            V = k.tile(m2a, [128, NT, 8, 65], BF16, "V")
            k.memset("pool", V, V[:, :, :, 64:65], 1.0)
            for j in range(NT):
                bank = P.ps_alloc()
                for rc in range(2):
                    k.mm(bank[:, :], ckvT[:, rc, j * 128:(j + 1) * 128], wuv[:, rc, :], rc == 0, rc == 1, [ckvT, wuv], [bank])
                k.cp("act" if j % 2 else "dve", V[:, j, :, 0:64], bank[:, :].rearrange("p (h d) -> p h d", h=8), [bank], [], accs=[V])
                P.ps_release(bank)
            QT = [k.tile(m2a, [96, TP], BF16, "QT") for _ in range(2)]
            KT = [k.tile(m2a, [96, TP], BF16, "KT") for _ in range(2)]
            rt = [k.tile(m2a, [96, 512], F32, "ropet") for _ in range(2)]
            pT = [k.tile(m2a, [128, 4, 128], BF16, "pT") for _ in range(2)]
            rcp = k.tile(m2a, [128, 2], F32, "rcp")
            nblk = 0
            csf = csfs[0]
            k.load("sp", csf, csf[64:96, 0, 0:16], i["cos_fm"][:, TP:TP + 16])
            k.load("sp", csf, csf[64:96, 1, 0:16], i["sin_fm"][:, TP:TP + 16], accs=True)
            ba, bb = P.ps_alloc(), P.ps_alloc()
            for h in range(8):
                for kc in range(3):
                    k.mm(ba[0:96, h * 16:(h + 1) * 16], wuq[:, kc, h * 96:(h + 1) * 96], cqT[:, kc, TP:TP + 16], kc == 0, kc == 2, [wuq, cqT], [ba])
                for kc in range(3):
                    k.mm(bb[0:96, h * 16:(h + 1) * 16], wqr[:, kc, h, :], cqT[:, kc, TP:TP + 16], kc == 0, kc == 2, [wqr, cqT], [bb])
            k.cp("act", qn_s[:, :, :], ba[0:64, 0:128].rearrange("p (h t) -> p h t", h=8), [ba], [qn_s])
            r_ = rt[0]
            cos_b = csf[64:96, 0, 0:16].unsqueeze(1).to_broadcast([32, 8, 16])
            sin_b = csf[64:96, 1, 0:16].unsqueeze(1).to_broadcast([32, 8, 16])
            rv = r_[64:96, 0:128].rearrange("p (h t) -> p h t", h=8)
            k.tt("dve", rv, ba[64:96, 0:128].rearrange("p (h t) -> p h t", h=8), cos_b, ALU.mult, [ba, csf], [r_])
            k.tt("dve", qr_s[64:96, :, :], bb[64:96, 0:128].rearrange("p (h t) -> p h t", h=8), sin_b, ALU.mult, [bb, csf], [qr_s])
            k.tt("pool", qr_s[64:96, :, :], qr_s[64:96, :, :], rv, ALU.add, [qr_s, r_], [qr_s])
            P.ps_release(ba)
            P.ps_release(bb)
            nblk = 1
            m2b = contextlib.ExitStack()
            sgen = mla_sample(k, l, m2b, cqT, ckvT, krT, ckvs, wuk, wuv, qn_s, qr_s)
            BLK = 512
            for h in range(8):
                qt, kt = QT[h % 2], KT[h % 2]
                for c0 in range(0, TP, BLK):
                    n = min(BLK, TP - c0)
                    csf = csfs[nblk % 2]
                    nblk += 1
                    k.load("sp", csf, csf[64:96, 0, 0:n], i["cos_fm"][:, c0:c0 + n])
                    k.load("sp", csf, csf[64:96, 1, 0:n], i["sin_fm"][:, c0:c0 + n], accs=True)
                    ba, bb = P.ps_alloc(), P.ps_alloc()
                    for kc in range(3):
                        k.mm(ba[0:96, 0:n], wuq[:, kc, h * 96:(h + 1) * 96], cqT[:, kc, c0:c0 + n], kc == 0, kc == 2, [wuq, cqT], [ba])
                    for kc in range(3):
                        k.mm(bb[0:96, 0:n], wqr[:, kc, h, :], cqT[:, kc, c0:c0 + n], kc == 0, kc == 2, [wqr, cqT], [bb])
                    k.cp("act", qt[0:64, c0:c0 + n], ba[0:64, 0:n], [ba], [], accs=[qt])
                    r_ = rt[(c0 // BLK) % 2]
                    k.tt("dve", r_[64:96, 0:n], ba[64:96, 0:n], csf[64:96, 0, 0:n], ALU.mult, [ba, csf], [r_])
                    k.tt("dve", qt[64:96, c0:c0 + n], bb[64:96, 0:n], csf[64:96, 1, 0:n], ALU.mult, [bb, csf], [], accs=[qt])
                    k.tt("pool", qt[64:96, c0:c0 + n], qt[64:96, c0:c0 + n], r_[64:96, 0:n], ALU.add, [qt, r_], [qt])
                    P.ps_release(ba)
                    P.ps_release(bb)
                k.cp("pool", kt[64:96, :], krT[64:96, 0:TP], [krT], [kt])
                for c0 in range(0, TP, BLK):
                    n = min(BLK, TP - c0)
                    ba = P.ps_alloc()
                    for rc in range(2):
                        k.mm(ba[0:64, 0:n], wuk[:, rc, h * 64:(h + 1) * 64], ckvT[:, rc, c0:c0 + n], rc == 0, rc == 1, [wuk, ckvT], [ba])
                    k.cp("act", kt[0:64, c0:c0 + n], ba[0:64, 0:n], [ba], [], accs=[kt])
                    P.ps_release(ba)
                for qi in range(NT):
                    acc = P.ps_alloc()
                    for j0 in range(0, qi + 1, 4):
                        nj = min(4, qi + 1 - j0)
                        sc = P.ps_alloc()
                        for jj in range(nj):
                            j = j0 + jj
                            k.mm(sc[:, jj * 128:(jj + 1) * 128], kt[:, j * 128:(j + 1) * 128], qt[:, qi * 128:(qi + 1) * 128], True, True,
                                 [kt, qt], [sc])
                        p_ = pT[(j0 // 4 + qi) % 2]
                        k.act(p_[:, 0:nj, :], sc[:, 0:nj * 128].rearrange("p (j t) -> p j t", j=nj), AF.Exp, [sc], [p_], scale=MLA_SCALE)
                        P.ps_release(sc)
                        if j0 + nj - 1 == qi:
                            k.tt("pool", p_[:, nj - 1, :], p_[:, nj - 1, :], c["pk_p"][:, 0, :], ALU.mult, [p_, c["pk_p"]], [p_])
                        for jj in range(nj):
                            j = j0 + jj
                            k.mm(acc[:, 0:65], p_[:, jj, :], V[:, j, h, :], j == 0, j == qi, [p_, V], [acc])
                    P.op("dve", lambda e: e.reciprocal(rcp[:, 0:1], acc[:, 64:65]), reads=[acc.b], writes=[rcp.b])
                    k.ts("dve", o_all[:, qi, h * 64:(h + 1) * 64], acc[:, 0:64], rcp[:, 0:1], None, ALU.mult, None, [acc, rcp], [], accs=[o_all])
                    P.ps_release(acc)
                    next(sgen, None)
            for _ in sgen:
                pass
            P.barrier()
            m2b.close()
            m2a.close()
            wg = k.tile(m2, [128, 8, 512], BF16, "wg")
            wo = k.tile(m2, [128, 4, 1024], BF16, "mwo")
            load_w_cast(k, wg, wg, i["w_in"][l][:, OFF["gate"]:OFF["gate"] + 512], 512)
            load_w_cast(k, wo, wo, i["w_out"][l][1024:1536, :], 1024, kchunks=4)
            gs = [k.tile(m2, [128, 512], BF16, "gs") for _ in range(2)]
            yT = [k.tile(m2, [128, 4, 128], BF16, "myT") for _ in range(2)]
            rtile = k.tile(m2, [128, 1024], F32, "mrt")
            for ti in range(NT):
                tl = k.tiles[ti]
                r0, L = tl[0], tl[1]
                g_, y_ = gs[ti % 2], yT[ti % 2]
                bank = P.ps_alloc()
                for kk in range(8):
                    k.mm(bank[0:L, :], k.hT[:, kk, r0:r0 + L], wg[:, kk, :], kk == 0, kk == 7, [k.hT, wg], [bank])
                k.act(g_[0:L, :], bank[0:L, :], SILU, [bank], [g_])
                P.ps_release(bank)
                k.tt("dve", g_[0:L, :], g_[0:L, :], o_all[0:L, ti, :], ALU.mult, [g_, o_all], [g_])
                bank = P.ps_alloc()
                bv = bf_view(bank[:, :])
                for kk in range(4):
                    k.tr(bv[:, kk * 128:kk * 128 + L], g_[0:L, kk * 128:(kk + 1) * 128], c["ident_b"][0:L, 0:L], [g_, c["ident_b"]], [bank])
                k.cp("act", y_[:, :, 0:L], bv[:, 0:512].rearrange("p (c t) -> p c t", c=4)[:, :, 0:L], [bank], [y_])
                P.ps_release(bank)
                out_proj_add(k, rtile, tl, y_, 4, wo)
            mla_sample_out(k, l, m2, wg, wo, rtile)
            P.barrier()


def dyn_page_load(k, tl_, out_l, l, idxl, idx, first):
    P = k.P
    cat2 = k.i["cache_cat"][l].rearrange("n s r -> (n s) r")
    off = bass.IndirectOffsetOnAxis(ap=idxl[:, idx:idx + 1], axis=0)
    P.dma("pool", lambda e: e.indirect_dma_start(out=out_l, out_offset=None, in_=cat2, in_offset=off), tl_.b,
          reads=[idxl.b], writes=[tl_.b] if first else [], accs=[] if first else [tl_.b])


def mla_sample(k, l, m2, cqT, ckvT, krT, ckvs, wuk, wuv, qn_s, qr_s):
    P, i, c = k.P, k.i, k.c
    TP, NPG = k.TP, k.NPG
    IDB = c["ident_b"]
    wukT = k.tile(m2, [64, 8, 256], BF16, "wukT")
    for h in range(8):
        bank = P.ps_alloc()
        bv = bf_view(bank[:, :])
        for rc in range(2):
            k.tr(bv[0:64, rc * 128:(rc + 1) * 128], wuk[:, rc, h * 64:(h + 1) * 64], IDB[:, :], [wuk, IDB], [bank])
        k.cp("act" if h % 2 else "dve", wukT[:, h, :], bv[0:64, 0:256], [bank], [], accs=[wukT])
        P.ps_release(bank)
    qlat = k.tile(m2, [128, 2, 8, 16], BF16, "qlat")
    bank = P.ps_alloc()
    for rc in range(2):
        for h in range(8):
            k.mm(bank[:, (rc * 8 + h) * 16:(rc * 8 + h + 1) * 16], wukT[:, h, rc * 128:(rc + 1) * 128], qn_s[:, h, :], True, True,
                 [wukT, qn_s], [bank])
    k.cp("act", qlat[:, :, :, :], bank[:, 0:256].rearrange("p (r h t) -> p r h t", r=2, h=8), [bank], [qlat])
    P.ps_release(bank)
    smask = k.tile(m2, [16, 128], F32, "smask")
    k.load("sp", smask, smask[:, :], i["smask"])
    ptb = k.tile(m2, [128, 4 * NPG], I32, "ptb")
    k.load("sp", ptb, ptb[:, :], i["page_table"].rearrange("b n -> (b n)").partition_broadcast(128))
    iot = k.tile(m2, [128, 1], F32, "iot")
    k.load("sp", iot, iot[:, :], i["iota_p"])
    pt = k.tile(m2, [128, 4 * NPG], I32, "idxl")
    k.ts("dve", pt[:, :], ptb[:, :], 128.0, iot[:, 0:1], ALU.mult, ALU.add, [ptb, iot], [pt])
    wuvm = k.tile(m2, [128, 2, 8, 128], BF16, "wuvm")
    k.memset("pool", wuvm, wuvm[:], 0.0)
    wv4 = wuv[:, :, :].rearrange("p k (h d) -> p k h d", h=8)
    for h in range(8):
        k.cp("pool", wuvm[:, :, h, (h % 2) * 64:(h % 2) * 64 + 64], wv4[:, :, h, :], [wuv], [], accs=[wuvm])
    lat4 = [k.tile(m2, [128, 4, 288], F32, "lat4") for _ in range(3)]
    lat4b = [k.tile(m2, [128, 4, 257], BF16, "lat4b") for _ in range(2)]
    rope4b = [k.tile(m2, [128, 4, 96], BF16, "rope4b") for _ in range(2)]
    latT = [k.tile(m2, [128, 4, 2, 128], BF16, "latT") for _ in range(2)]
    ropeT = [k.tile(m2, [96, 4, 128], BF16, "ropeT") for _ in range(2)]
    pTs = [k.tile(m2, [128, 4, 32], BF16, "pTs") for _ in range(2)]
    for j in range(2):
        k.memset("pool", lat4b[j], lat4b[j][:, :, 256:257], 1.0)
        k.memset("pool", rope4b[j], rope4b[j][:, :, 0:64], 0.0)
    pn = k.tile(m2, [16, 32], F32, "pn")
    pnb = k.tile(m2, [16, 32], BF16, "pnb")
    rc_ = k.tile(m2, [32, 1], F32, "rcs")
    olat = k.tile(m2, [32, 256], BF16, "olat")
    olatT = k.tile(m2, [128, 2, 32], BF16, "olatT")
    yraw = P.ps_alloc()
    k.m_yraw = yraw
    gi = 0
    for b in range(4):
        acc = P.ps_alloc()
        qs = slice(b * 4, (b + 1) * 4)
        for g0 in range(0, NPG, 4):
            npg = min(4, NPG - g0)
            L4, L4b, R4b, LT, RT, PT_ = lat4[gi % 3], lat4b[gi % 2], rope4b[gi % 2], latT[gi % 2], ropeT[gi % 2], pTs[gi % 2]
            gi += 1
            for pg in range(npg):
                idx = b * NPG + g0 + pg
                dyn_page_load(k, L4, L4[:, pg, :], l, pt, idx, pg == 0)
            h1 = (npg + 1) // 2
            k.cp("dve", L4b[:, 0:h1, 0:256], L4[:, 0:h1, 0:256], [L4], [], accs=[L4b])
            if npg > h1:
                k.cp("act", L4b[:, h1:npg, 0:256], L4[:, h1:npg, 0:256], [L4], [], accs=[L4b])
            k.cp("dve", R4b[:, 0:npg, 64:96], L4[:, 0:npg, 256:288], [L4], [], accs=[R4b])
            ba = P.ps_alloc()
            bva = bf_view(ba[:, :])
            for pg in range(npg):
                for rc in range(2):
                    k.tr(bva[:, (pg * 2 + rc) * 128:(pg * 2 + rc + 1) * 128], L4b[:, pg, rc * 128:(rc + 1) * 128], IDB[:, :], [L4b, IDB], [ba])
            k.cp("act", LT[:, 0:npg, :, :], bva[:, 0:npg * 256].rearrange("p (g r s) -> p g r s", g=npg, r=2), [ba], [LT])
            P.ps_release(ba)
            bb = P.ps_alloc()
            bvb = bf_view(bb[:, :])
            for pg in range(npg):
                k.tr(bvb[0:96, pg * 128:(pg + 1) * 128], R4b[:, pg, :], IDB[:, :], [R4b, IDB], [bb])
            k.cp("dve", RT[64:96, 0:npg, :], bvb[64:96, 0:npg * 128].rearrange("p (g s) -> p g s", g=npg), [bb], [RT])
            P.ps_release(bb)
            sc = P.ps_alloc()
            for pg in range(npg):
                ov = sc[:, pg * 32:(pg + 1) * 32].rearrange("p (h t) -> p h t", h=8)
                k.mm(ov, LT[:, pg, 0, :], qlat[:, 0, :, qs], True, False, [LT, qlat], [sc])
                k.mm(ov, LT[:, pg, 1, :], qlat[:, 1, :, qs], False, False, [LT, qlat], [sc])
                k.mm(ov, RT[64:96, pg, :], qr_s[64:96, :, qs], False, True, [RT, qr_s], [sc])
            k.act(PT_[:, 0:npg, :], sc[:, 0:npg * 32].rearrange("p (g q) -> p g q", g=npg), AF.Exp, [sc], [PT_], scale=MLA_SCALE)
            P.ps_release(sc)
            for pg in range(npg):
                k.mm(acc[0:32, 0:257], PT_[:, pg, :], L4b[:, pg, :], g0 == 0 and pg == 0, False, [PT_, L4b], [acc])
            yield
        sc = P.ps_alloc()
        ov = sc[0:16, 0:32].rearrange("p (h t) -> p h t", h=8)
        k.mm(ov, ckvT[:, 0, TP:TP + 16], qlat[:, 0, :, qs], True, False, [ckvT, qlat], [sc])
        k.mm(ov, ckvT[:, 1, TP:TP + 16], qlat[:, 1, :, qs], False, False, [ckvT, qlat], [sc])
        k.mm(ov, krT[64:96, TP:TP + 16], qr_s[64:96, :, qs], False, True, [krT, qr_s], [sc])
        k.act(pn[:, :], sc[0:16, 0:32], AF.Exp, [sc], [pn], scale=MLA_SCALE)
        P.ps_release(sc)
        k.tt("dve", pnb[:, :], pn[:, :], smask[:, b * 32:(b + 1) * 32], ALU.mult, [pn, smask], [pnb])
        k.mm(acc[0:32, 0:257], pnb[:, :], ckvs[0:16, :], False, True, [pnb, ckvs], [acc])
        P.op("dve", lambda e: e.reciprocal(rc_[:, :], acc[0:32, 256:257]), reads=[acc.b], writes=[rc_.b])
        k.ts("dve", olat[:, :], acc[0:32, 0:256], rc_[:, 0:1], None, ALU.mult, None, [acc, rc_], [olat])
        P.ps_release(acc)
        bank = P.ps_alloc()
        bv = bf_view(bank[:, :])
        for rc in range(2):
            k.tr(bv[:, rc * 32:(rc + 1) * 32], olat[:, rc * 128:(rc + 1) * 128], IDB[0:32, 0:32], [olat, IDB], [bank])
        k.cp("act", olatT[:, :, :], bv[:, 0:64].rearrange("p (r q) -> p r q", r=2), [bank], [olatT])
        P.ps_release(bank)
        for cidx in range(4):
            n = 0
            for hh in range(2):
                h = cidx * 2 + hh
                for rc in range(2):
                    k.mm(yraw[:, cidx * 16 + b * 4:cidx * 16 + b * 4 + 4], wuvm[:, rc, h, :], olatT[:, rc, h * 4:(h + 1) * 4], n == 0, n == 3,
                         [wuvm, olatT], [yraw])
                    n += 1


def mla_sample_out(k, l, m2, wg, wo, rtile):
    P = k.P
    TP = k.TP
    tl = k.tiles[k.NT]
    yraw = k.m_yraw
    gsT = k.tile(m2, [128, 64], F32, "gsT")
    yTs = k.tile(m2, [128, 4, 16], BF16, "yTs")
    bank = P.ps_alloc()
    for cidx in range(4):
        for kk in range(8):
            k.mm(bank[:, cidx * 16:(cidx + 1) * 16], wg[:, kk, cidx * 128:(cidx + 1) * 128], k.hT[:, kk, TP:TP + 16], kk == 0, kk == 7,
                 [wg, k.hT], [bank])
    k.act(gsT[:, :], bank[:, 0:64], SILU, [bank], [gsT])
    P.ps_release(bank)
    k.tt("dve", yTs[:, :, :], yraw[:, 0:64].rearrange("p (c t) -> p c t", c=4), gsT[:, :].rearrange("p (c t) -> p c t", c=4), ALU.mult,
         [yraw, gsT], [yTs])
    P.ps_release(yraw)
    out_proj_add(k, rtile, tl, yTs, 4, wo)


def sample_mask():
    m = np.zeros((16, 4, 8, 4), np.float32)
    for s_ in range(16):
        b, p = s_ // 4, s_ % 4
        for t in range(4):
            if p <= t:
                m[s_, b, :, t] = 1.0
    return m.reshape(16, 128)


def build(TP, NPG, NPHYS):
    k = K(TP, NPG, NPHYS)
    setup(k)
    for l in range(DEPTH):
        boundary(k, l)
        ssd_phase(k, l)
        mla_phase(k, l)
        gdn_phase(k, l)
    boundary(k, DEPTH)
    finish(k)
    return k


def core_inputs(k, c, inp, consts):
    TP, NPG = k.TP, k.NPG
    m = {}
    m["x_all"] = np.concatenate([inp["x_prompt"][c], inp["x_sample"][4 * c:4 * c + 4].reshape(16, D)], 0)
    for l_ in range(DEPTH):
        m["cache_cat%d" % l_] = inp["_cache_cat"][l_]
    m["st_ssd_conv"] = inp["state_ssd_conv"][:, 4 * c:4 * c + 4]
    m["st_ssd"] = inp["state_ssd"][:, 4 * c:4 * c + 4].reshape(DEPTH, 4, 1024, 128)
    m["st_gdn_conv"] = inp["state_gdn_conv"][:, 4 * c:4 * c + 4]
    m["st_gdn"] = inp["state_gdn"][:, 4 * c:4 * c + 4]
    m["page_table"] = inp["page_table"][4 * c:4 * c + 4].astype(np.int32)
    for nm in ["emb_ln_g", "emb_ln_b", "w_in", "ssd_conv_w", "ssd_conv_b", "ssd_dt_bias", "ssd_a_log", "ssd_d", "ssd_norm_w",
               "mla_q_norm_w", "mla_w_uq", "mla_kv_norm_w", "gdn_conv_w", "gdn_dt_bias", "gdn_a_log", "gdn_norm_w", "w_out",
               "ln_g", "ln_b"]:
        m[nm] = inp[nm]
    m["mla_w_uk"] = inp["mla_w_uk"].reshape(DEPTH, 256, 512)
    m["mla_w_uv"] = inp["mla_w_uv"].reshape(DEPTH, 256, 512)
    m.update(consts)
    return {a: np.ascontiguousarray(b) for a, b in m.items() if a in k.inputs}


def make_consts(TP, NPG):
    c = host_consts(128, 4, 4)
    pos = np.concatenate([np.arange(TP), np.tile(NPG * 128 + np.arange(4), 4)])
    cs, sn = rope_tables(pos)
    c["cos_fm"] = np.ascontiguousarray(np.concatenate([cs, cs], 1).T)
    c["sin_fm"] = np.ascontiguousarray(np.concatenate([sn, sn], 1).T)
    c["cos_tm"], c["sin_tm"] = cs, sn
    c["smask"] = sample_mask()
    c["iota_p"] = np.arange(128, dtype=np.float32)[:, None]
    return c


def run(inp, ncores):
    inp = {a: np.asarray(b) for a, b in inp.items()}
    TP = inp["x_prompt"].shape[1]
    NPG = inp["page_table"].shape[1]
    NPHYS = inp["cache_kv_latent"].shape[1]
    k = build(TP, NPG, NPHYS)
    consts = make_consts(TP, NPG)
    inp["_cache_cat"] = [np.concatenate([inp["cache_kv_latent"][l_], inp["cache_k_rope"][l_]], axis=-1) for l_ in range(DEPTH)]
    in_maps = [core_inputs(k, c, inp, consts) for c in range(ncores)]
    res = run_bass_kernel_spmd(k.nc, in_maps, core_ids=list(range(ncores)))
    R = res.results
    f = np.float32
    B, BS = ncores, 4 * ncores
    y_p = np.stack([R[c]["y_all"][:TP] for c in range(B)]).astype(f)
    y_s = np.concatenate([R[c]["y_all"][TP:].reshape(4, 4, D) for c in range(B)]).astype(f)

    def pcat(nm, shp):
        return np.stack([np.asarray(R[c][nm]).reshape((DEPTH,) + shp) for c in range(B)], 1).astype(f)

    def scat(nm, shp):
        return np.concatenate([np.asarray(R[c][nm]).reshape((DEPTH, 4) + shp) for c in range(B)], 1).astype(f)

    return (y_p, y_s,
            pcat("p_lat", (TP, 256)), pcat("p_rope", (TP, 32)), pcat("p_ssd_conv", (3, 1536)), pcat("p_ssd", (16, 64, 128)),
            pcat("p_gdn_conv", (3, 1536)), pcat("p_gdn", (4, 128, 128)),
            scat("s_lat", (4, 256)), scat("s_rope", (4, 32)), scat("s_ssd_conv", (3, 1536)), scat("s_ssd", (16, 64, 128)),
            scat("s_gdn_conv", (3, 1536)), scat("s_gdn", (4, 128, 128)))


def kernel(**inputs):
    return run(inputs, NCORES)
```

```python
import contextlib
import numpy as np
import ml_dtypes
import concourse.bass as bass
import concourse.mybir as mybir
from concourse.bass_utils import run_bass_kernel_spmd

F32 = mybir.dt.float32
BF16 = mybir.dt.bfloat16
I32 = mybir.dt.int32
AF = mybir.ActivationFunctionType
import os
SILU = AF.Silu if os.environ.get('NOSILU') is None else AF.Sigmoid
ALU = mybir.AluOpType
AX = mybir.AxisListType

D = 1024
DEPTH = 2
NCORES = 8
ALPHA = float((2 * DEPTH) ** 0.25)
LN_EPS, RMS_EPS, L2_EPS = 1e-5, 1e-6, 1e-6
OFF = dict(ssd_z=0, ssd_xbc=1024, ssd_dt=2560, cq=2576, ckv=2960, kr=3216, gate=3248,
           gdn_qkv=3760, gdn_z=5296, gdn_b=5808, gdn_a=5812, end=5816)
MLA_SCALE = float(96 ** -0.5)
NEG = -30000.0


class Buf:
    __slots__ = ("name", "writers", "readers", "dma", "base")

    def __init__(self, name):
        self.name = name
        self.writers = {}
        self.readers = {}
        self.base = {}
        self.dma = None


class Prog:
    CE = ("pe", "act", "dve", "pool")

    def __init__(self, nc, st, n_dma_sems=64):
        self.nc = nc
        self.eng = {"pe": nc.tensor, "act": nc.scalar, "dve": nc.vector, "pool": nc.gpsimd, "sp": nc.sync}
        self.esem = {e: st.enter_context(nc.semaphore("es_" + e)) for e in self.CE}
        self.cnt = {e: 0 for e in self.CE}
        self.waited = {e: {} for e in self.eng}
        self.dpool = [[st.enter_context(nc.semaphore("ds%d" % i)), 0] for i in range(n_dma_sems)]
        self.dfree = {"sp": list(range(0, n_dma_sems * 5 // 8)), "pool": list(range(n_dma_sems * 5 // 8, n_dma_sems))}
        self.dlive = {}
        self.psum_free = []
        self.n_inst = 0

    def _wait(self, eng, sem, val):
        w = self.waited[eng]
        if w.get(id(sem), 0) >= val:
            return
        w[id(sem)] = val
        self.eng[eng].wait_ge(sem, val)
        self.n_inst += 1

    def _deps(self, eng, reads, writes, accs):
        deps = []
        for b in reads:
            deps += [(d, "raw") for d in b.writers.values()]
            if b.name.startswith("psb"):
                deps += [(d, "war") for d in b.readers.values()]
        for b in writes:
            deps += [(d, "waw") for d in b.writers.values()]
            deps += [(d, "war") for d in b.readers.values()]
        for b in accs:
            deps += [(d, "war") for d in b.readers.values()]
            deps += [(d, "waw") for d in b.base.values()]
        for (sem, val, src), kind in deps:
            if src == eng and eng == "pe":
                continue
            self._wait(eng, sem, val)

    def _post(self, ev, reads, writes, accs):
        k = id(ev[0])
        for b in reads:
            b.readers[k] = ev
        for b in writes:
            b.writers = {k: ev}
            b.base = {k: ev}
            b.readers = {}
        for b in accs:
            b.writers[k] = ev

    def op(self, eng, fn, reads=(), writes=(), accs=()):
        self._deps(eng, reads, writes, accs)
        inst = fn(self.eng[eng])
        self.cnt[eng] += 1
        inst.then_inc(self.esem[eng], 1)
        self.n_inst += 1
        self._post((self.esem[eng], self.cnt[eng], eng), reads, writes, accs)

    def dma(self, q, fn, sbuf, reads=(), writes=(), accs=()):
        self._deps(q, reads, writes, accs)
        if sbuf.dma is None:
            sbuf.dma = {}
        if q not in sbuf.dma:
            sbuf.dma[q] = self.dfree[q].pop(0)
            self.dlive[(id(sbuf), q)] = (sbuf, q)
        ent = self.dpool[sbuf.dma[q]]
        inst = fn(self.eng[q])
        ent[1] += 16
        inst.then_inc(ent[0], 16)
        self.n_inst += 1
        self._post((ent[0], ent[1], "dma"), reads, writes, accs)

    def barrier(self, release=True):
        for e in self.eng:
            for e2 in self.CE:
                if e2 != e and self.cnt[e2] > 0:
                    self._wait(e, self.esem[e2], self.cnt[e2])
            for b, q in self.dlive.values():
                ent = self.dpool[b.dma[q]]
                self._wait(e, ent[0], ent[1])
        if release:
            for b, q in list(self.dlive.values()):
                self.dfree[q].append(b.dma[q])
                del b.dma[q]
            self.dlive = {}

    def ps_alloc(self):
        assert self.psum_free, "out of PSUM banks"
        return self.psum_free.pop(0)

    def ps_release(self, bank):
        self.psum_free.append(bank)


class T:
    __slots__ = ("t", "b")

    def __init__(self, t, name):
        self.t = t
        self.b = Buf(name)

    def __getitem__(self, k):
        return self.t[k]


def bc_last(ap, n):
    sh = list(ap.shape)
    return ap.unsqueeze(len(sh)).to_broadcast(sh + [n])


def host_consts(LP, NS, LS):
    c = {}
    c["ident_f"] = np.eye(128, dtype=np.float32)
    c["ident_b"] = np.eye(128, dtype=np.float32).astype(ml_dtypes.bfloat16)
    c["ones_f"] = np.ones((128, 128), np.float32)

    def pack(seq, pos, L):
        i = np.arange(L)
        same = seq[:, None] == seq[None, :]
        U = (same & (pos[:, None] <= pos[None, :])).astype(np.float32)
        mneg_st = np.where(same & (pos[:, None] <= pos[None, :]), 0.0, NEG).astype(np.float32)
        mneg_ts = np.where(same & (pos[None, :] < pos[:, None]), 0.0, NEG).astype(np.float32)
        m01_st = (same & (pos[:, None] <= pos[None, :])).astype(np.float32)
        out = np.zeros((128, 4, 128), np.float32)
        out[:L, 0, :L] = U
        out[:L, 1, :L] = mneg_st
        out[:L, 2, :L] = mneg_ts
        out[:L, 3, :L] = same.astype(np.float32)
        out[:, 1, :][out[:, 1, :] == 0] += 0.0
        out[L:, 1, :] = NEG
        out[L:, 2, :] = NEG
        out[:L, 1, L:] = NEG
        out[:L, 2, L:] = NEG
        return out, m01_st

    seq = np.zeros(LP, np.int64)
    pos = np.arange(LP)
    c["pk_p"], _ = pack(seq, pos, LP)
    Ls = NS * LS
    seq = np.arange(Ls) // LS
    pos = np.arange(Ls) % LS
    c["pk_s"], _ = pack(seq, pos, Ls)
    si = np.zeros((128, NS, 128), np.float32)
    for b in range(NS):
        si[b * LS:(b + 1) * LS, b, :] = 1.0
    c["seqind_s"] = si
    cm = np.zeros((128, NS, Ls), np.float32)
    for b in range(NS):
        cm[:, b, b * LS:(b + 1) * LS] = 1.0
    c["cmask_s"] = cm.astype(ml_dtypes.bfloat16)
    nl = 7
    lm = np.zeros((128, nl, 128), np.float32)
    i = np.arange(128)
    for j in range(nl):
        bsz = 1 << j
        same_pair = (i[:, None] // (2 * bsz)) == (i[None, :] // (2 * bsz))
        up = (i[:, None] % (2 * bsz)) >= bsz
        lo = (i[None, :] % (2 * bsz)) < bsz
        lm[:, j, :] = (same_pair & up & lo).astype(np.float32)
    c["lmask"] = lm
    c["lmaskT"] = np.ascontiguousarray(lm.transpose(2, 1, 0))
    return c


def rope_tables(positions):
    half = 16
    inv = (10000.0 ** (-np.arange(half, dtype=np.float32) / half)).astype(np.float32)
    ang = positions.astype(np.float32)[:, None] * inv[None, :]
    return np.cos(ang).astype(np.float32), np.sin(ang).astype(np.float32)


class K:
    def __init__(self, TP, NPG, NPHYS, debug=()):
        self.TP, self.NPG, self.NPHYS = TP, NPG, NPHYS
        self.NT = TP // 128
        self.NS, self.LS = 4, 4
        self.LSS = 16
        self.debug = set(debug)
        self.cut = None
        self.nc = nc = bass.Bass("TRN2", target_bir_lowering=False)
        self.st = contextlib.ExitStack()
        self.P = Prog(nc, self.st)
        self.uid = 0
        self.inputs = {}
        self.outputs = {}

    def din(self, name, shape, dt=F32):
        ap = self.nc.dram_tensor(name, list(shape), dt, kind="ExternalInput").ap()
        self.inputs[name] = ap
        return ap

    def dout(self, name, shape, dt=F32):
        ap = self.nc.dram_tensor(name, list(shape), dt, kind="ExternalOutput").ap()
        self.outputs[name] = ap
        return ap

    def tile(self, scope, shape, dt=F32, name="t"):
        self.uid += 1
        nm = "%s_%d" % (name, self.uid)
        return T(scope.enter_context(self.nc.sbuf_tensor(nm, list(shape), dt)), nm)

    def mm(self, out, lhsT, rhs, start, stop, reads, writes):
        self.P.op("pe", lambda e: e.matmul(out, lhsT, rhs, start=start, stop=stop),
                  reads=[x.b for x in reads], writes=[x.b for x in writes])

    def tr(self, out, in_, ident, reads, writes):
        self.P.op("pe", lambda e: e.transpose(out, in_, ident),
                  reads=[x.b for x in reads], writes=[x.b for x in writes])

    def act(self, out, in_, func, reads, writes, bias=None, scale=None, accum_out=None, accs=()):
        kw = {}
        if bias is not None:
            kw["bias"] = bias
        if scale is not None:
            kw["scale"] = scale
        if accum_out is not None:
            kw["accum_out"] = accum_out
        self.P.op("act", lambda e: e.activation(out, in_, func, **kw),
                  reads=[x.b for x in reads], writes=[x.b for x in writes], accs=[x.b for x in accs])

    def tt(self, eng, out, in0, in1, op, reads, writes, accs=()):
        self.P.op(eng, lambda e: e.tensor_tensor(out, in0, in1, op),
                  reads=[x.b for x in reads], writes=[x.b for x in writes], accs=[x.b for x in accs])

    def ts(self, eng, out, in0, s1, s2, op0, op1, reads, writes, accs=()):
        if op1 is None:
            f = lambda e: e.tensor_scalar(out, in0, s1, None, op0)
        else:
            f = lambda e: e.tensor_scalar(out, in0, s1, s2, op0, op1)
        self.P.op(eng, f, reads=[x.b for x in reads], writes=[x.b for x in writes], accs=[x.b for x in accs])

    def stt(self, out, in0, scalar, in1, op0, op1, reads, writes, accs=()):
        self.P.op("dve", lambda e: e.scalar_tensor_tensor(out, in0, scalar, in1, op0, op1),
                  reads=[x.b for x in reads], writes=[x.b for x in writes], accs=[x.b for x in accs])

    def cp(self, eng, out, in_, reads, writes, accs=()):
        if eng == "act":
            f = lambda e: e.copy(out, in_)
        else:
            f = lambda e: e.tensor_copy(out, in_)
        self.P.op(eng, f, reads=[x.b for x in reads], writes=[x.b for x in writes], accs=[x.b for x in accs])

    def memset(self, eng, t, ap, val):
        self.P.op(eng, lambda e: e.memset(ap, val), writes=[t.b])

    def load(self, q, t, out, in_, accs=False, **kw):
        self.P.dma(q, lambda e: e.dma_start(out=out, in_=in_, **kw), t.b,
                   writes=[] if accs else [t.b], accs=[t.b] if accs else [])

    def store(self, q, t, out, in_, **kw):
        self.P.dma(q, lambda e: e.dma_start(out=out, in_=in_, **kw), t.b, reads=[t.b])

    def dbg(self, name, t, ap, shape, dt=F32):
        if name in self.debug:
            o = self.dout("dbg_" + name, shape, dt)
            self.store("sp", t, o, ap)


def setup(k):
    nc, st, TP, NPG, NPHYS = k.nc, k.st, k.TP, k.NPG, k.NPHYS
    NTOK = TP + k.LSS
    k.NTOK = NTOK
    i = k.i = {}
    i["x_all"] = k.din("x_all", [NTOK, D])
    i["cache_cat"] = [k.din("cache_cat%d" % l_, [NPHYS, 128, 288]) for l_ in range(DEPTH)]
    i["st_ssd_conv"] = k.din("st_ssd_conv", [DEPTH, 4, 3, 1536])
    i["st_ssd"] = k.din("st_ssd", [DEPTH, 4, 1024, 128])
    i["st_gdn_conv"] = k.din("st_gdn_conv", [DEPTH, 4, 3, 1536])
    i["st_gdn"] = k.din("st_gdn", [DEPTH, 4, 4, 128, 128])
    i["page_table"] = k.din("page_table", [4, NPG], I32)
    for nm, sh in [("emb_ln_g", [D]), ("emb_ln_b", [D]), ("w_in", [DEPTH, D, 5816]),
                   ("ssd_conv_w", [DEPTH, 4, 1536]), ("ssd_conv_b", [DEPTH, 1536]),
                   ("ssd_dt_bias", [DEPTH, 16]), ("ssd_a_log", [DEPTH, 16]), ("ssd_d", [DEPTH, 16]),
                   ("ssd_norm_w", [DEPTH, 1024]), ("mla_q_norm_w", [DEPTH, 384]),
                   ("mla_w_uq", [DEPTH, 384, 768]), ("mla_kv_norm_w", [DEPTH, 256]),
                   ("mla_w_uk", [DEPTH, 256, 512]), ("mla_w_uv", [DEPTH, 256, 512]),
                   ("gdn_conv_w", [DEPTH, 4, 1536]), ("gdn_dt_bias", [DEPTH, 4]),
                   ("gdn_a_log", [DEPTH, 4]), ("gdn_norm_w", [DEPTH, 128]),
                   ("w_out", [DEPTH, 2048, D]), ("ln_g", [DEPTH, D]), ("ln_b", [DEPTH, D])]:
        i[nm] = k.din(nm, sh)
    i["ident_f"] = k.din("ident_f", [128, 128])
    i["ident_b"] = k.din("ident_b", [128, 128], BF16)
    i["ones_f"] = k.din("ones_f", [128, 128])
    i["pk_p"] = k.din("pk_p", [128, 4, 128])
    i["pk_s"] = k.din("pk_s", [128, 4, 128])
    i["seqind_s"] = k.din("seqind_s", [128, 4, 128])
    i["cmask_s"] = k.din("cmask_s", [128, 4, 16], BF16)
    i["lmask"] = k.din("lmask", [128, 7, 128])
    i["lmaskT"] = k.din("lmaskT", [128, 7, 128])
    i["cos_fm"] = k.din("cos_fm", [32, NTOK])
    i["sin_fm"] = k.din("sin_fm", [32, NTOK])
    i["cos_tm"] = k.din("cos_tm", [NTOK, 16])
    i["sin_tm"] = k.din("sin_tm", [NTOK, 16])
    i["smask"] = k.din("smask", [16, 128])
    i["iota_p"] = k.din("iota_p", [128, 1])
    o = k.o = {}
    o["y_all"] = k.dout("y_all", [NTOK, D])
    o["p_lat"] = k.dout("p_lat", [DEPTH, TP, 256])
    o["p_rope"] = k.dout("p_rope", [DEPTH, TP, 32])
    o["p_ssd_conv"] = k.dout("p_ssd_conv", [DEPTH, 1, 3, 1536])
    o["p_ssd"] = k.dout("p_ssd", [DEPTH, 1, 1024, 128])
    o["p_gdn_conv"] = k.dout("p_gdn_conv", [DEPTH, 1, 3, 1536])
    o["p_gdn"] = k.dout("p_gdn", [DEPTH, 1, 4, 128, 128])
    o["s_lat"] = k.dout("s_lat", [DEPTH, 16, 256])
    o["s_rope"] = k.dout("s_rope", [DEPTH, 16, 32])
    o["s_ssd_conv"] = k.dout("s_ssd_conv", [DEPTH, 4, 3, 1536])
    o["s_ssd"] = k.dout("s_ssd", [DEPTH, 4, 1024, 128])
    o["s_gdn_conv"] = k.dout("s_gdn_conv", [DEPTH, 4, 3, 1536])
    o["s_gdn"] = k.dout("s_gdn", [DEPTH, 4, 4, 128, 128])
    k.r_dram = k.dout("r_scratch", [NTOK, D])
    k.hT = k.tile(st, [128, 8, NTOK], BF16, "hT")
    c = k.c = {}
    for nm, sh, dt in [("ident_f", [128, 128], F32), ("ident_b", [128, 128], BF16), ("ones_f", [128, 128], F32),
                       ("pk_p", [128, 4, 128], F32), ("pk_s", [128, 4, 128], F32)]:
        c[nm] = k.tile(st, sh, dt, nm)
        k.load("sp", c[nm], c[nm][:], i[nm])
    k.banks = []
    for b in range(8):
        t = T(st.enter_context(nc.psum_tensor("psb%d" % b, [128, 512], F32)), "psb%d" % b)
        k.banks.append(t)
        k.P.psum_free.append(t)
    k.tiles = [(t * 128, 128, 1, 128, "pk_p") for t in range(k.NT)] + [(TP, 16, 4, 4, "pk_s")]


def bf_view(bank_ap):
    return bank_ap.bitcast(BF16)


def boundary(k, l):
    P, i = k.P, k.i
    with contextlib.ExitStack() as ph:
        gb = k.tile(ph, [128, 2, D], F32, "gb")
        if l == 0:
            g_src, b_src = i["emb_ln_g"], i["emb_ln_b"]
        else:
            g_src, b_src = i["ln_g"][l - 1], i["ln_b"][l - 1]
        k.load("sp", gb, gb[:, 0, :], g_src.partition_broadcast(128))
        k.load("sp", gb, gb[:, 1, :], b_src.partition_broadcast(128), accs=True)
        xs = [k.tile(ph, [128, D], F32, "xs") for _ in range(2)]
        hs = [k.tile(ph, [128, D], F32, "hs") for _ in range(2)]
        hb = [k.tile(ph, [128, D], BF16, "hb") for _ in range(2)]
        sm = [k.tile(ph, [128, 16], F32, "sm") for _ in range(2)]
        for ti, (r0, L, NS, Lb, pk) in enumerate(k.tiles):
            x, h, hbt, s = xs[ti % 2], hs[ti % 2], hb[ti % 2], sm[ti % 2]
            src = i["x_all"] if l == 0 else k.r_dram
            k.load("sp", x, x[0:L, :], src[r0:r0 + L, :])
            P.op("dve", lambda e: e.bn_stats(s[0:L, 0:6], x[0:L, 0:512]), reads=[x.b], writes=[s.b])
            P.op("dve", lambda e: e.bn_stats(s[0:L, 6:12], x[0:L, 512:1024]), reads=[x.b], accs=[s.b])
            P.op("dve", lambda e: e.bn_aggr(s[0:L, 12:14], s[0:L, 0:12]), reads=[s.b], accs=[s.b])
            k.ts("dve", s[0:L, 14:15], s[0:L, 13:14], LN_EPS, None, ALU.add, None, [s], [], accs=[s])
            k.act(s[0:L, 15:16], s[0:L, 14:15], AF.Ln, [s], [], accs=[s])
            k.act(s[0:L, 14:15], s[0:L, 15:16], AF.Exp, [s], [], scale=-0.5, accs=[s])
            k.ts("dve", h[0:L, :], x[0:L, :], s[0:L, 12:13], s[0:L, 14:15], ALU.subtract, ALU.mult, [x, s], [h])
            k.tt("pool", h[0:L, :], h[0:L, :], gb[0:L, 0, :], ALU.mult, [h, gb], [h])
            k.tt("dve", h[0:L, :], h[0:L, :], gb[0:L, 1, :], ALU.add, [h, gb], [h])
            if l == DEPTH:
                k.store("sp", h, k.o["y_all"][r0:r0 + L, :], h[0:L, :])
                continue
            P.op("act", lambda e: e.mul(x[0:L, :], h[0:L, :], ALPHA), reads=[h.b], writes=[x.b])
            k.store("sp", x, k.r_dram[r0:r0 + L, :], x[0:L, :])
            k.cp("pool", hbt[0:L, :], h[0:L, :], [h], [hbt])
            bank = P.ps_alloc()
            bv = bf_view(bank[:, :])
            for kk in range(8):
                k.tr(bv[:, kk * 128:kk * 128 + L], hbt[0:L, kk * 128:(kk + 1) * 128], k.c["ident_b"][0:L, 0:L],
                     [hbt, k.c["ident_b"]], [bank])
            k.cp("act", k.hT[:, :, r0:r0 + L], bv[:, 0:1024].rearrange("p (c t) -> p c t", c=8)[:, :, 0:L],
                 [bank], [], accs=[k.hT])
            P.ps_release(bank)
            if "hT" in k.debug and ti == 0:
                pass
        P.barrier()


def finish(k):
    P = k.P
    P.barrier(release=False)
    k.st.close()


def load_w_cast(k, t, dst3, src2, ncols, kchunks=8):
    for kk in range(kchunks):
        k.load("pool", t, dst3[:, kk, 0:ncols], src2[kk * 128:(kk + 1) * 128, :], accs=True, max_dma_last_dim=2048)


def fm_vec(k, t, dst, src1d, nchunk):
    with k.nc.allow_non_contiguous_dma(reason="tiny per-partition parameter vectors"):
        k.load("sp", t, dst, src1d.rearrange("(c p) -> p c", p=128), accs=True)


def conv_prep(k, ph, conv_w, conv_b):
    cw = k.tile(ph, [128, 12, 4], F32, "cw")
    for kk in range(4):
        with k.nc.allow_non_contiguous_dma(reason="tiny conv taps"):
            k.load("sp", cw, cw[:, :, kk], conv_w[kk].rearrange("(c p) -> p c", p=128), accs=True)
    cb = None
    if conv_b is not None:
        cb = k.tile(ph, [128, 12], F32, "cb")
        fm_vec(k, cb, cb[:, :], conv_b, 12)
    dg = k.tile(ph, [128, 12, 4, 128], BF16, "dg")
    for c in range(12):
        for kk in range(4):
            eng = "pool" if (c + kk) % 2 else "dve"
            k.ts(eng, dg[:, c, kk, :], k.c["ident_f"][:, :], cw[:, c, kk:kk + 1], None, ALU.mult, None,
                 [k.c["ident_f"], cw], [], accs=[dg])
    return dg, cw, cb


def conv_tile(k, tl, pre, prev_pre, wx, dg, first, st_in, csout, want_state):
    P = k.P
    r0, L, NS, Lb, pk = tl
    if first:
        if st_in is None:
            k.memset("pool", pre, pre[:, :, :, 0:3], 0.0)
        else:
            tmp = st_in
            k.cp("pool", pre[:, :, :, 0:3], tmp[:, :, :, :], [tmp], [pre])
    else:
        k.cp("pool", pre[:, :, :, 0:3], prev_pre[:, :, :, Lb:Lb + 3], [prev_pre], [pre])
    for cg in range(3):
        bank = P.ps_alloc()
        for cc in range(4):
            c = cg * 4 + cc
            for kk in range(8):
                k.mm(bank[:, cc * 128:cc * 128 + L], wx[:, kk, c * 128:(c + 1) * 128], k.hT[:, kk, r0:r0 + L],
                     kk == 0, kk == 7, [wx, k.hT], [bank])
        src = bank[:, :].rearrange("p (c t) -> p c t", c=4)[:, :, 0:L].rearrange("p c (b t) -> p c b t", b=NS)
        k.cp("act", pre[:, cg * 4:(cg + 1) * 4, :, 3:3 + Lb], src, [bank], [], accs=[pre])
        if want_state:
            k.cp("dve", csout[:, cg * 4:(cg + 1) * 4, :, :], src[:, :, :, Lb - 3:Lb], [bank], [], accs=[csout])
        P.ps_release(bank)
    outs = []
    for cg in range(3):
        bank = P.ps_alloc()
        for cc in range(4):
            c = cg * 4 + cc
            for kk in range(4):
                k.mm(bank[:, cc * 128:cc * 128 + L].rearrange("p (b t) -> p b t", b=NS), dg[:, c, kk, :],
                     pre[:, c, :, kk:kk + Lb], kk == 0, kk == 3, [dg, pre], [bank])
        outs.append(bank)
    return outs


def conv_state_io(k, ph, st_dram, NS):
    t = k.tile(ph, [128, 12, NS, 3], F32, "cst")
    with k.nc.allow_non_contiguous_dma(reason="small conv state"):
        for b in range(NS):
            for j in range(3):
                k.load("sp", t, t[:, :, b, j], st_dram[b, j].rearrange("(c p) -> p c", p=128), accs=True)
    return t


def conv_state_store(k, csout, out_dram, NS):
    with k.nc.allow_non_contiguous_dma(reason="small conv state"):
        for b in range(NS):
            for j in range(3):
                k.store("sp", csout, out_dram[b, j].rearrange("(c p) -> p c", p=128), csout[:, :, b, j])


def out_proj_add(k, ph_tiles, tl, yT, nk, wo):
    P = k.P
    r0, L, NS, Lb, pk = tl
    rt = ph_tiles
    k.load("sp", rt, rt[0:L, :], k.r_dram[r0:r0 + L, :])
    for nb in range(2):
        bank = P.ps_alloc()
        for kk in range(nk):
            k.mm(bank[0:L, :], yT[:, kk, 0:L], wo[:, kk, nb * 512:(nb + 1) * 512], kk == 0, kk == nk - 1, [yT, wo], [bank])
        k.tt("dve", rt[0:L, nb * 512:(nb + 1) * 512], rt[0:L, nb * 512:(nb + 1) * 512], bank[0:L, :], ALU.add,
             [rt, bank], [], accs=[rt])
        P.ps_release(bank)
    k.store("sp", rt, k.r_dram[r0:r0 + L, :], rt[0:L, :])


def ssd_phase(k, l):
    P, i, o, c = k.P, k.i, k.o, k.c
    with contextlib.ExitStack() as ph:
        wz = k.tile(ph, [128, 8, 1024], BF16, "wz")
        wx = k.tile(ph, [128, 8, 1536], BF16, "wx")
        wdt = k.tile(ph, [128, 8, 16], BF16, "wdt")
        wo = k.tile(ph, [128, 8, 1024], BF16, "wo")
        load_w_cast(k, wz, wz, i["w_in"][l][:, OFF["ssd_z"]:OFF["ssd_z"] + 1024], 1024)
        load_w_cast(k, wx, wx, i["w_in"][l][:, OFF["ssd_xbc"]:OFF["ssd_xbc"] + 1536], 1536)
        load_w_cast(k, wdt, wdt, i["w_in"][l][:, OFF["ssd_dt"]:OFF["ssd_dt"] + 16], 16)
        load_w_cast(k, wo, wo, i["w_out"][l][0:1024, :], 1024)
        dg, cw, cb = conv_prep(k, ph, i["ssd_conv_w"][l], i["ssd_conv_b"][l])
        hp = k.tile(ph, [128, 3, 16], F32, "hp")
        k.load("sp", hp, hp[:, 0, :], i["ssd_dt_bias"][l].partition_broadcast(128))
        k.load("sp", hp, hp[:, 1, :], i["ssd_a_log"][l].partition_broadcast(128), accs=True)
        k.load("sp", hp, hp[:, 2, :], i["ssd_d"][l].partition_broadcast(128), accs=True)
        k.act(hp[:, 1, :], hp[:, 1, :], AF.Exp, [hp], [hp])
        k.ts("dve", hp[:, 1, :], hp[:, 1, :], -1.0, None, ALU.mult, None, [hp], [hp])
        nw = k.tile(ph, [128, 8], F32, "nw")
        fm_vec(k, nw, nw[:, :], i["ssd_norm_w"][l], 8)
        seqind = k.tile(ph, [128, 4, 128], F32, "seqind")
        k.load("sp", seqind, seqind[:], i["seqind_s"])
        cmask = k.tile(ph, [128, 4, 16], BF16, "cmask")
        k.load("sp", cmask, cmask[:], i["cmask_s"])
        W = {}
        for nm, sh, dt in [("zs", [128, 1024], BF16), ("xs_fm", [128, 8, 128], F32), ("B_fm", [128, 2, 128], BF16),
                           ("C_fm", [128, 2, 128], BF16), ("x_tok", [128, 16, 64], F32), ("B_tok", [128, 2, 128], BF16),
                           ("sm", [128, 8, 16], F32), ("xdt", [128, 16, 64], BF16), ("xdtw", [128, 16, 64], BF16),
                           ("adtb", [128, 4, 128], F32), ("tmp", [128, 4, 128], F32), ("dec", [128, 4, 128], F32),
                           ("scT", [128, 16, 128], BF16), ("t1", [128, 16, 64], F32), ("y", [128, 16, 64], F32),
                           ("yn", [128, 1024], BF16), ("yT", [128, 8, 128], BF16), ("rt", [128, 1024], F32),
                           ("elast", [128, 4, 16], F32), ("junk", [128, 512], BF16), ("Bm", [128, 2, 128], BF16),
                           ("Cm", [128, 4, 2, 16], BF16)]:
            W[nm] = k.tile(ph, sh, dt, nm)

        if k.cut == "loads":
            P.barrier()
            return

        def run_stream(tiles, NS, Lb, st_conv_dram, st_dram, out_conv, out_st):
            with contextlib.ExitStack() as sp:
                pres = [k.tile(sp, [128, 12, NS, Lb + 3], BF16, "pre") for _ in range(2)]
                csout = k.tile(sp, [128, 12, NS, 3], F32, "csout")
                hst = [k.tile(sp, [128, 2, 512], F32, "hst") for _ in range(NS)]
                hsb = [k.tile(sp, [128, 2, 512], BF16, "hsb") for _ in range(NS)]
                st_in = None
                if st_dram is None:
                    for b in range(NS):
                        k.memset("pool", hst[b], hst[b][:], 0.0)
                        k.memset("pool", hsb[b], hsb[b][:], 0.0)
                else:
                    if not os.environ.get('NOCSIN'):
                        st_in = conv_state_io(k, sp, st_conv_dram, NS)
                    stg = k.tile(sp, [128, 8, 128], F32, "stg")
                    for b in range(NS if not os.environ.get('NOSTIN') else 0):
                        k.load("sp", stg, stg[:], st_dram[b].rearrange("(j p) n -> p j n", p=128))
                        for half in range(2):
                            bank = P.ps_alloc()
                            for jj in range(4):
                                j = half * 4 + jj
                                k.tr(bank[:, jj * 128:(jj + 1) * 128], stg[:, j, :], c["ident_f"][:, :], [stg, c["ident_f"]], [bank])
                            k.cp("dve", hst[b][:, half, :], bank[:, :], [bank], [], accs=[hst[b]])
                            k.cp("act", hsb[b][:, half, :], hst[b][:, half, :], [hst[b]], [], accs=[hsb[b]])
                            P.ps_release(bank)
                for ti, tl in enumerate(tiles):
                    ssd_tile(k, l, tl, W, pres[ti % 2], pres[(ti + 1) % 2], ti == 0, st_in, csout, ti == len(tiles) - 1,
                             hst, hsb, wz, wx, wdt, wo, dg, cb, hp, nw, seqind, cmask)
                if not os.environ.get('NOCS'):
                    conv_state_store(k, csout, out_conv, NS)
                stg2 = k.tile(sp, [128, 8, 128], F32, "stg2")
                for b in range(NS if not os.environ.get('NOSTOUT') else 0):
                    for half in range(2):
                        bank = P.ps_alloc()
                        for jj in range(4):
                            k.tr(bank[:, jj * 128:(jj + 1) * 128], hst[b][:, half, jj * 128:(jj + 1) * 128], c["ident_f"][:, :],
                                 [hst[b], c["ident_f"]], [bank])
                        k.cp("dve", stg2[:, half * 4:(half + 1) * 4, :], bank[:, :].rearrange("p (j n) -> p j n", j=4),
                             [bank], [stg2] if half == 0 else [], accs=[] if half == 0 else [stg2])
                        P.ps_release(bank)
                    k.store("sp", stg2, out_st[b].rearrange("(j p) n -> p j n", p=128), stg2[:])
                P.barrier()

        run_stream(k.tiles[:k.NT], 1, 128, None, None, o["p_ssd_conv"][l], o["p_ssd"][l])
        if not os.environ.get("NOSAMPLE"):
            run_stream(k.tiles[k.NT:], 4, 4, i["st_ssd_conv"][l], i["st_ssd"][l], o["s_ssd_conv"][l], o["s_ssd"][l])
        P.barrier()


def ssd_tile(k, l, tl, W, pre, prev_pre, first, st_in, csout, last, hst, hsb, wz, wx, wdt, wo, dg, cb, hp, nw, seqind, cmask):
    P, c = k.P, k.c
    r0, L, NS, Lb, pkn = tl
    pk = c[pkn]
    U, MNEG, SAME = pk[0:L, 0, 0:L], pk[0:L, 1, 0:L], pk[0:L, 3, 0:L]
    sm = W["sm"]
    zs = W["zs"]
    for nb in range(2):
        bank = P.ps_alloc()
        for kk in range(8):
            k.mm(bank[0:L, :], k.hT[:, kk, r0:r0 + L], wz[:, kk, nb * 512:(nb + 1) * 512], kk == 0, kk == 7, [k.hT, wz], [bank])
        k.act(zs[0:L, nb * 512:(nb + 1) * 512], bank[0:L, :], SILU, [bank], [zs] if nb == 0 else [], accs=[] if nb == 0 else [zs])
        P.ps_release(bank)
    if k.cut is not None and 2 > int(k.cut):
        return
    cbanks = conv_tile(k, tl, pre, prev_pre, wx, dg, first, st_in, csout, last)
    xs_fm, B_fm, C_fm = W["xs_fm"], W["B_fm"], W["C_fm"]
    for cg in range(3):
        bank = cbanks[cg]
        for cc in range(4):
            ch = cg * 4 + cc
            if ch < 8:
                dst, dt_ = xs_fm[:, ch, 0:L], xs_fm
            elif ch < 10:
                dst, dt_ = B_fm[:, ch - 8, 0:L], B_fm
            else:
                dst, dt_ = C_fm[:, ch - 10, 0:L], C_fm
            k.act(dst, bank[:, cc * 128:cc * 128 + L], SILU, [bank, cb], [], bias=cb[:, ch:ch + 1], accs=[dt_])
        P.ps_release(bank)
    if k.cut is not None and 3 > int(k.cut):
        return
    x_tok, B_tok = W["x_tok"], W["B_tok"]
    for half in range(2):
        bank = P.ps_alloc()
        for jj in range(4):
            ch = half * 4 + jj
            k.tr(bank[0:L, jj * 128:(jj + 1) * 128], xs_fm[:, ch, 0:L], c["ident_f"][:, :], [xs_fm, c["ident_f"]], [bank])
        k.cp("act" if half else "dve", x_tok[0:L, half * 8:(half + 1) * 8, :],
             bank[0:L, :].rearrange("p (h d) -> p h d", h=8), [bank], [], accs=[x_tok])
        P.ps_release(bank)
    bank = P.ps_alloc()
    bv = bf_view(bank[:, :])
    for g in range(2):
        k.tr(bv[0:L, g * 128:(g + 1) * 128], B_fm[:, g, 0:L], c["ident_b"][:, :], [B_fm, c["ident_b"]], [bank])
    k.cp("dve", B_tok[0:L, :, :], bv[0:L, 0:256].rearrange("p (g n) -> p g n", g=2), [bank], [B_tok])
    P.ps_release(bank)
    if k.cut is not None and 4 > int(k.cut):
        return
    bank = P.ps_alloc()
    for kk in range(8):
        k.mm(bank[0:L, 0:16], k.hT[:, kk, r0:r0 + L], wdt[:, kk, :], kk == 0, kk == 7, [k.hT, wdt], [bank])
    k.tt("dve", sm[0:L, 6, :], bank[0:L, 0:16], hp[0:L, 0, :], ALU.add, [bank, hp], [sm])
    k.act(sm[0:L, 6, :], sm[0:L, 6, :], AF.Exp, [sm], [sm])
    k.act(sm[0:L, 0, :], sm[0:L, 6, :], AF.Ln, [sm], [sm], bias=1.0)
    k.tt("dve", sm[0:L, 1, :], sm[0:L, 0, :], hp[0:L, 1, :], ALU.mult, [sm, hp], [sm])
    k.mm(bank[0:L, 16:32], U, sm[0:L, 1, :], True, True, [pk, sm], [bank])
    k.mm(bank[0:L, 32:48], SAME, sm[0:L, 1, :], True, True, [pk, sm], [bank])
    elast = W["elast"]
    for b in range(NS):
        lhs = c["ones_f"][0:L, :] if NS == 1 else seqind[0:L, b, :]
        k.mm(bank[:, 64 + b * 16:64 + (b + 1) * 16], lhs, sm[0:L, 1, :], True, True, [c["ones_f"], seqind, sm], [bank])
    k.cp("dve", sm[0:L, 2, :], bank[0:L, 16:32], [bank], [sm])
    k.ts("dve", sm[0:L, 3, :], bank[0:L, 16:32], -1.0, None, ALU.mult, None, [bank], [sm])
    k.act(sm[0:L, 4, :], bank[0:L, 16:32], AF.Exp, [bank], [sm])
    k.tt("dve", sm[0:L, 6, :], bank[0:L, 32:48], sm[0:L, 2, :], ALU.subtract, [bank, sm], [sm])
    k.act(sm[0:L, 6, :], sm[0:L, 6, :], AF.Exp, [sm], [sm])
    k.tt("dve", sm[0:L, 5, :], sm[0:L, 6, :], sm[0:L, 0, :], ALU.mult, [sm], [sm])
    k.act(elast[:, 0:NS, :], bank[:, 64:64 + NS * 16].rearrange("p (b h) -> p b h", b=NS), AF.Exp, [bank], [elast])
    P.ps_release(bank)
    xdt, xdtw = W["xdt"], W["xdtw"]
    k.tt("dve", xdt[0:L, :, :], x_tok[0:L, :, :], bc_last(sm[0:L, 0, :], 64), ALU.mult, [x_tok, sm], [xdt])
    k.tt("pool", xdtw[0:L, :, :], x_tok[0:L, :, :], bc_last(sm[0:L, 5, :], 64), ALU.mult, [x_tok, sm], [xdtw])
    if k.cut is not None and 5 > int(k.cut):
        return
    adtb, tmp, dec, scT = W["adtb"], W["tmp"], W["dec"], W["scT"]
    cbb = P.ps_alloc()
    for g in range(2):
        k.mm(cbb[0:L, g * 128:g * 128 + L], B_fm[:, g, 0:L], C_fm[:, g, 0:L], True, True, [B_fm, C_fm], [cbb])
    for hq in range(4):
        k.cp("pool", adtb[0:L, :, 0:L], bc_last(sm[0:L, 1, hq * 4:(hq + 1) * 4], L), [sm], [adtb])
        bank = P.ps_alloc()
        for hh in range(4):
            h = hq * 4 + hh
            k.mm(bank[0:L, hh * 128:hh * 128 + L], adtb[0:L, hh, 0:L], U, True, True, [adtb, pk], [bank])
        bview = bank[0:L, :].rearrange("p (h t) -> p h t", h=4)[:, :, 0:L]
        k.tt("dve", tmp[0:L, :, 0:L], bview, MNEG.unsqueeze(1).to_broadcast([L, 4, L]), ALU.add, [bank, pk], [tmp])
        P.ps_release(bank)
        for hh in range(4):
            h = hq * 4 + hh
            k.act(dec[0:L, hh, 0:L], tmp[0:L, hh, 0:L], AF.Exp, [tmp, sm], [dec] if hh == 0 else [], bias=sm[0:L, 3, h:h + 1],
                  accs=[] if hh == 0 else [dec])
        g = hq // 2
        k.tt("dve", scT[0:L, hq * 4:(hq + 1) * 4, 0:L], dec[0:L, :, 0:L],
             cbb[0:L, g * 128:g * 128 + L].unsqueeze(1).to_broadcast([L, 4, L]), ALU.mult, [dec, cbb], [], accs=[scT])
    P.ps_release(cbb)
    if k.cut is not None and 6 > int(k.cut):
        return
    yb = [P.ps_alloc(), P.ps_alloc()]
    for h in range(16):
        g = h // 8
        k.mm(yb[g][0:L, (h % 8) * 64:(h % 8 + 1) * 64], scT[0:L, h, 0:L], xdt[0:L, h, :], True, True, [scT, xdt], [yb[g]])
    Cm = W["Cm"]
    if NS > 1:
        for b in range(NS):
            k.tt("pool", Cm[:, b, :, 0:L], C_fm[:, :, 0:L], cmask[:, b, 0:L].unsqueeze(1).to_broadcast([128, 2, L]),
                 ALU.mult, [C_fm, cmask], [], accs=[Cm])
    t1, y = W["t1"], W["y"]
    for g in range(2):
        bank = P.ps_alloc()
        for b in range(NS):
            lhs = C_fm[:, g, 0:L] if NS == 1 else Cm[:, b, g, 0:L]
            k.mm(bank[0:L, :], lhs, hsb[b][:, g, :], b == 0, b == NS - 1, [C_fm, Cm, hsb[b]], [bank])
        k.tt("dve", t1[0:L, g * 8:(g + 1) * 8, :], bank[0:L, :].rearrange("p (h d) -> p h d", h=8),
             bc_last(sm[0:L, 4, g * 8:(g + 1) * 8], 64), ALU.mult, [bank, sm], [], accs=[t1])
        P.ps_release(bank)
    k.tt("pool", y[0:L, :, :], x_tok[0:L, :, :], bc_last(hp[0:L, 2, :], 64), ALU.mult, [x_tok, hp], [y])
    k.tt("pool", t1[0:L, :, :], t1[0:L, :, :], y[0:L, :, :], ALU.add, [t1, y], [t1])
    for g in range(2):
        k.tt("dve", y[0:L, g * 8:(g + 1) * 8, :], yb[g][0:L, :].rearrange("p (h d) -> p h d", h=8),
             t1[0:L, g * 8:(g + 1) * 8, :], ALU.add, [yb[g], t1], [], accs=[y])
        P.ps_release(yb[g])
    if k.cut is not None and 7 > int(k.cut):
        return
    Bm = W["Bm"]
    for b in range(NS):
        if NS > 1:
            k.ts("pool", Bm[0:L, :, :], B_tok[0:L, :, :], seqind[0:L, b, 0:1], None, ALU.mult, None, [B_tok, seqind], [Bm])
        for g in range(2):
            bank = P.ps_alloc()
            lhs = B_tok[0:L, g, :] if NS == 1 else Bm[0:L, g, :]
            k.mm(bank[:, :], lhs, xdtw[0:L, g * 8:(g + 1) * 8, :], True, True, [B_tok, Bm, xdtw], [bank])
            hv = hst[b][:, g, :].rearrange("p (h d) -> p h d", h=8)
            k.tt("pool", hv, hv, bc_last(elast[:, b, g * 8:(g + 1) * 8], 64), ALU.mult, [hst[b], elast], [hst[b]])
            k.tt("dve", hst[b][:, g, :], hst[b][:, g, :], bank[:, :], ALU.add, [hst[b], bank], [hst[b]])
            k.cp("act", hsb[b][:, g, :], hst[b][:, g, :], [hst[b]], [], accs=[hsb[b]])
            P.ps_release(bank)
    if k.cut is not None and 8 > int(k.cut):
        return
    yf = y[0:L, :, :]
    k.tt("dve", yf, yf, zs[0:L, :].rearrange("p (h d) -> p h d", h=16), ALU.mult, [y, zs], [y])
    junk, yn, yT = W["junk"], W["yn"], W["yT"]
    for g in range(2):
        k.act(junk[0:L, :], y[0:L, g * 8:(g + 1) * 8, :], AF.Square, [y], [junk], accum_out=sm[0:L, 7, g:g + 1], accs=[sm])
    k.ts("dve", sm[0:L, 7, 2:4], sm[0:L, 7, 0:2], 1.0 / 512, RMS_EPS, ALU.mult, ALU.add, [sm], [sm])
    k.act(sm[0:L, 7, 2:4], sm[0:L, 7, 2:4], AF.Ln, [sm], [sm])
    k.act(sm[0:L, 7, 4:6], sm[0:L, 7, 2:4], AF.Exp, [sm], [sm], scale=-0.5)
    for g in range(2):
        k.ts("dve", yn[0:L, g * 512:(g + 1) * 512], y[0:L, g * 8:(g + 1) * 8, :], sm[0:L, 7, 4 + g:5 + g], None, ALU.mult, None,
             [y, sm], [], accs=[yn])
    bank = P.ps_alloc()
    bv = bf_view(bank[:, :])
    for kk in range(8):
        k.tr(bv[:, kk * 128:kk * 128 + L], yn[0:L, kk * 128:(kk + 1) * 128], c["ident_b"][0:L, 0:L], [yn, c["ident_b"]], [bank])
    for kk in range(8):
        k.ts("dve", yT[:, kk, 0:L], bv[:, kk * 128:kk * 128 + L], nw[:, kk:kk + 1], None, ALU.mult, None,
             [bank, nw], [], accs=[yT])
    P.ps_release(bank)
    out_proj_add(k, W["rt"], tl, yT, 8, wo)


def gdn_phase(k, l):
    P, i, o, c = k.P, k.i, k.o, k.c
    with contextlib.ExitStack() as ph:
        wq = k.tile(ph, [128, 8, 1536], BF16, "gwq")
        wz = k.tile(ph, [128, 8, 512], BF16, "gwz")
        wba = k.tile(ph, [128, 8, 8], BF16, "gwba")
        wo = k.tile(ph, [128, 4, 1024], BF16, "gwo")
        load_w_cast(k, wq, wq, i["w_in"][l][:, OFF["gdn_qkv"]:OFF["gdn_qkv"] + 1536], 1536)
        load_w_cast(k, wz, wz, i["w_in"][l][:, OFF["gdn_z"]:OFF["gdn_z"] + 512], 512)
        load_w_cast(k, wba, wba, i["w_in"][l][:, OFF["gdn_b"]:OFF["gdn_b"] + 8], 8)
        load_w_cast(k, wo, wo, i["w_out"][l][1536:2048, :], 1024, kchunks=4)
        dg, cw, _ = conv_prep(k, ph, i["gdn_conv_w"][l], None)
        hp = k.tile(ph, [128, 2, 4], F32, "ghp")
        k.load("sp", hp, hp[:, 0, :], i["gdn_dt_bias"][l].partition_broadcast(128))
        k.load("sp", hp, hp[:, 1, :], i["gdn_a_log"][l].partition_broadcast(128), accs=True)
        k.act(hp[:, 1, :], hp[:, 1, :], AF.Exp, [hp], [hp])
        k.ts("dve", hp[:, 1, :], hp[:, 1, :], -1.0, None, ALU.mult, None, [hp], [hp])
        gnw = k.tile(ph, [128, 128], F32, "gnw")
        k.load("sp", gnw, gnw[:, :], i["gdn_norm_w"][l].partition_broadcast(128))
        seqind = k.tile(ph, [128, 4, 128], F32, "seqind")
        k.load("sp", seqind, seqind[:], i["seqind_s"])
        cmask = k.tile(ph, [128, 4, 16], BF16, "cmask")
        k.load("sp", cmask, cmask[:], i["cmask_s"])
        lm = k.tile(ph, [128, 7, 128], F32, "lm")
        k.load("sp", lm, lm[:], i["lmask"])
        lmT = k.tile(ph, [128, 7, 128], F32, "lmT")
        k.load("sp", lmT, lmT[:], i["lmaskT"])
        W = {}
        for nm, sh, dt in [("qkv_fm", [128, 12, 128], F32), ("qkv_tok", [128, 12, 128], F32), ("sm", [128, 12, 8], F32),
                           ("qk_b", [128, 8, 128], BF16), ("qkT", [128, 8, 128], BF16), ("zs", [128, 512], BF16),
                           ("gbc", [128, 4, 128], F32), ("tmp", [128, 4, 128], F32), ("decst", [128, 4, 128], F32),
                           ("dects", [128, 4, 128], F32), ("A", [128, 4, 128], F32), ("AT", [128, 4, 128], F32),
                           ("Tm", [128, 1, 8], F32), ("TT", [128, 1, 8], F32), ("X", [128, 1, 8], F32),
                           ("attnT", [128, 4, 128], BF16), ("vb", [128, 4, 128], F32), ("kbg", [128, 4, 128], F32),
                           ("kd", [128, 4, 128], BF16), ("wTn", [128, 4, 4, 128], F32), ("vn_f", [128, 4, 128], F32),
                           ("vn_b", [128, 4, 128], BF16), ("os", [128, 4, 128], F32), ("of", [128, 4, 128], F32),
                           ("y", [128, 512], BF16), ("yT", [128, 4, 128], BF16), ("rt", [128, 1024], F32),
                           ("elast", [128, 4, 4], F32), ("junk", [128, 128], BF16), ("qTm", [128, 4, 4, 16], BF16), ("kdm", [128, 4, 128], BF16)]:
            W[nm] = k.tile(ph, sh, dt, "g" + nm)
        for nm in ["Tg", "TTg", "Xg", "tg"]:
            W[nm] = [k.tile(ph, [128, 2, 128], F32, "g" + nm) for _ in range(2)]

        def run_stream(tiles, NS, Lb, st_conv_dram, st_dram, out_conv, out_st):
            with contextlib.ExitStack() as sp:
                pres = [k.tile(sp, [128, 12, NS, Lb + 3], BF16, "gpre") for _ in range(2)]
                csout = k.tile(sp, [128, 12, NS, 3], F32, "gcsout")
                S = [k.tile(sp, [128, 4, 128], F32, "gS") for _ in range(NS)]
                Sb = [k.tile(sp, [128, 4, 128], BF16, "gSb") for _ in range(NS)]
                st_in = None
                for b in range(NS):
                    if st_dram is None:
                        k.memset("pool", S[b], S[b][:], 0.0)
                    else:
                        k.load("sp", S[b], S[b][:], st_dram[b].rearrange("h d e -> d h e"))
                    k.cp("act", Sb[b][:], S[b][:], [S[b]], [Sb[b]])
                if st_dram is not None:
                    st_in = conv_state_io(k, sp, st_conv_dram, NS)
                for ti, tl in enumerate(tiles):
                    gdn_tile(k, l, tl, W, pres[ti % 2], pres[(ti + 1) % 2], ti == 0, st_in, csout, ti == len(tiles) - 1,
                             S, Sb, wq, wz, wba, wo, dg, hp, gnw, seqind, cmask, lm, lmT)
                conv_state_store(k, csout, out_conv, NS)
                for b in range(NS):
                    k.store("sp", S[b], out_st[b].rearrange("h d e -> d h e"), S[b][:])
                P.barrier()

        run_stream(k.tiles[:k.NT], 1, 128, None, None, o["p_gdn_conv"][l], o["p_gdn"][l])
        run_stream(k.tiles[k.NT:], 4, 4, i["st_gdn_conv"][l], i["st_gdn"][l], o["s_gdn_conv"][l], o["s_gdn"][l])
        P.barrier()


def gdn_tile(k, l, tl, W, pre, prev_pre, first, st_in, csout, last, S, Sb, wq, wz, wba, wo, dg, hp, gnw, seqind, cmask, lm, lmT):
    P, c = k.P, k.c
    r0, L, NS, Lb, pkn = tl
    pk = c[pkn]
    U, MNEG_ST, MNEG_TS, SAME = pk[0:L, 0, 0:L], pk[0:L, 1, 0:L], pk[0:L, 2, 0:L], pk[0:L, 3, 0:L]
    IDF = c["ident_f"]
    sm = W["sm"]
    nlev = int(np.log2(Lb))

    def b4(ap2):
        return ap2.unsqueeze(1).to_broadcast([L, 4, L])

    cbanks = conv_tile(k, tl, pre, prev_pre, wq, dg, first, st_in, csout, last)
    qkv_fm, qkv_tok = W["qkv_fm"], W["qkv_tok"]
    for cg in range(3):
        k.act(qkv_fm[:, cg * 4:(cg + 1) * 4, 0:L], cbanks[cg][:, :].rearrange("p (c t) -> p c t", c=4)[:, :, 0:L], SILU,
              [cbanks[cg]], [], accs=[qkv_fm])
        P.ps_release(cbanks[cg])
    for cg in range(3):
        bank = P.ps_alloc()
        for cc in range(4):
            k.tr(bank[0:L, cc * 128:(cc + 1) * 128], qkv_fm[:, cg * 4 + cc, 0:L], IDF[:, :], [qkv_fm, IDF], [bank])
        k.cp("dve" if cg % 2 else "act", qkv_tok[0:L, cg * 4:(cg + 1) * 4, :], bank[0:L, :].rearrange("p (c d) -> p c d", c=4),
             [bank], [], accs=[qkv_tok])
        P.ps_release(bank)
    zs = W["zs"]
    bank = P.ps_alloc()
    for kk in range(8):
        k.mm(bank[0:L, :], k.hT[:, kk, r0:r0 + L], wz[:, kk, :], kk == 0, kk == 7, [k.hT, wz], [bank])
    k.act(zs[0:L, :], bank[0:L, :], SILU, [bank], [zs])
    P.ps_release(bank)
    bank = P.ps_alloc()
    for kk in range(8):
        k.mm(bank[0:L, 0:8], k.hT[:, kk, r0:r0 + L], wba[:, kk, :], kk == 0, kk == 7, [k.hT, wba], [bank])
    k.act(sm[0:L, 6, 0:4], bank[0:L, 0:4], AF.Exp, [bank], [sm], scale=-1.0)
    k.ts("dve", sm[0:L, 6, 0:4], sm[0:L, 6, 0:4], 1.0, None, ALU.add, None, [sm], [sm])
    P.op("dve", lambda e: e.reciprocal(sm[0:L, 3, 0:4], sm[0:L, 6, 0:4]), reads=[sm.b], writes=[sm.b])
    k.tt("dve", sm[0:L, 6, 4:8], bank[0:L, 4:8], hp[0:L, 0, :], ALU.add, [bank, hp], [sm])
    k.act(sm[0:L, 6, 4:8], sm[0:L, 6, 4:8], AF.Exp, [sm], [sm])
    k.act(sm[0:L, 6, 4:8], sm[0:L, 6, 4:8], AF.Ln, [sm], [sm], bias=1.0)
    k.tt("dve", sm[0:L, 3, 4:8], sm[0:L, 6, 4:8], hp[0:L, 1, :], ALU.mult, [sm, hp], [sm])
    g = sm[0:L, 3, 4:8]
    k.mm(bank[0:L, 16:20], U, g, True, True, [pk, sm], [bank])
    k.mm(bank[0:L, 32:36], SAME, g, True, True, [pk, sm], [bank])
    elast = W["elast"]
    for b in range(NS):
        lhs = c["ones_f"][0:L, :] if NS == 1 else seqind[0:L, b, :]
        k.mm(bank[:, 64 + b * 4:64 + (b + 1) * 4], lhs, g, True, True, [c["ones_f"], seqind, sm], [bank])
    k.cp("dve", sm[0:L, 4, 0:4], bank[0:L, 16:20], [bank], [sm])
    k.ts("dve", sm[0:L, 4, 4:8], bank[0:L, 16:20], -1.0, None, ALU.mult, None, [bank], [sm])
    k.act(sm[0:L, 5, 0:4], bank[0:L, 16:20], AF.Exp, [bank], [sm])
    k.tt("dve", sm[0:L, 6, 0:4], bank[0:L, 32:36], sm[0:L, 4, 0:4], ALU.subtract, [bank, sm], [sm])
    k.act(sm[0:L, 5, 4:8], sm[0:L, 6, 0:4], AF.Exp, [sm], [sm])
    k.act(elast[:, 0:NS, :], bank[:, 64:64 + NS * 4].rearrange("p (b h) -> p b h", b=NS), AF.Exp, [bank], [elast])
    P.ps_release(bank)
    k.tt("dve", sm[0:L, 7, 0:4], sm[0:L, 3, 0:4], sm[0:L, 5, 0:4], ALU.mult, [sm], [sm])
    junk = W["junk"]
    for j in range(8):
        k.act(junk[0:L, :], qkv_tok[0:L, j, :], AF.Square, [qkv_tok], [junk], accum_out=sm[0:L, 0, j:j + 1], accs=[sm])
    k.ts("dve", sm[0:L, 1, :], sm[0:L, 0, :], L2_EPS, None, ALU.add, None, [sm], [sm])
    k.act(sm[0:L, 1, :], sm[0:L, 1, :], AF.Ln, [sm], [sm])
    k.act(sm[0:L, 1, :], sm[0:L, 1, :], AF.Exp, [sm], [sm], scale=-0.5)
    k.ts("dve", sm[0:L, 1, 0:4], sm[0:L, 1, 0:4], float(128 ** -0.5), None, ALU.mult, None, [sm], [sm])
    qk_b, qkT = W["qk_b"], W["qkT"]
    k.tt("dve", qkv_tok[0:L, 0:8, :], qkv_tok[0:L, 0:8, :], bc_last(sm[0:L, 1, :], 128), ALU.mult, [qkv_tok, sm], [qkv_tok])
    k.cp("pool", qk_b[0:L, :, :], qkv_tok[0:L, 0:8, :], [qkv_tok], [qk_b])
    bank = P.ps_alloc()
    bv = bf_view(bank[:, :])
    for j in range(8):
        k.tr(bv[:, j * 128:j * 128 + L], qk_b[0:L, j, :], c["ident_b"][0:L, 0:L], [qk_b, c["ident_b"]], [bank])
    k.cp("act", qkT[:, :, 0:L], bv[:, 0:1024].rearrange("p (c t) -> p c t", c=8)[:, :, 0:L], [bank], [qkT])
    P.ps_release(bank)
    vb, kbg, kd = W["vb"], W["kbg"], W["kd"]
    k.tt("dve", vb[0:L, :, :], qkv_tok[0:L, 8:12, :], bc_last(sm[0:L, 3, 0:4], 128), ALU.mult, [qkv_tok, sm], [vb])
    k.tt("pool", kbg[0:L, :, :], qkv_tok[0:L, 4:8, :], bc_last(sm[0:L, 7, 0:4], 128), ALU.mult, [qkv_tok, sm], [kbg])
    k.tt("pool", kd[0:L, :, :], qkv_tok[0:L, 4:8, :], bc_last(sm[0:L, 5, 4:8], 128), ALU.mult, [qkv_tok, sm], [kd])
    gbc, tmp, decst, dects = W["gbc"], W["tmp"], W["decst"], W["dects"]
    k.cp("pool", gbc[0:L, :, 0:L], bc_last(g, L), [sm], [gbc])
    gb = P.ps_alloc()
    for h in range(4):
        k.mm(gb[0:L, h * 128:h * 128 + L], gbc[0:L, h, 0:L], U, True, True, [gbc, pk], [gb])
    gbv = gb[0:L, :].rearrange("p (h t) -> p h t", h=4)[:, :, 0:L]
    k.tt("dve", tmp[0:L, :, 0:L], gbv, b4(MNEG_ST), ALU.add, [gb, pk], [tmp])
    for h in range(4):
        k.act(decst[0:L, h, 0:L], tmp[0:L, h, 0:L], AF.Exp, [tmp, sm], [], bias=sm[0:L, 4, 4 + h:5 + h], accs=[decst])
    k.stt(tmp[0:L, :, 0:L], gbv, -1.0, b4(MNEG_TS), ALU.mult, ALU.add, [gb, pk, decst], [tmp])
    P.ps_release(gb)
    for h in range(4):
        k.act(dects[0:L, h, 0:L], tmp[0:L, h, 0:L], AF.Exp, [tmp, sm], [], bias=sm[0:L, 4, h:h + 1], accs=[dects])
    A, AT, attnT = W["A"], W["AT"], W["attnT"]
    kkb = P.ps_alloc()
    qkb = P.ps_alloc()
    for h in range(4):
        k.mm(kkb[0:L, h * 128:h * 128 + L], qkT[:, 4 + h, 0:L], qkT[:, 4 + h, 0:L], True, True, [qkT], [kkb])
        k.mm(qkb[0:L, h * 128:h * 128 + L], qkT[:, 4 + h, 0:L], qkT[:, h, 0:L], True, True, [qkT], [qkb])
    for h in range(4):
        k.stt(A[0:L, h, 0:L], kkb[0:L, h * 128:h * 128 + L], sm[0:L, 3, h:h + 1], dects[0:L, h, 0:L], ALU.mult, ALU.mult,
              [kkb, sm, dects], [], accs=[A])
    P.ps_release(kkb)
    k.tt("dve", attnT[0:L, :, 0:L], qkb[0:L, :].rearrange("p (h t) -> p h t", h=4)[:, :, 0:L], decst[0:L, :, 0:L], ALU.mult,
         [qkb, decst], [attnT])
    P.ps_release(qkb)
    Tm, TT, X = W["Tm"], W["TT"], W["X"]

    def transpose4(dst, src):
        bank = P.ps_alloc()
        for h in range(4):
            k.tr(bank[0:L, h * 128:h * 128 + L], src[0:L, h, 0:L], IDF[0:L, 0:L], [src, IDF], [bank])
        k.cp("act", dst[0:L, :, 0:L], bank[0:L, :].rearrange("p (h t) -> p h t", h=4)[:, :, 0:L], [bank], [dst])
        P.ps_release(bank)

    transpose4(AT, A)
    Tg, TTg, Xg, tg = W["Tg"], W["TTg"], W["Xg"], W["tg"]

    def b2(ap2):
        return ap2.unsqueeze(1).to_broadcast([L, 2, L])

    for g in range(2):
        hs = slice(2 * g, 2 * g + 2)
        k.tt("dve", Xg[g][0:L, :, 0:L], A[0:L, hs, 0:L], b2(lm[0:L, 0, 0:L]), ALU.mult, [A, lm], [Xg[g]])
        k.tt("dve", Tg[g][0:L, :, 0:L], b2(IDF[0:L, 0:L]), Xg[g][0:L, :, 0:L], ALU.subtract, [IDF, Xg[g]], [Tg[g]])
        k.tt("pool", tg[g][0:L, :, 0:L], AT[0:L, hs, 0:L], b2(lmT[0:L, 0, 0:L]), ALU.mult, [AT, lmT], [tg[g]])
        k.tt("pool", TTg[g][0:L, :, 0:L], b2(IDF[0:L, 0:L]), tg[g][0:L, :, 0:L], ALU.subtract, [IDF, tg[g]], [TTg[g]])
    for j in range(1, nlev):
        bx = [P.ps_alloc(), P.ps_alloc()]
        for g in range(2):
            for hh in range(2):
                k.mm(bx[g][0:L, hh * 128:hh * 128 + L], AT[0:L, 2 * g + hh, 0:L], Tg[g][0:L, hh, 0:L], True, True, [AT, Tg[g]], [bx[g]])
        for g in range(2):
            k.cp("act", Xg[g][0:L, :, 0:L], bx[g][0:L, 0:256].rearrange("p (h t) -> p h t", h=2)[:, :, 0:L], [bx[g]], [Xg[g]])
            P.ps_release(bx[g])
        bm = [P.ps_alloc(), P.ps_alloc()]
        for g in range(2):
            for hh in range(2):
                k.mm(bm[g][0:L, hh * 128:hh * 128 + L], TTg[g][0:L, hh, 0:L], Xg[g][0:L, hh, 0:L], True, True, [TTg[g], Xg[g]], [bm[g]])
        for g in range(2):
            k.tt("dve", tg[g][0:L, :, 0:L], bm[g][0:L, 0:256].rearrange("p (h t) -> p h t", h=2)[:, :, 0:L], b2(lm[0:L, j, 0:L]), ALU.mult,
                 [bm[g], lm], [tg[g]])
            P.ps_release(bm[g])
            k.tt("dve", Tg[g][0:L, :, 0:L], Tg[g][0:L, :, 0:L], tg[g][0:L, :, 0:L], ALU.subtract, [Tg[g], tg[g]], [Tg[g]])
        bt = [P.ps_alloc(), P.ps_alloc()]
        for g in range(2):
            for hh in range(2):
                k.tr(bt[g][0:L, hh * 128:hh * 128 + L], Tg[g][0:L, hh, 0:L], IDF[0:L, 0:L], [Tg[g], IDF], [bt[g]])
        for g in range(2):
            k.cp("act", TTg[g][0:L, :, 0:L], bt[g][0:L, 0:256].rearrange("p (h t) -> p h t", h=2)[:, :, 0:L], [bt[g]], [TTg[g]])
            P.ps_release(bt[g])

    class _TT:
        b = None
    def TTh(h):
        return TTg[h // 2][0:L, h % 2, 0:L], TTg[h // 2]
    wTn, vn_f, vn_b = W["wTn"], W["vn_f"], W["vn_b"]
    bank = P.ps_alloc()
    for h in range(4):
        k.mm(bank[:, h * 128:h * 128 + L], kbg[0:L, h, :], TTh(h)[0], True, True, [kbg, TTh(h)[1]], [bank])
    wsrc = bank[:, :].rearrange("p (h t) -> p h t", h=4)[:, :, 0:L]
    if NS == 1:
        k.ts("dve", wTn[:, 0, :, 0:L], wsrc, -1.0, None, ALU.mult, None, [bank], [wTn])
    else:
        for b in range(NS):
            k.stt(wTn[:, b, :, 0:L], wsrc, -1.0, cmask[:, b, 0:L].unsqueeze(1).to_broadcast([128, 4, L]), ALU.mult, ALU.mult,
                  [bank, cmask], [] if b else [wTn], accs=[wTn] if b else [])
    P.ps_release(bank)
    bank = P.ps_alloc()
    for h in range(4):
        k.mm(bank[0:L, h * 128:(h + 1) * 128], TTh(h)[0], vb[0:L, h, :], True, False, [TTh(h)[1], vb], [bank])
        for b in range(NS):
            k.mm(bank[0:L, h * 128:(h + 1) * 128], wTn[:, b, h, 0:L], S[b][:, h, :], False, b == NS - 1, [wTn, S[b]], [bank])
    k.cp("dve", vn_f[0:L, :, :], bank[0:L, :].rearrange("p (h e) -> p h e", h=4), [bank], [vn_f])
    P.ps_release(bank)
    k.cp("act", vn_b[0:L, :, :], vn_f[0:L, :, :], [vn_f], [vn_b])
    osb, of, qTm = W["os"], W["of"], W["qTm"]
    if NS > 1:
        for b in range(NS):
            k.tt("pool", qTm[:, b, :, 0:L], qkT[:, 0:4, 0:L], cmask[:, b, 0:L].unsqueeze(1).to_broadcast([128, 4, L]), ALU.mult,
                 [qkT, cmask], [] if b else [qTm], accs=[qTm] if b else [])
    bank = P.ps_alloc()
    for h in range(4):
        for b in range(NS):
            lhs = qkT[:, h, 0:L] if NS == 1 else qTm[:, b, h, 0:L]
            k.mm(bank[0:L, h * 128:(h + 1) * 128], lhs, Sb[b][:, h, :], b == 0, b == NS - 1, [qkT, qTm, Sb[b]], [bank])
    k.tt("dve", osb[0:L, :, :], bank[0:L, :].rearrange("p (h e) -> p h e", h=4), bc_last(sm[0:L, 5, 0:4], 128), ALU.mult,
         [bank, sm], [osb])
    P.ps_release(bank)
    bank = P.ps_alloc()
    for h in range(4):
        k.mm(bank[0:L, h * 128:(h + 1) * 128], attnT[0:L, h, 0:L], vn_b[0:L, h, :], True, True, [attnT, vn_b], [bank])
    k.tt("dve", of[0:L, :, :], bank[0:L, :].rearrange("p (h e) -> p h e", h=4), osb[0:L, :, :], ALU.add, [bank, osb], [of])
    P.ps_release(bank)
    for b in range(NS):
        if NS > 1:
            kdm = W["kdm"]
            k.ts("pool", kdm[0:L, :, :], kd[0:L, :, :], seqind[0:L, b, 0:1], None, ALU.mult, None, [kd, seqind], [kdm])
        else:
            kdm = kd
        bank = P.ps_alloc()
        for h in range(4):
            k.mm(bank[:, h * 128:(h + 1) * 128], kdm[0:L, h, :], vn_b[0:L, h, :], True, True, [kdm, vn_b], [bank])
        k.tt("pool", S[b][:, :, :], S[b][:, :, :], bc_last(elast[:, b, :], 128), ALU.mult, [S[b], elast], [S[b]])
        k.tt("dve", S[b][:, :, :], S[b][:, :, :], bank[:, :].rearrange("p (h e) -> p h e", h=4), ALU.add, [S[b], bank], [S[b]])
        k.cp("act", Sb[b][:, :, :], S[b][:, :, :], [S[b]], [Sb[b]])
        P.ps_release(bank)
    for h in range(4):
        k.act(junk[0:L, :], of[0:L, h, :], AF.Square, [of], [junk], accum_out=sm[0:L, 8, h:h + 1], accs=[sm])
    k.ts("dve", sm[0:L, 8, 4:8], sm[0:L, 8, 0:4], 1.0 / 128, RMS_EPS, ALU.mult, ALU.add, [sm], [sm])
    k.act(sm[0:L, 8, 4:8], sm[0:L, 8, 4:8], AF.Ln, [sm], [sm])
    k.act(sm[0:L, 8, 4:8], sm[0:L, 8, 4:8], AF.Exp, [sm], [sm], scale=-0.5)
    k.tt("dve", of[0:L, :, :], of[0:L, :, :], bc_last(sm[0:L, 8, 4:8], 128), ALU.mult, [of, sm], [of])
    k.tt("pool", of[0:L, :, :], of[0:L, :, :], gnw[0:L, :].unsqueeze(1).to_broadcast([L, 4, 128]), ALU.mult, [of, gnw], [of])
    y, yT = W["y"], W["yT"]
    k.tt("dve", y[0:L, :].rearrange("p (h e) -> p h e", h=4), of[0:L, :, :], zs[0:L, :].rearrange("p (h e) -> p h e", h=4), ALU.mult,
         [of, zs], [y])
    bank = P.ps_alloc()
    bv = bf_view(bank[:, :])
    for kk in range(4):
        k.tr(bv[:, kk * 128:kk * 128 + L], y[0:L, kk * 128:(kk + 1) * 128], c["ident_b"][0:L, 0:L], [y, c["ident_b"]], [bank])
    k.cp("act", yT[:, :, 0:L], bv[:, 0:512].rearrange("p (c t) -> p c t", c=4)[:, :, 0:L], [bank], [yT])
    P.ps_release(bank)
    out_proj_add(k, W["rt"], tl, yT, 4, wo)


def mla_phase(k, l):
    P, i, o, c = k.P, k.i, k.o, k.c
    TP, NT, NTOK = k.TP, k.NT, k.NTOK
    with contextlib.ExitStack() as ph:
        cqT = k.tile(ph, [128, 3, NTOK], BF16, "cqT")
        ckvT = k.tile(ph, [128, 2, NTOK], BF16, "ckvT")
        krT = k.tile(ph, [96, NTOK], BF16, "krT")
        ckvs = k.tile(ph, [16, 257], BF16, "ckvs")
        with contextlib.ExitStack() as m1:
            wm = k.tile(m1, [128, 8, 672], BF16, "wm")
            load_w_cast(k, wm, wm, i["w_in"][l][:, OFF["cq"]:OFF["cq"] + 672], 672)
            qnw = k.tile(m1, [128, 3], F32, "qnw")
            fm_vec(k, qnw, qnw[:, :], i["mla_q_norm_w"][l], 3)
            kvw = k.tile(m1, [128, 256], F32, "kvw")
            k.load("sp", kvw, kvw[:, :], i["mla_kv_norm_w"][l].partition_broadcast(128))
            k.memset("pool", ckvs, ckvs[:, 256:257], 1.0)
            tw = [dict(cq=k.tile(m1, [128, 384], BF16, "cqn"), ckv=k.tile(m1, [128, 256], F32, "ckv"),
                       ckb=k.tile(m1, [128, 256], BF16, "ckb"), kr=k.tile(m1, [128, 32], F32, "kr"),
                       krr=k.tile(m1, [128, 32], F32, "krr"), krb=k.tile(m1, [128, 96], BF16, "krb"),
                       cs=k.tile(m1, [128, 2, 16], F32, "cs"), sm=k.tile(m1, [128, 8], F32, "sm"),
                       t=k.tile(m1, [128, 4, 16], F32, "t"), junk=k.tile(m1, [128, 384], BF16, "junk")) for _ in range(2)]
            for w in tw:
                k.memset("pool", w["krb"], w["krb"][:, 0:64], 0.0)
            for ti, (r0, L, NS, Lb, pkn) in enumerate(k.tiles):
                w = tw[ti % 2]
                sm = w["sm"]
                k.load("sp", w["cs"], w["cs"][0:L, 0, :], i["cos_tm"][r0:r0 + L, :])
                k.load("sp", w["cs"], w["cs"][0:L, 1, :], i["sin_tm"][r0:r0 + L, :], accs=True)
                b0, b1 = P.ps_alloc(), P.ps_alloc()
                for kk in range(8):
                    k.mm(b0[0:L, :], k.hT[:, kk, r0:r0 + L], wm[:, kk, 0:512], kk == 0, kk == 7, [k.hT, wm], [b0])
                for kk in range(8):
                    k.mm(b1[0:L, 0:160], k.hT[:, kk, r0:r0 + L], wm[:, kk, 512:672], kk == 0, kk == 7, [k.hT, wm], [b1])
                k.act(w["junk"][0:L, :], b0[0:L, 0:384], AF.Square, [b0], [w["junk"]], accum_out=sm[0:L, 0:1], accs=[sm])
                k.cp("dve", w["ckv"][0:L, 0:128], b0[0:L, 384:512], [b0], [w["ckv"]])
                k.cp("dve", w["ckv"][0:L, 128:256], b1[0:L, 0:128], [b1], [], accs=[w["ckv"]])
                k.cp("dve", w["kr"][0:L, :], b1[0:L, 128:160], [b1], [w["kr"]])
                k.act(w["junk"][0:L, 0:256], w["ckv"][0:L, :], AF.Square, [w["ckv"]], [w["junk"]], accum_out=sm[0:L, 1:2], accs=[sm])
                k.ts("dve", sm[0:L, 2:3], sm[0:L, 0:1], 1.0 / 384, RMS_EPS, ALU.mult, ALU.add, [sm], [sm])
                k.ts("dve", sm[0:L, 3:4], sm[0:L, 1:2], 1.0 / 256, RMS_EPS, ALU.mult, ALU.add, [sm], [sm])
                k.act(sm[0:L, 2:4], sm[0:L, 2:4], AF.Ln, [sm], [sm])
                k.act(sm[0:L, 2:4], sm[0:L, 2:4], AF.Exp, [sm], [sm], scale=-0.5)
                k.ts("dve", w["cq"][0:L, :], b0[0:L, 0:384], sm[0:L, 2:3], None, ALU.mult, None, [b0, sm], [w["cq"]])
                P.ps_release(b0)
                P.ps_release(b1)
                k.stt(w["ckv"][0:L, :], w["ckv"][0:L, :], sm[0:L, 3:4], kvw[0:L, :], ALU.mult, ALU.mult, [w["ckv"], sm, kvw], [w["ckv"]])
                lat_out = o["p_lat"][l][r0:r0 + L, :] if NS == 1 else o["s_lat"][l]
                k.store("sp", w["ckv"], lat_out, w["ckv"][0:L, :])
                k.cp("pool", w["ckb"][0:L, :], w["ckv"][0:L, :], [w["ckv"]], [w["ckb"]])
                if NS > 1:
                    k.cp("pool", ckvs[0:L, 0:256], w["ckv"][0:L, :], [w["ckv"]], [], accs=[ckvs])
                kr, krr, t, cs = w["kr"], w["krr"], w["t"], w["cs"]
                k.tt("dve", t[0:L, 0, :], kr[0:L, 0:16], cs[0:L, 0, :], ALU.mult, [kr, cs], [t])
                k.tt("dve", t[0:L, 1, :], kr[0:L, 16:32], cs[0:L, 1, :], ALU.mult, [kr, cs], [t])
                k.tt("dve", t[0:L, 2, :], kr[0:L, 16:32], cs[0:L, 0, :], ALU.mult, [kr, cs], [t])
                k.tt("dve", t[0:L, 3, :], kr[0:L, 0:16], cs[0:L, 1, :], ALU.mult, [kr, cs], [t])
                k.tt("dve", krr[0:L, 0:16], t[0:L, 0, :], t[0:L, 1, :], ALU.subtract, [t], [krr])
                k.tt("dve", krr[0:L, 16:32], t[0:L, 2, :], t[0:L, 3, :], ALU.add, [t], [krr])
                rope_out = o["p_rope"][l][r0:r0 + L, :] if NS == 1 else o["s_rope"][l]
                k.store("sp", krr, rope_out, krr[0:L, :])
                k.cp("pool", w["krb"][0:L, 64:96], krr[0:L, :], [krr], [], accs=[w["krb"]])
                bank = P.ps_alloc()
                bv = bf_view(bank[:, :])
                for j in range(3):
                    k.tr(bv[:, j * 128:j * 128 + L], w["cq"][0:L, j * 128:(j + 1) * 128], c["ident_b"][0:L, 0:L], [w["cq"], c["ident_b"]], [bank])
                for j in range(2):
                    k.tr(bv[:, (3 + j) * 128:(3 + j) * 128 + L], w["ckb"][0:L, j * 128:(j + 1) * 128], c["ident_b"][0:L, 0:L],
                         [w["ckb"], c["ident_b"]], [bank])
                k.tr(bv[0:96, 5 * 128:5 * 128 + L], w["krb"][0:L, :], c["ident_b"][0:L, 0:L], [w["krb"], c["ident_b"]], [bank])
                for j in range(3):
                    k.ts("dve", cqT[:, j, r0:r0 + L], bv[:, j * 128:j * 128 + L], qnw[:, j:j + 1], None, ALU.mult, None, [bank, qnw], [], accs=[cqT])
                k.cp("act", ckvT[:, :, r0:r0 + L], bv[:, 384:640].rearrange("p (c t) -> p c t", c=2)[:, :, 0:L], [bank], [], accs=[ckvT])
                k.cp("act", krT[64:96, r0:r0 + L], bv[64:96, 640:640 + L], [bank], [], accs=[krT])
                P.ps_release(bank)
            P.barrier()
        with contextlib.ExitStack() as m2:
            wuq = k.tile(m2, [128, 3, 768], BF16, "wuq")
            wuk = k.tile(m2, [128, 2, 512], BF16, "wuk")
            wuv = k.tile(m2, [128, 2, 512], BF16, "wuv")
            load_w_cast(k, wuq, wuq, i["mla_w_uq"][l], 768, kchunks=3)
            load_w_cast(k, wuk, wuk, i["mla_w_uk"][l], 512, kchunks=2)
            load_w_cast(k, wuv, wuv, i["mla_w_uv"][l], 512, kchunks=2)
            wqr = k.tile(m2, [128, 3, 8, 96], BF16, "wqr")
            wq4 = wuq[:, :, :].rearrange("p k (h d) -> p k h d", h=8)
            k.memset("pool", wqr, wqr[:, :, :, 0:64], 0.0)
            k.ts("dve", wqr[:, :, :, 64:80], wq4[:, :, :, 80:96], -1.0, None, ALU.mult, None, [wuq], [], accs=[wqr])
            k.cp("dve", wqr[:, :, :, 80:96], wq4[:, :, :, 64:80], [wuq], [], accs=[wqr])
            o_all = k.tile(m2, [128, NT, 512], BF16, "o_all")
            qn_s = k.tile(m2, [64, 8, 16], BF16, "qn_s")
            qr_s = k.tile(m2, [96, 8, 16], BF16, "qr_s")
            m2a = contextlib.ExitStack()
            csfs = [k.tile(m2a, [96, 2, 512], F32, "csf") for _ in range(2)]
            V = k.tile(m2a, [128, NT, 8, 65], BF16, "V")
            k.memset("pool", V, V[:, :, :, 64:65], 1.0)
            for j in range(NT):
                bank = P.ps_alloc()
                for rc in range(2):
                    k.mm(bank[:, :], ckvT[:, rc, j * 128:(j + 1) * 128], wuv[:, rc, :], rc == 0, rc == 1, [ckvT, wuv], [bank])
                k.cp("act" if j % 2 else "dve", V[:, j, :, 0:64], bank[:, :].rearrange("p (h d) -> p h d", h=8), [bank], [], accs=[V])
                P.ps_release(bank)
            QT = [k.tile(m2a, [96, NTOK], BF16, "QT") for _ in range(2)]
            KT = [k.tile(m2a, [96, TP], BF16, "KT") for _ in range(2)]
            rt = [k.tile(m2a, [96, 512], F32, "ropet") for _ in range(2)]
            pT = [k.tile(m2a, [128, 4, 128], BF16, "pT") for _ in range(3)]
            rcp = k.tile(m2a, [128, 2], F32, "rcp")
            nblk = 0
            BLK = 512
            for h in range(8):
                qt, kt = QT[h % 2], KT[h % 2]
                for c0 in range(0, NTOK, BLK):
                    n = min(BLK, NTOK - c0)
                    csf = csfs[nblk % 2]
                    nblk += 1
                    k.load("sp", csf, csf[64:96, 0, 0:n], i["cos_fm"][:, c0:c0 + n])
                    k.load("sp", csf, csf[64:96, 1, 0:n], i["sin_fm"][:, c0:c0 + n], accs=True)
                    ba, bb = P.ps_alloc(), P.ps_alloc()
                    for kc in range(3):
                        k.mm(ba[0:96, 0:n], wuq[:, kc, h * 96:(h + 1) * 96], cqT[:, kc, c0:c0 + n], kc == 0, kc == 2, [wuq, cqT], [ba])
                    for kc in range(3):
                        k.mm(bb[0:96, 0:n], wqr[:, kc, h, :], cqT[:, kc, c0:c0 + n], kc == 0, kc == 2, [wqr, cqT], [bb])
                    k.cp("act", qt[0:64, c0:c0 + n], ba[0:64, 0:n], [ba], [], accs=[qt])
                    r_ = rt[(c0 // BLK) % 2]
                    k.tt("dve", r_[64:96, 0:n], ba[64:96, 0:n], csf[64:96, 0, 0:n], ALU.mult, [ba, csf], [r_])
                    k.tt("dve", qt[64:96, c0:c0 + n], bb[64:96, 0:n], csf[64:96, 1, 0:n], ALU.mult, [bb, csf], [], accs=[qt])
                    k.tt("pool", qt[64:96, c0:c0 + n], qt[64:96, c0:c0 + n], r_[64:96, 0:n], ALU.add, [qt, r_], [qt])
                    P.ps_release(ba)
                    P.ps_release(bb)
                k.cp("pool", qn_s[:, h, :], qt[0:64, TP:TP + 16], [qt], [], accs=[qn_s])
                k.cp("pool", qr_s[64:96, h, :], qt[64:96, TP:TP + 16], [qt], [], accs=[qr_s])
                k.cp("pool", kt[64:96, :], krT[64:96, 0:TP], [krT], [kt])
                for c0 in range(0, TP, BLK):
                    n = min(BLK, TP - c0)
                    ba = P.ps_alloc()
                    for rc in range(2):
                        k.mm(ba[0:64, 0:n], wuk[:, rc, h * 64:(h + 1) * 64], ckvT[:, rc, c0:c0 + n], rc == 0, rc == 1, [wuk, ckvT], [ba])
                    k.cp("act", kt[0:64, c0:c0 + n], ba[0:64, 0:n], [ba], [], accs=[kt])
                    P.ps_release(ba)
                groups = [(qi, j0, min(4, qi + 1 - j0)) for qi in range(NT) for j0 in range(0, qi + 1, 4)]

                def emit_scores(gidx):
                    qi, j0, nj = groups[gidx]
                    sc = P.ps_alloc()
                    for jj in range(nj):
                        j = j0 + jj
                        k.mm(sc[:, jj * 128:(jj + 1) * 128], kt[:, j * 128:(j + 1) * 128], qt[:, qi * 128:(qi + 1) * 128], True, True,
                             [kt, qt], [sc])
                    p_ = pT[gidx % 3]
                    k.act(p_[:, 0:nj, :], sc[:, 0:nj * 128].rearrange("p (j t) -> p j t", j=nj), AF.Exp, [sc], [p_], scale=MLA_SCALE)
                    P.ps_release(sc)
                    if j0 + nj - 1 == qi:
                        k.tt("pool", p_[:, nj - 1, :], p_[:, nj - 1, :], c["pk_p"][:, 0, :], ALU.mult, [p_, c["pk_p"]], [p_])

                acc = None
                emit_scores(0)
                for gidx, (qi, j0, nj) in enumerate(groups):
                    if gidx + 1 < len(groups):
                        emit_scores(gidx + 1)
                    if j0 == 0:
                        acc = P.ps_alloc()
                    p_ = pT[gidx % 3]
                    for jj in range(nj):
                        j = j0 + jj
                        k.mm(acc[:, 0:65], p_[:, jj, :], V[:, j, h, :], j == 0, j == qi, [p_, V], [acc])
                    if j0 + nj - 1 == qi:
                        P.op("dve", lambda e: e.reciprocal(rcp[:, 0:1], acc[:, 64:65]), reads=[acc.b], writes=[rcp.b])
                        k.ts("dve", o_all[:, qi, h * 64:(h + 1) * 64], acc[:, 0:64], rcp[:, 0:1], None, ALU.mult, None, [acc, rcp], [],
                             accs=[o_all])
                        P.ps_release(acc)
            P.barrier()
            m2a.close()
            with contextlib.ExitStack() as m2b:
                mla_sample(k, l, m2b, cqT, ckvT, krT, ckvs, wuk, wuv, qn_s, qr_s)
                P.barrier()
            wg = k.tile(m2, [128, 8, 512], BF16, "wg")
            wo = k.tile(m2, [128, 4, 1024], BF16, "mwo")
            load_w_cast(k, wg, wg, i["w_in"][l][:, OFF["gate"]:OFF["gate"] + 512], 512)
            load_w_cast(k, wo, wo, i["w_out"][l][1024:1536, :], 1024, kchunks=4)
            gs = [k.tile(m2, [128, 512], BF16, "gs") for _ in range(2)]
            yT = [k.tile(m2, [128, 4, 128], BF16, "myT") for _ in range(2)]
            rtile = k.tile(m2, [128, 1024], F32, "mrt")
            for ti in range(NT):
                tl = k.tiles[ti]
                r0, L = tl[0], tl[1]
                g_, y_ = gs[ti % 2], yT[ti % 2]
                bank = P.ps_alloc()
                for kk in range(8):
                    k.mm(bank[0:L, :], k.hT[:, kk, r0:r0 + L], wg[:, kk, :], kk == 0, kk == 7, [k.hT, wg], [bank])
                k.act(g_[0:L, :], bank[0:L, :], SILU, [bank], [g_])
                P.ps_release(bank)
                k.tt("dve", g_[0:L, :], g_[0:L, :], o_all[0:L, ti, :], ALU.mult, [g_, o_all], [g_])
                bank = P.ps_alloc()
                bv = bf_view(bank[:, :])
                for kk in range(4):
                    k.tr(bv[:, kk * 128:kk * 128 + L], g_[0:L, kk * 128:(kk + 1) * 128], c["ident_b"][0:L, 0:L], [g_, c["ident_b"]], [bank])
                k.cp("act", y_[:, :, 0:L], bv[:, 0:512].rearrange("p (c t) -> p c t", c=4)[:, :, 0:L], [bank], [y_])
                P.ps_release(bank)
                out_proj_add(k, rtile, tl, y_, 4, wo)
            mla_sample_out(k, l, m2, wg, wo, rtile)
            P.barrier()


def dyn_page_load(k, tl_, out_l, l, idxl, idx, first):
    P = k.P
    cat2 = k.i["cache_cat"][l].rearrange("n s r -> (n s) r")
    off = bass.IndirectOffsetOnAxis(ap=idxl[:, idx:idx + 1], axis=0)
    P.dma("pool", lambda e: e.indirect_dma_start(out=out_l, out_offset=None, in_=cat2, in_offset=off), tl_.b,
          reads=[idxl.b], writes=[tl_.b] if first else [], accs=[] if first else [tl_.b])


def mla_sample(k, l, m2, cqT, ckvT, krT, ckvs, wuk, wuv, qn_s, qr_s):
    P, i, c = k.P, k.i, k.c
    TP, NPG = k.TP, k.NPG
    IDB = c["ident_b"]
    wukT = k.tile(m2, [64, 8, 256], BF16, "wukT")
    for h in range(8):
        bank = P.ps_alloc()
        bv = bf_view(bank[:, :])
        for rc in range(2):
            k.tr(bv[0:64, rc * 128:(rc + 1) * 128], wuk[:, rc, h * 64:(h + 1) * 64], IDB[:, :], [wuk, IDB], [bank])
        k.cp("act" if h % 2 else "dve", wukT[:, h, :], bv[0:64, 0:256], [bank], [], accs=[wukT])
        P.ps_release(bank)
    qlat = k.tile(m2, [128, 2, 8, 16], BF16, "qlat")
    bank = P.ps_alloc()
    for rc in range(2):
        for h in range(8):
            k.mm(bank[:, (rc * 8 + h) * 16:(rc * 8 + h + 1) * 16], wukT[:, h, rc * 128:(rc + 1) * 128], qn_s[:, h, :], True, True,
                 [wukT, qn_s], [bank])
    k.cp("act", qlat[:, :, :, :], bank[:, 0:256].rearrange("p (r h t) -> p r h t", r=2, h=8), [bank], [qlat])
    P.ps_release(bank)
    smask = k.tile(m2, [16, 128], F32, "smask")
    k.load("sp", smask, smask[:, :], i["smask"])
    ptb = k.tile(m2, [128, 4 * NPG], I32, "ptb")
    k.load("sp", ptb, ptb[:, :], i["page_table"].rearrange("b n -> (b n)").partition_broadcast(128))
    iot = k.tile(m2, [128, 1], F32, "iot")
    k.load("sp", iot, iot[:, :], i["iota_p"])
    pt = k.tile(m2, [128, 4 * NPG], I32, "idxl")
    k.ts("dve", pt[:, :], ptb[:, :], 128.0, iot[:, 0:1], ALU.mult, ALU.add, [ptb, iot], [pt])
    wuvm = k.tile(m2, [128, 2, 8, 128], BF16, "wuvm")
    k.memset("pool", wuvm, wuvm[:], 0.0)
    wv4 = wuv[:, :, :].rearrange("p k (h d) -> p k h d", h=8)
    for h in range(8):
        k.cp("pool", wuvm[:, :, h, (h % 2) * 64:(h % 2) * 64 + 64], wv4[:, :, h, :], [wuv], [], accs=[wuvm])
    lat4 = [k.tile(m2, [128, 4, 288], F32, "lat4") for _ in range(6)]
    lat4b = [k.tile(m2, [128, 4, 257], BF16, "lat4b") for _ in range(3)]
    rope4b = [k.tile(m2, [128, 4, 96], BF16, "rope4b") for _ in range(3)]
    latT = [k.tile(m2, [128, 4, 2, 128], BF16, "latT") for _ in range(3)]
    ropeT = [k.tile(m2, [96, 4, 128], BF16, "ropeT") for _ in range(3)]
    pTs = [k.tile(m2, [128, 4, 32], BF16, "pTs") for _ in range(3)]
    for j in range(3):
        k.memset("pool", lat4b[j], lat4b[j][:, :, 256:257], 1.0)
        k.memset("pool", rope4b[j], rope4b[j][:, :, 0:64], 0.0)
    pn = k.tile(m2, [16, 32], F32, "pn")
    pnb = k.tile(m2, [16, 32], BF16, "pnb")
    rc_ = k.tile(m2, [32, 1], F32, "rcs")
    olat = k.tile(m2, [32, 256], BF16, "olat")
    olatT = k.tile(m2, [128, 2, 32], BF16, "olatT")
    yraw = P.ps_alloc()
    k.m_yraw = yraw
    gi = 0
    for b in range(4):
        acc = P.ps_alloc()
        qs = slice(b * 4, (b + 1) * 4)
        for g0 in range(0, NPG, 4):
            npg = min(4, NPG - g0)
            L4, L4b, R4b, LT, RT, PT_ = lat4[gi % 6], lat4b[gi % 3], rope4b[gi % 3], latT[gi % 3], ropeT[gi % 3], pTs[gi % 3]
            gi += 1
            for pg in range(npg):
                idx = b * NPG + g0 + pg
                dyn_page_load(k, L4, L4[:, pg, :], l, pt, idx, pg == 0)
            h1 = (npg + 1) // 2
            k.cp("dve", L4b[:, 0:h1, 0:256], L4[:, 0:h1, 0:256], [L4], [], accs=[L4b])
            if npg > h1:
                k.cp("act", L4b[:, h1:npg, 0:256], L4[:, h1:npg, 0:256], [L4], [], accs=[L4b])
            k.cp("dve", R4b[:, 0:npg, 64:96], L4[:, 0:npg, 256:288], [L4], [], accs=[R4b])
            ba = P.ps_alloc()
            bva = bf_view(ba[:, :])
            for pg in range(npg):
                for rc in range(2):
                    k.tr(bva[:, (pg * 2 + rc) * 128:(pg * 2 + rc + 1) * 128], L4b[:, pg, rc * 128:(rc + 1) * 128], IDB[:, :], [L4b, IDB], [ba])
            k.cp("act", LT[:, 0:npg, :, :], bva[:, 0:npg * 256].rearrange("p (g r s) -> p g r s", g=npg, r=2), [ba], [LT])
            P.ps_release(ba)
            bb = P.ps_alloc()
            bvb = bf_view(bb[:, :])
            for pg in range(npg):
                k.tr(bvb[0:96, pg * 128:(pg + 1) * 128], R4b[:, pg, :], IDB[:, :], [R4b, IDB], [bb])
            k.cp("dve", RT[64:96, 0:npg, :], bvb[64:96, 0:npg * 128].rearrange("p (g s) -> p g s", g=npg), [bb], [RT])
            P.ps_release(bb)
            sc = P.ps_alloc()
            for pg in range(npg):
                ov = sc[:, pg * 32:(pg + 1) * 32].rearrange("p (h t) -> p h t", h=8)
                k.mm(ov, LT[:, pg, 0, :], qlat[:, 0, :, qs], True, False, [LT, qlat], [sc])
                k.mm(ov, LT[:, pg, 1, :], qlat[:, 1, :, qs], False, False, [LT, qlat], [sc])
                k.mm(ov, RT[64:96, pg, :], qr_s[64:96, :, qs], False, True, [RT, qr_s], [sc])
            k.act(PT_[:, 0:npg, :], sc[:, 0:npg * 32].rearrange("p (g q) -> p g q", g=npg), AF.Exp, [sc], [PT_], scale=MLA_SCALE)
            P.ps_release(sc)
            for pg in range(npg):
                k.mm(acc[0:32, 0:257], PT_[:, pg, :], L4b[:, pg, :], g0 == 0 and pg == 0, False, [PT_, L4b], [acc])
        sc = P.ps_alloc()
        ov = sc[0:16, 0:32].rearrange("p (h t) -> p h t", h=8)
        k.mm(ov, ckvT[:, 0, TP:TP + 16], qlat[:, 0, :, qs], True, False, [ckvT, qlat], [sc])
        k.mm(ov, ckvT[:, 1, TP:TP + 16], qlat[:, 1, :, qs], False, False, [ckvT, qlat], [sc])
        k.mm(ov, krT[64:96, TP:TP + 16], qr_s[64:96, :, qs], False, True, [krT, qr_s], [sc])
        k.act(pn[:, :], sc[0:16, 0:32], AF.Exp, [sc], [pn], scale=MLA_SCALE)
        P.ps_release(sc)
        k.tt("dve", pnb[:, :], pn[:, :], smask[:, b * 32:(b + 1) * 32], ALU.mult, [pn, smask], [pnb])
        k.mm(acc[0:32, 0:257], pnb[:, :], ckvs[0:16, :], False, True, [pnb, ckvs], [acc])
        P.op("dve", lambda e: e.reciprocal(rc_[:, :], acc[0:32, 256:257]), reads=[acc.b], writes=[rc_.b])
        k.ts("dve", olat[:, :], acc[0:32, 0:256], rc_[:, 0:1], None, ALU.mult, None, [acc, rc_], [olat])
        P.ps_release(acc)
        bank = P.ps_alloc()
        bv = bf_view(bank[:, :])
        for rc in range(2):
            k.tr(bv[:, rc * 32:(rc + 1) * 32], olat[:, rc * 128:(rc + 1) * 128], IDB[0:32, 0:32], [olat, IDB], [bank])
        k.cp("act", olatT[:, :, :], bv[:, 0:64].rearrange("p (r q) -> p r q", r=2), [bank], [olatT])
        P.ps_release(bank)
        for cidx in range(4):
            n = 0
            for hh in range(2):
                h = cidx * 2 + hh
                for rc in range(2):
                    k.mm(yraw[:, cidx * 16 + b * 4:cidx * 16 + b * 4 + 4], wuvm[:, rc, h, :], olatT[:, rc, h * 4:(h + 1) * 4], n == 0, n == 3,
                         [wuvm, olatT], [yraw])
                    n += 1


def mla_sample_out(k, l, m2, wg, wo, rtile):
    P = k.P
    TP = k.TP
    tl = k.tiles[k.NT]
    yraw = k.m_yraw
    gsT = k.tile(m2, [128, 64], F32, "gsT")
    yTs = k.tile(m2, [128, 4, 16], BF16, "yTs")
    bank = P.ps_alloc()
    for cidx in range(4):
        for kk in range(8):
            k.mm(bank[:, cidx * 16:(cidx + 1) * 16], wg[:, kk, cidx * 128:(cidx + 1) * 128], k.hT[:, kk, TP:TP + 16], kk == 0, kk == 7,
                 [wg, k.hT], [bank])
    k.act(gsT[:, :], bank[:, 0:64], SILU, [bank], [gsT])
    P.ps_release(bank)
    k.tt("dve", yTs[:, :, :], yraw[:, 0:64].rearrange("p (c t) -> p c t", c=4), gsT[:, :].rearrange("p (c t) -> p c t", c=4), ALU.mult,
         [yraw, gsT], [yTs])
    P.ps_release(yraw)
    out_proj_add(k, rtile, tl, yTs, 4, wo)


def sample_mask():
    m = np.zeros((16, 4, 8, 4), np.float32)
    for s_ in range(16):
        b, p = s_ // 4, s_ % 4
        for t in range(4):
            if p <= t:
                m[s_, b, :, t] = 1.0
    return m.reshape(16, 128)


def build(TP, NPG, NPHYS):
    k = K(TP, NPG, NPHYS)
    setup(k)
    for l in range(DEPTH):
        boundary(k, l)
        ssd_phase(k, l)
        mla_phase(k, l)
        gdn_phase(k, l)
    boundary(k, DEPTH)
    finish(k)
    return k


def core_inputs(k, c, inp, consts):
    TP, NPG = k.TP, k.NPG
    m = {}
    m["x_all"] = np.concatenate([inp["x_prompt"][c], inp["x_sample"][4 * c:4 * c + 4].reshape(16, D)], 0)
    for l_ in range(DEPTH):
        m["cache_cat%d" % l_] = inp["_cache_cat"][l_]
    m["st_ssd_conv"] = inp["state_ssd_conv"][:, 4 * c:4 * c + 4]
    m["st_ssd"] = inp["state_ssd"][:, 4 * c:4 * c + 4].reshape(DEPTH, 4, 1024, 128)
    m["st_gdn_conv"] = inp["state_gdn_conv"][:, 4 * c:4 * c + 4]
    m["st_gdn"] = inp["state_gdn"][:, 4 * c:4 * c + 4]
    m["page_table"] = inp["page_table"][4 * c:4 * c + 4].astype(np.int32)
    for nm in ["emb_ln_g", "emb_ln_b", "w_in", "ssd_conv_w", "ssd_conv_b", "ssd_dt_bias", "ssd_a_log", "ssd_d", "ssd_norm_w",
               "mla_q_norm_w", "mla_w_uq", "mla_kv_norm_w", "gdn_conv_w", "gdn_dt_bias", "gdn_a_log", "gdn_norm_w", "w_out",
               "ln_g", "ln_b"]:
        m[nm] = inp[nm]
    m["mla_w_uk"] = inp["mla_w_uk"].reshape(DEPTH, 256, 512)
    m["mla_w_uv"] = inp["mla_w_uv"].reshape(DEPTH, 256, 512)
    m.update(consts)
    return {a: np.ascontiguousarray(b) for a, b in m.items() if a in k.inputs}


def make_consts(TP, NPG):
    c = host_consts(128, 4, 4)
    pos = np.concatenate([np.arange(TP), np.tile(NPG * 128 + np.arange(4), 4)])
    cs, sn = rope_tables(pos)
    c["cos_fm"] = np.ascontiguousarray(np.concatenate([cs, cs], 1).T)
    c["sin_fm"] = np.ascontiguousarray(np.concatenate([sn, sn], 1).T)
    c["cos_tm"], c["sin_tm"] = cs, sn
    c["smask"] = sample_mask()
    c["iota_p"] = np.arange(128, dtype=np.float32)[:, None]
    return c


def run(inp, ncores):
    inp = {a: np.asarray(b) for a, b in inp.items()}
    TP = inp["x_prompt"].shape[1]
    NPG = inp["page_table"].shape[1]
    NPHYS = inp["cache_kv_latent"].shape[1]
    k = build(TP, NPG, NPHYS)
    consts = make_consts(TP, NPG)
    inp["_cache_cat"] = [np.concatenate([inp["cache_kv_latent"][l_], inp["cache_k_rope"][l_]], axis=-1) for l_ in range(DEPTH)]
    in_maps = [core_inputs(k, c, inp, consts) for c in range(ncores)]
    res = run_bass_kernel_spmd(k.nc, in_maps, core_ids=list(range(ncores)))
    R = res.results
    f = np.float32
    B, BS = ncores, 4 * ncores
    y_p = np.stack([R[c]["y_all"][:TP] for c in range(B)]).astype(f)
    y_s = np.concatenate([R[c]["y_all"][TP:].reshape(4, 4, D) for c in range(B)]).astype(f)

    def pcat(nm, shp):
        return np.stack([np.asarray(R[c][nm]).reshape((DEPTH,) + shp) for c in range(B)], 1).astype(f)

    def scat(nm, shp):
        return np.concatenate([np.asarray(R[c][nm]).reshape((DEPTH, 4) + shp) for c in range(B)], 1).astype(f)

    return (y_p, y_s,
            pcat("p_lat", (TP, 256)), pcat("p_rope", (TP, 32)), pcat("p_ssd_conv", (3, 1536)), pcat("p_ssd", (16, 64, 128)),
            pcat("p_gdn_conv", (3, 1536)), pcat("p_gdn", (4, 128, 128)),
            scat("s_lat", (4, 256)), scat("s_rope", (4, 32)), scat("s_ssd_conv", (3, 1536)), scat("s_ssd", (16, 64, 128)),
            scat("s_gdn_conv", (3, 1536)), scat("s_gdn", (4, 128, 128)))


def kernel(**inputs):
    return run(inputs, NCORES)
```
